# Optimizing a Trainium2 kernel written in Bass

```python
import math
import jax, jax.numpy as jnp
from jax import lax
import numpy as np

D_MODEL = 2048
BATCH = 8
SEQ = 2048
DEPTH = 4

N_MIXERS = 3
EPS = 1e-6
BLOCK = 128
NEG_INF = -1e30

N_BUCKETS = 32
MAX_DISTANCE = 128
N_BIAS_HEADS = 32

MEM_LEN = 256
MEM_HEADS = 4
MEM_HEAD_DIM = 256
MEM_WIDTH = MEM_HEADS * MEM_HEAD_DIM

SELF_WIDTH = D_MODEL
BRANCH_WIDTH = SELF_WIDTH + MEM_WIDTH

A_HEADS = 16
A_Q_LORA = 1536
A_KV_LORA = 512
A_NOPE = 128
A_ROPE = 64
A_V = 128
ROPE_THETA = 10000.0
A_IN = A_Q_LORA + A_KV_LORA + A_ROPE + MEM_WIDTH + BRANCH_WIDTH

B_HEADS = N_BIAS_HEADS
B_KV_HEADS = 4
B_HEAD_DIM = 64
IDX_HEADS = 16
IDX_DIM = 64
IDX_TOPK_MAX = 256
B_IN = (B_HEADS * B_HEAD_DIM + 2 * B_KV_HEADS * B_HEAD_DIM + IDX_HEADS * IDX_DIM
        + IDX_DIM + IDX_HEADS + MEM_WIDTH + BRANCH_WIDTH)

C_HEADS = N_BIAS_HEADS
C_KV_HEADS = 4
C_HEAD_DIM = 64
WINDOW = 128
C_IN = C_HEADS * C_HEAD_DIM + 2 * C_KV_HEADS * C_HEAD_DIM + MEM_WIDTH + BRANCH_WIDTH

N_A = (DEPTH + 2) // 3
N_B = (DEPTH + 1) // 3
N_C = DEPTH // 3

kernel_name = "hybrid_mla_dsa_swa_interleaved"


def _rmsnorm(x, g):
    xf = x.astype(jnp.float32)
    y = xf * lax.rsqrt(jnp.mean(xf * xf, axis=-1, keepdims=True) + EPS)
    return (y * g.astype(jnp.float32)).astype(x.dtype)


def _split(h, sizes):
    offs = [int(v) for v in np.cumsum(sizes)[:-1]]
    return jnp.split(h, offs, axis=-1)


def _rope(x):
    S, d = x.shape[1], x.shape[-1]
    inv = 1.0 / (ROPE_THETA ** (jnp.arange(0, d, 2, dtype=jnp.float32) / d))
    ang = jnp.arange(S, dtype=jnp.float32)[:, None] * inv[None, :]
    if x.ndim == 4:
        ang = ang[:, None, :]
    cos, sin = jnp.cos(ang), jnp.sin(ang)
    xf = x.astype(jnp.float32)
    x1, x2 = xf[..., : d // 2], xf[..., d // 2:]
    return jnp.concatenate([x1 * cos - x2 * sin, x2 * cos + x1 * sin], axis=-1).astype(x.dtype)


def _t5_bucket(rel):
    n = jnp.maximum(rel, 0)
    max_exact = N_BUCKETS // 2
    nf = jnp.maximum(n, 1).astype(jnp.float32)
    large = max_exact + (jnp.log(nf / max_exact) / math.log(MAX_DISTANCE / max_exact)
                         * (N_BUCKETS - max_exact)).astype(jnp.int32)
    large = jnp.minimum(large, N_BUCKETS - 1)
    return jnp.where(n < max_exact, n, large)


def _to_blocks(t):
    B, S = t.shape[0], t.shape[1]
    return t.reshape((B, S // BLOCK, BLOCK) + t.shape[2:]).swapaxes(0, 1)


def _from_blocks(o):
    nb, B = o.shape[0], o.shape[1]
    return o.swapaxes(0, 1).reshape((B, nb * BLOCK) + o.shape[3:])


def _mla_branch(h, w_in, q_norm, w_uq, kv_norm, w_ukv):
    B, S, _ = h.shape
    nb = S // BLOCK
    c_q, c_kv, k_rope, mq, gate = _split(h @ w_in, [A_Q_LORA, A_KV_LORA, A_ROPE, MEM_WIDTH, BRANCH_WIDTH])
    q = (_rmsnorm(c_q, q_norm) @ w_uq).reshape(B, S, A_HEADS, A_NOPE + A_ROPE)
    kv = (_rmsnorm(c_kv, kv_norm) @ w_ukv).reshape(B, S, A_HEADS, A_NOPE + A_V)
    q_nope, q_rope = q[..., :A_NOPE], _rope(q[..., A_NOPE:])
    k_nope, v = kv[..., :A_NOPE], kv[..., A_NOPE:]
    k_rope = _rope(k_rope)
    scale = (A_NOPE + A_ROPE) ** -0.5
    key_idx = jnp.arange(S)

    def block(args):
        qn, qr, start = args
        s = (jnp.einsum('bqhd,bkhd->bhqk', qn, k_nope, preferred_element_type=jnp.float32)
             + jnp.einsum('bqhd,bkd->bhqk', qr, k_rope, preferred_element_type=jnp.float32)) * scale
        causal = (start + jnp.arange(BLOCK))[:, None] >= key_idx[None, :]
        p = jax.nn.softmax(jnp.where(causal, s, NEG_INF), axis=-1)
        return jnp.einsum('bhqk,bkhd->bqhd', p.astype(v.dtype), v)

    o = lax.map(block, (_to_blocks(q_nope), _to_blocks(q_rope), jnp.arange(nb) * BLOCK))
    return _from_blocks(o).reshape(B, S, A_HEADS * A_V), mq, gate


def _dsa_branch(h, w_in, rel_bias):
    B, S, _ = h.shape
    nb = S // BLOCK
    k_top = min(IDX_TOPK_MAX, S // 4)
    G = B_HEADS // B_KV_HEADS
    q, k, v, iq, ik, iw, mq, gate = _split(h @ w_in, [
        B_HEADS * B_HEAD_DIM, B_KV_HEADS * B_HEAD_DIM, B_KV_HEADS * B_HEAD_DIM,
        IDX_HEADS * IDX_DIM, IDX_DIM, IDX_HEADS, MEM_WIDTH, BRANCH_WIDTH])
    q = q.reshape(B, S, B_KV_HEADS, G, B_HEAD_DIM)
    k = k.reshape(B, S, B_KV_HEADS, B_HEAD_DIM)
    v = v.reshape(B, S, B_KV_HEADS, B_HEAD_DIM)
    iq = iq.reshape(B, S, IDX_HEADS, IDX_DIM)
    key_idx = jnp.arange(S)
    gather = jax.vmap(lambda t, i: t[i])

    def block(args):
        qb, iqb, iwb, start = args
        qi = start + jnp.arange(BLOCK)
        dots = jnp.einsum('bqhd,bkd->bqhk', iqb, ik, preferred_element_type=jnp.float32) * IDX_DIM ** -0.5
        score = jnp.einsum('bqhk,bqh->bqk', jax.nn.relu(dots), iwb.astype(jnp.float32)) * IDX_HEADS ** -0.5
        score = jnp.where(qi[:, None] >= key_idx[None, :], score, NEG_INF)
        _, sel = lax.top_k(score, k_top)
        kg = gather(k, sel)
        vg = gather(v, sel)
        rel = qi[None, :, None] - sel
        bias = rel_bias[_t5_bucket(rel)].astype(jnp.float32)
        bias = bias.reshape(B, BLOCK, k_top, B_KV_HEADS, G).transpose(0, 1, 3, 4, 2)
        s = jnp.einsum('bqgjd,bqkgd->bqgjk', qb, kg, preferred_element_type=jnp.float32) * B_HEAD_DIM ** -0.5 + bias
        s = jnp.where((rel >= 0)[:, :, None, None, :], s, NEG_INF)
        p = jax.nn.softmax(s, axis=-1)
        return jnp.einsum('bqgjk,bqkgd->bqgjd', p.astype(vg.dtype), vg)

    o = lax.map(block, (_to_blocks(q), _to_blocks(iq), _to_blocks(iw), jnp.arange(nb) * BLOCK))
    return _from_blocks(o).reshape(B, S, B_HEADS * B_HEAD_DIM), mq, gate


def _swa_branch(h, w_in, sinks, rel_bias):
    B, S, _ = h.shape
    nb = S // BLOCK
    G = C_HEADS // C_KV_HEADS
    q, k, v, mq, gate = _split(h @ w_in, [
        C_HEADS * C_HEAD_DIM, C_KV_HEADS * C_HEAD_DIM, C_KV_HEADS * C_HEAD_DIM, MEM_WIDTH, BRANCH_WIDTH])
    q = q.reshape(B, S, C_KV_HEADS, G, C_HEAD_DIM)

    def band(t):
        tb = t.reshape(B, nb, BLOCK, C_KV_HEADS, C_HEAD_DIM)
        prev = jnp.concatenate([jnp.zeros_like(tb[:, :1]), tb[:, :-1]], axis=1)
        return jnp.concatenate([prev, tb], axis=2).swapaxes(0, 1)

    qi = jnp.arange(BLOCK)
    kj = jnp.arange(2 * BLOCK)
    rel = qi[:, None] + BLOCK - kj[None, :]
    in_window = (rel >= 0) & (rel < WINDOW)
    bias = rel_bias[_t5_bucket(rel)].astype(jnp.float32)
    bias = bias.reshape(BLOCK, 2 * BLOCK, C_KV_HEADS, G).transpose(2, 3, 0, 1)
    sink = sinks.astype(jnp.float32).reshape(C_KV_HEADS, G)[None, :, :, None, None]
    scale = C_HEAD_DIM ** -0.5

    def block(args):
        qb, kb, vb, start = args
        valid = in_window & ((start - BLOCK + kj) >= 0)[None, :]
        s = jnp.einsum('bqgjd,bkgd->bgjqk', qb, kb, preferred_element_type=jnp.float32) * scale + bias
        s = jnp.where(valid, s, NEG_INF)
        m = jnp.maximum(jnp.max(s, axis=-1, keepdims=True), sink)
        e = jnp.exp(s - m)
        p = e / (jnp.sum(e, axis=-1, keepdims=True) + jnp.exp(sink - m))
        return jnp.einsum('bgjqk,bkgd->bqgjd', p.astype(vb.dtype), vb)

    o = lax.map(block, (_to_blocks(q), band(k), band(v), jnp.arange(nb) * BLOCK))
    return _from_blocks(o).reshape(B, S, C_HEADS * C_HEAD_DIM), mq, gate


def _memory_attention(mq, mem_kv):
    B, S, _ = mq.shape
    mk, mv = jnp.split(mem_kv, 2, axis=-1)
    mq = mq.reshape(B, S, MEM_HEADS, MEM_HEAD_DIM)
    mk = mk.reshape(B, -1, MEM_HEADS, MEM_HEAD_DIM)
    mv = mv.reshape(B, -1, MEM_HEADS, MEM_HEAD_DIM)
    s = jnp.einsum('bqhd,bkhd->bhqk', mq, mk, preferred_element_type=jnp.float32) * MEM_HEAD_DIM ** -0.5
    p = jax.nn.softmax(s, axis=-1)
    return jnp.einsum('bhqk,bkhd->bqhd', p.astype(mv.dtype), mv).reshape(B, S, MEM_WIDTH)


def setup_inputs(seed: int = 0) -> dict:
    key = jax.random.key(seed)
    ks = jax.random.split(key, 16)
    nrm = jax.random.normal
    f32 = jnp.float32
    return {
        "x": nrm(ks[0], (BATCH, SEQ, D_MODEL), f32),
        "mem": nrm(ks[1], (BATCH, MEM_LEN, D_MODEL), f32),
        "norm_in": 1.0 + 0.02 * nrm(ks[2], (DEPTH, D_MODEL), f32),
        "final_norm": 1.0 + 0.02 * nrm(ks[3], (D_MODEL,), f32),
        "mem_norm": 1.0 + 0.02 * nrm(ks[4], (D_MODEL,), f32),
        "rel_bias": 0.2 * nrm(ks[5], (N_BUCKETS, N_BIAS_HEADS), f32),
        "w_in_a": nrm(ks[6], (N_A, D_MODEL, A_IN), f32) * D_MODEL ** -0.5,
        "a_q_norm": 1.0 + 0.02 * nrm(ks[7], (N_A, A_Q_LORA), f32),
        "w_uq": nrm(ks[8], (N_A, A_Q_LORA, A_HEADS * (A_NOPE + A_ROPE)), f32) * A_Q_LORA ** -0.5,
        "a_kv_norm": 1.0 + 0.02 * nrm(ks[9], (N_A, A_KV_LORA), f32),
        "w_ukv": nrm(ks[10], (N_A, A_KV_LORA, A_HEADS * (A_NOPE + A_V)), f32) * A_KV_LORA ** -0.5,
        "w_in_b": nrm(ks[11], (N_B, D_MODEL, B_IN), f32) * D_MODEL ** -0.5,
        "w_in_c": nrm(ks[12], (N_C, D_MODEL, C_IN), f32) * D_MODEL ** -0.5,
        "c_sinks": nrm(ks[13], (N_C, C_HEADS), f32),
        "w_mem_kv": nrm(ks[14], (DEPTH, D_MODEL, 2 * MEM_WIDTH), f32) * D_MODEL ** -0.5,
        "w_out": nrm(ks[15], (DEPTH, BRANCH_WIDTH, D_MODEL), f32) * BRANCH_WIDTH ** -0.5,
    }


def reference(x, mem, norm_in, final_norm, mem_norm, rel_bias, w_in_a, a_q_norm, w_uq,
              a_kv_norm, w_ukv, w_in_b, w_in_c, c_sinks, w_mem_kv, w_out):
    mem_n = _rmsnorm(mem, mem_norm)
    for i in range(DEPTH):
        h = _rmsnorm(x, norm_in[i])
        kind, j = i % N_MIXERS, i // N_MIXERS
        if kind == 0:
            self_out, mq, gate = _mla_branch(h, w_in_a[j], a_q_norm[j], w_uq[j], a_kv_norm[j], w_ukv[j])
        elif kind == 1:
            self_out, mq, gate = _dsa_branch(h, w_in_b[j], rel_bias)
        else:
            self_out, mq, gate = _swa_branch(h, w_in_c[j], c_sinks[j], rel_bias)
        mem_out = _memory_attention(mq, mem_n @ w_mem_kv[i])
        y = jnp.concatenate([self_out, mem_out], axis=-1) * jax.nn.silu(gate)
        x = x + y @ w_out[i]
    return _rmsnorm(x, final_norm)
```

```python
import math
import numpy as np
import concourse.bass as bass
import concourse.mybir as mybir
from concourse.bass_utils import run_bass_kernel_spmd

F32 = mybir.dt.float32
BF16 = mybir.dt.bfloat16
AF = mybir.ActivationFunctionType
ALU = mybir.AluOpType

S = 2048
D = 2048
NT = 16
EPS = 1e-6
TB = 1024
GTB = 2048
NPASS = S // TB
TBT = TB // 128
TBB = TB // 512
NEGM = -30000.0
MASKC = 2992.0

A_Q, A_KV, A_ROPE = 1536, 512, 64
MEMW = 1024
BRW = 3072


def chunk_fm(W):
    Kd, N = W.shape
    return np.ascontiguousarray(W.reshape(Kd // 128, 128, N // 128, 128).transpose(2, 1, 0, 3))


def chunk_tm(W):
    Kd, N = W.shape
    return np.ascontiguousarray(W.reshape(Kd // 128, 128, N).transpose(1, 0, 2))


def col_pc(g):
    return np.ascontiguousarray(g.reshape(-1, 128).T)


def t5_bucket_np(n):
    n = np.maximum(n, 0)
    nf = np.maximum(n, 1).astype(np.float32)
    large = 16 + (np.log(nf / np.float32(16)) / np.float32(math.log(128 / 16)) * np.float32(16)).astype(np.int32)
    large = np.minimum(large, 31)
    return np.where(n < 16, n, large)


class Buf:
    __slots__ = ("name", "w", "r", "excl")

    def __init__(self, name="", excl=False):
        self.name = name
        self.w = None
        self.r = {}
        self.excl = excl


class Eng:
    def __init__(self, K, eng, name, nring=0, selfwait=True):
        self.K = K
        self.e = eng
        self.name = name
        self.selfwait = selfwait
        self.sem = K.new_sem("s_" + name)
        self.cnt = 0
        self.waited = {}
        self.rsem = [K.new_sem("d_%s%d" % (name, i)) for i in range(nring)]
        self.rcnt = [0] * nring
        self.ri = 0
        self.pr = []
        self.pw = []

    def wait(self, tok):
        if tok is None or self.K.stopped:
            return
        sem, val = tok
        if (not self.selfwait) and sem is self.sem:
            return
        k = id(sem)
        if self.waited.get(k, 0) >= val:
            return
        self.e.wait_ge(sem, val)
        self.waited[k] = val

    def deps(self, reads, writes):
        for b in reads:
            self.wait(b.w)
            if b.excl:
                for k, t in b.r.items():
                    if k != self.name:
                        self.wait(t)
        for b in writes:
            self.wait(b.w)
            for t in b.r.values():
                self.wait(t)

    def _commit(self, tok, key, reads, writes):
        for b in writes:
            b.w = tok
            b.r = {}
        for b in reads:
            b.r[key] = tok

    def op(self, fn, reads=(), writes=(), signal=True):
        if self.K.stopped:
            return None
        self.deps(reads, writes)
        inst = fn(self.e)
        if signal:
            self.cnt += 1
            inst.then_inc(self.sem, 1)
            tok = (self.sem, self.cnt)
            self._commit(tok, self.name, list(reads) + self.pr, list(writes) + self.pw)
            self.pr = []
            self.pw = []
            return tok
        self.pr += list(reads)
        self.pw += list(writes)
        return None

    def dma(self, out, in_, reads=(), writes=(), **kw):
        if self.K.stopped:
            return None
        self.deps(reads, writes)
        k = self.ri % len(self.rsem)
        self.ri += 1
        if self.rcnt[k] > 0:
            self.wait((self.rsem[k], 16 * self.rcnt[k]))
        inst = self.e.dma_start(out=out, in_=in_, **kw)
        self.rcnt[k] += 1
        inst.then_inc(self.rsem[k], 16)
        tok = (self.rsem[k], 16 * self.rcnt[k])
        self._commit(tok, "%s_d%d" % (self.name, k), reads, writes)
        return tok

    def all_tokens(self):
        toks = []
        if self.cnt:
            toks.append((self.sem, self.cnt))
        for s, c in zip(self.rsem, self.rcnt):
            if c:
                toks.append((s, 16 * c))
        return toks


class Ring:
    def __init__(self, items):
        self.items = items
        self.i = 0

    def next(self):
        it = self.items[self.i % len(self.items)]
        self.i += 1
        return it


class Kern:
    def __init__(self, nc, stack):
        self.nc = nc
        self.stack = stack
        self.nsem = 0
        self.stopped = False
        self.pe = Eng(self, nc.tensor, "pe", selfwait=False)
        self.act = Eng(self, nc.scalar, "act")
        self.dve = Eng(self, nc.vector, "dve")
        self.pool = Eng(self, nc.gpsimd, "pool", nring=8)
        self.sp = Eng(self, nc.sync, "sp", nring=12)
        self.engs = [self.pe, self.act, self.dve, self.pool, self.sp]

    def new_sem(self, name):
        self.nsem += 1
        return self.stack.enter_context(self.nc.semaphore(name))

    def barrier(self):
        toks = []
        for e in self.engs:
            toks += e.all_tokens()
        for e in self.engs:
            for t in toks:
                if t[0] is e.sem and not e.selfwait:
                    continue
                e.wait(t)

    def sb(self, stack, name, shape, dt, nbuf=None):
        self.nsb = getattr(self, "nsb", 0) + 1
        name = "sb%d_%s" % (self.nsb, name)
        t = stack.enter_context(self.nc.sbuf_tensor(name, shape, dt))
        if nbuf is None:
            return t, Buf(name)
        return t, [Buf("%s%d" % (name, i)) for i in range(nbuf)]


def build_program(layers, final_norm=True, dbg=(), stop_after=None, wplan=None, wlog=None):
    from contextlib import ExitStack

    nc = bass.Bass("TRN2", target_bir_lowering=False)
    top = ExitStack()
    with top:
        K = Kern(nc, top)
        pe, act, dve, pool, sp = K.pe, K.act, K.dve, K.pool, K.sp

        class _Stop(Exception):
            pass

        def chk(name):
            if stop_after == name:
                K.barrier()
                K.stopped = True

        def din(name, shape, dt=F32):
            return nc.dram_tensor(name, list(shape), dt, kind="ExternalInput").ap()

        def dscr(name, shape, dt):
            kind = "ExternalOutput" if name in dbg else "Internal"
            return nc.dram_tensor(name, list(shape), dt, kind=kind).ap()

        x_in = din("x", [S, D])
        mem_in = din("mem", [256, D])
        out_d = nc.dram_tensor("out", [S, D], F32, kind="ExternalOutput").ap()
        gains = din("gains", [6, D])
        cosT = din("cosT", [64, S])
        sinT = din("sinT", [64, S])
        rotT_d = din("rotT", [64, 64])
        cmask_d = din("cmask", [128, 3, 128])
        ident_d = din("ident", [128, 128])
        cb_d = din("cbias", [1, 32])
        sink_d = din("sinkpc", [128, 16])
        wbias_d = din("wbias", [16, 128, 2, 640])

        Wd = {}
        for L in layers:
            kind, j = L % 3, L // 3
            p = "L%d_" % L
            if kind == 0:
                Wd[p + "cq"] = din(p + "cq", [12, 128, 16, 128])
                Wd[p + "ckv"] = din(p + "ckv", [4, 128, 16, 128])
                Wd[p + "kr"] = din(p + "kr", [128, 16, 64])
                Wd[p + "gq"] = din(p + "gq", [128, 12])
                Wd[p + "gkv"] = din(p + "gkv", [128, 4])
                Wd[p + "uqn"] = din(p + "uqn", [16, 128, 12, 128])
                Wd[p + "uqr"] = din(p + "uqr", [8, 128, 12, 128])
                Wd[p + "ukvk"] = din(p + "ukvk", [16, 128, 4, 128])
                Wd[p + "ukvv"] = din(p + "ukvv", [4, 128, 4, 512])
            else:
                Wd[p + "q"] = din(p + "q", [16, 128, 16, 128])
                Wd[p + "k"] = din(p + "k", [2, 128, 16, 128])
                Wd[p + "v"] = din(p + "v", [128, 16, 256])
                if kind == 1:
                    Wd[p + "iq"] = din(p + "iq", [8, 128, 16, 128])
                    Wd[p + "ik"] = din(p + "ik", [128, 16, 64])
                    Wd[p + "iw"] = din(p + "iw", [128, 16, 16])
            Wd[p + "mq"] = din(p + "mq", [8, 128, 16, 128])
            Wd[p + "gate"] = din(p + "gate", [24, 128, 16, 128])
            Wd[p + "mk"] = din(p + "mk", [8, 128, 16, 128])
            Wd[p + "mv"] = din(p + "mv", [2, 128, 16, 512])
            Wd[p + "out"] = din(p + "out", [4, 128, 24, 512])

        xres = dscr("xres", [S, D], F32)
        o_scr = dscr("o_scr", [24, 128, S], BF16)
        sg_scr = dscr("sg_scr", [24, 128, S], BF16)
        q_scr = dscr("q_scr", [16, 192, S], BF16)
        k_scr = dscr("k_scr", [16, 128, S], BF16)
        kr_scr = dscr("kr_scr", [64, S], BF16)
        v_scr = dscr("v_scr", [16, 128, S], BF16)
        iq_scr = dscr("iq_scr", [8, 128, S], BF16)
        ik_scr = dscr("ik_scr", [64, S], BF16)
        iw_scr = dscr("iw_scr", [S, 16], F32)
        mask_scr = dscr("mask_scr", [16, 128, S], BF16)
        mask_b = Buf()
        xres_b = [[Buf() for _ in range(4)] for _ in range(NT)]
        o_b = [Buf() for _ in range(24)]
        sg_b = [Buf() for _ in range(24)]
        q_b = [Buf() for _ in range(16)]
        k_b = [Buf() for _ in range(16)]
        kr_b = Buf()
        v_b = [Buf() for _ in range(16)]
        iq_b = [Buf() for _ in range(8)]
        ik_b = Buf()
        iw_b = Buf()

        ident, ident_b = K.sb(top, "ident", [128, 128], BF16)
        ones, ones_b = K.sb(top, "ones", [128, 128], BF16)
        cmask, cmask_b = K.sb(top, "cmask", [128, 3, 128], BF16)
        negtm, negtm_b = K.sb(top, "negtm", [128, 128], F32)
        rotT, rotT_b = K.sb(top, "rotT", [128, 128], F32)
        identC, identC_b = K.sb(top, "identC", [128, 128], BF16)
        negC8, negC8_b = K.sb(top, "negC8", [128, 1], F32)
        epsc, epsc_b = K.sb(top, "epsc", [128, 1], F32)
        gbc, gbc_b = K.sb(top, "gbc", [128, D], F32)
        memT, memT_b = K.sb(top, "memT", [128, 16, 256], BF16)
        mkT, mkT_b = K.sb(top, "mkT", [128, 8, 256], BF16)
        mv, mv_b = K.sb(top, "mv", [128, 2, 1024], BF16)
        wbs = []
        for i in range(2):
            t, b = K.sb(top, "wb%d" % i, [128, 12288], BF16)
            wbs.append((t, b))
        wring = Ring(wbs)
        f32ring = Ring([K.sb(top, "tf%d" % i, [128, 512], F32) for i in range(4)])
        bfring = Ring([K.sb(top, "tb%d" % i, [128, 512], BF16) for i in range(6)])
        colring = Ring([K.sb(top, "tc%d" % i, [128, 4], F32) for i in range(4)])
        banks = [(top.enter_context(nc.psum_tensor("ps%d" % i, [128, 512], F32)), Buf(excl=True)) for i in range(8)]
        psw = Ring(banks[0:4])
        psa = Ring(banks[4:8])

        def load_consts():
            with ExitStack() as st:
                stg, stg_b = K.sb(st, "cstg", [128, 3, 128], F32)
                sp.dma(stg[:, 0, :], ident_d, writes=[stg_b])
                dve.op(lambda e: e.tensor_copy(out=ident[:], in_=stg[:, 0, :]), reads=[stg_b], writes=[ident_b])
                dve.op(lambda e: e.tensor_scalar(out=identC[:], in0=stg[:, 0, :], scalar1=MASKC, scalar2=None, op0=ALU.mult),
                       reads=[stg_b], writes=[identC_b])
                dve.op(lambda e: e.memset(negC8[:], -MASKC * 0.125), writes=[negC8_b])
                sp.dma(stg[:], cmask_d, writes=[stg_b])
                dve.op(lambda e: e.tensor_copy(out=cmask[:], in_=stg[:]), reads=[stg_b], writes=[cmask_b])
                dve.op(lambda e: e.tensor_copy(out=negtm[:], in_=stg[:, 2, :]), reads=[stg_b], writes=[negtm_b])
                dve.op(lambda e: e.memset(ones[:], 1.0), writes=[ones_b])
                dve.op(lambda e: e.memset(epsc[:], EPS), writes=[epsc_b])
                dve.op(lambda e: e.memset(rotT[:], 0.0), writes=[rotT_b])
                sp.dma(rotT[0:64, 0:64], rotT_d, writes=[rotT_b])
                sp.dma(rotT[64:128, 64:128], rotT_d, writes=[rotT_b])
                K.barrier()

        def rstd_col(ssq_ap, ssq_buf, n):
            ct, cb = colring.next()
            act.op(lambda e: e.activation(out=ct[:, 1:2], in_=ssq_ap, func=AF.Ln, bias=epsc[:], scale=1.0 / n),
                   reads=[ssq_buf, epsc_b], writes=[cb])
            act.op(lambda e: e.activation(out=ct[:, 2:3], in_=ct[:, 1:2], func=AF.Exp, scale=-0.5),
                   reads=[cb], writes=[cb])
            return ct[:, 2:3], cb

        def load_gain(row):
            sp.dma(gbc[:], gains[row:row + 1, :].partition_broadcast(128), writes=[gbc_b])

        def norm_transpose(src_rows_fn, ntiles, dstT, dst_bufs, st):
            xts = Ring([K.sb(st, "xt%d" % i, [128, D], F32) for i in range(2)])
            hbs = Ring([K.sb(st, "hb%d" % i, [128, D], BF16) for i in range(2)])
            junk, junk_b = K.sb(st, "junk", [128, D], BF16)
            for t in range(ntiles):
                xt, xb = xts.next()
                src_ap, src_bufs = src_rows_fn(t)
                sp.dma(xt[:], src_ap, reads=src_bufs, writes=[xb])
                ct, cb = colring.next()
                act.op(lambda e: e.activation(out=junk[:], in_=xt[:], func=AF.Square, accum_out=ct[:, 0:1]),
                       reads=[xb], writes=[junk_b, cb])
                rs, rb = rstd_col(ct[:, 0:1], cb, D)
                hb, hbb = hbs.next()
                dve.op(lambda e: e.scalar_tensor_tensor(out=hb[:], in0=xt[:], scalar=rs, in1=gbc[:],
                                                        op0=ALU.mult, op1=ALU.mult),
                       reads=[xb, rb, gbc_b], writes=[hbb])
                for half in range(2):
                    pt, pb = psw.next()
                    pv = pt[:].bitcast(BF16)
                    for q in range(8):
                        kc = half * 8 + q
                        pe.op(lambda e: e.transpose(out=pv[:, q * 128:(q + 1) * 128], in_=hb[:, kc * 128:(kc + 1) * 128],
                                                    identity=ident[:]),
                              reads=[hbb, ident_b], writes=[pb], signal=(q == 7))
                    eng = act if half == 0 else dve
                    outv = dstT[:, half * 8:(half + 1) * 8, t * 128:(t + 1) * 128]
                    inv = pv[:, 0:1024].rearrange("p (k n) -> p k n", k=8)
                    if eng is act:
                        act.op(lambda e: e.copy(out=outv, in_=inv), reads=[pb], writes=[dst_bufs[t]])
                    else:
                        dve.op(lambda e: e.tensor_copy(out=outv, in_=inv), reads=[pb], writes=[dst_bufs[t]])

        if wlog is None:
            wlog = []
        wstate = {"i": 0, "issued": {}}

        def w_issue(desc):
            wt, wb = wring.next()
            kind = desc[0]
            if kind == "fm":
                _, name, c0, g, KC, ncol = desc
                src = Wd[name]
                n = g * KC * ncol
                assert n <= 12288
                if KC * ncol <= 2048:
                    pool.dma(wt[:, 0:n].rearrange("p (g x) -> p g x", g=g),
                             src[c0:c0 + g].rearrange("g p k n -> p g (k n)"), writes=[wb], max_dma_last_dim=8192)
                else:
                    m = KC * ncol
                    for gg in range(g):
                        pool.dma(wt[:, gg * m:(gg + 1) * m], src[c0 + gg].rearrange("p k n -> p (k n)"), writes=[wb],
                                 max_dma_last_dim=8192)
                return wt[:, 0:n].rearrange("p (g k n) -> p g k n", g=g, k=KC), wb
            _, name, idx, KC, ncol = desc
            src = Wd[name] if idx is None else Wd[name][idx]
            n = KC * ncol
            assert n <= 12288
            pool.dma(wt[:, 0:n], src.rearrange("p k n -> p (k n)"), writes=[wb], max_dma_last_dim=8192)
            return wt[:, 0:n].rearrange("p (k n) -> p k n", k=KC), wb

        def wload(desc):
            i = wstate["i"]
            wstate["i"] += 1
            wlog.append(desc)
            if wplan is None:
                return w_issue(desc)
            assert wplan[i] == desc, (i, wplan[i], desc)
            for k in (i, i + 1):
                if k < len(wplan) and k not in wstate["issued"]:
                    wstate["issued"][k] = w_issue(wplan[k])
            return wstate["issued"].pop(i)

        def load_w_fm(name, c0, g, KC, ncol=128):
            return wload(("fm", name, c0, g, KC, ncol))

        def load_w_tm(name, KC, ncol, idx=None):
            return wload(("tm", name, idx, KC, ncol))

        def linear_fm(src, nch, KC, rhs_fn, nblk, epi, G=4, M=128, ncol=128):
            c = 0
            pending = None
            groups = []
            while c < nch:
                g = min(G, nch - c)
                groups.append((c, g))
                c += g
            for gi, (c0, g) in enumerate(groups):
                wv, wb = load_w_fm(src, c0, g, KC, ncol)
                for cc in range(g):
                    for b in range(nblk):
                        pt, pb = psw.next()
                        for kc in range(KC):
                            rap, rbufs = rhs_fn(kc, b)
                            pe.op(lambda e: e.matmul(pt[0:M, :], lhsT=wv[:, cc, kc, 0:M], rhs=rap,
                                                     start=(kc == 0), stop=(kc == KC - 1)),
                                  reads=[wb] + rbufs, writes=[pb], signal=(kc == KC - 1))
                        epi(c0 + cc, b, pt, pb)

        def softmax_epilogue(Ot, Ob, Dt, Db, dst_ap, dst_bufs, bias_ap=None, bias_bufs=()):
            lt, lb = f32ring.next()
            if bias_ap is None:
                act.op(lambda e: e.activation(out=lt[:], in_=Dt[:], func=AF.Ln), reads=[Db], writes=[lb])
            else:
                act.op(lambda e: e.activation(out=lt[:], in_=Dt[:], func=AF.Ln, bias=bias_ap),
                       reads=[Db] + list(bias_bufs), writes=[lb])
            act.op(lambda e: e.activation(out=lt[:], in_=lt[:], func=AF.Exp, scale=-1.0), reads=[lb], writes=[lb])
            ot, ob = bfring.next()
            dve.op(lambda e: e.tensor_tensor(out=ot[:], in0=Ot[:], in1=lt[:], op=ALU.mult),
                   reads=[Ob, lb], writes=[ob])
            sp.dma(dst_ap, ot[:], reads=[ob], writes=list(dst_bufs))

        def mem_prep():
            with ExitStack() as st:
                load_gain(5)
                mb = [Buf() for _ in range(2)]
                norm_transpose(lambda t: (mem_in[t * 128:(t + 1) * 128, :], []), 2, memT, mb, st)
                K.barrier()

        def mem_kv(L):
            p = "L%d_" % L

            def epi(c, b, pt, pb):
                act.op(lambda e: e.copy(out=mkT[:, c, :], in_=pt[:, 0:256]), reads=[pb], writes=[mkT_b])
            c = 0
            for c0 in range(0, 8, 4):
                wv, wb = load_w_fm(p + "mk", c0, 4, 16)
                for cc in range(4):
                    pt, pb = psw.next()
                    for kc in range(16):
                        pe.op(lambda e: e.matmul(pt[:, 0:256], lhsT=wv[:, cc, kc, :], rhs=memT[:, kc, :],
                                                 start=(kc == 0), stop=(kc == 15)),
                              reads=[wb, memT_b], writes=[pb], signal=(kc == 15))
                    epi(c0 + cc, 0, pt, pb)
            for nb in range(2):
                wv, wb = load_w_tm(p + "mv", 16, 512, idx=nb)
                for mc in range(2):
                    pt, pb = psw.next()
                    for kc in range(16):
                        pe.op(lambda e: e.matmul(pt[:], lhsT=memT[:, kc, mc * 128:(mc + 1) * 128], rhs=wv[:, kc, :],
                                                 start=(kc == 0), stop=(kc == 15)),
                              reads=[wb, memT_b], writes=[pb], signal=(kc == 15))
                    act.op(lambda e: e.copy(out=mv[:, mc, nb * 512:(nb + 1) * 512], in_=pt[:]),
                           reads=[pb], writes=[mv_b])

        def mem_attention(mq, mq_b, tok0, tbb=TBB):
            for hm in range(4):
                for b in range(tbb):
                    ps_tiles = []
                    for mc in range(2):
                        pt, pb = psw.next()
                        for dc in range(2):
                            pe.op(lambda e: e.matmul(pt[:], lhsT=mkT[:, 2 * hm + dc, mc * 128:(mc + 1) * 128],
                                                     rhs=mq[:, 2 * hm + dc, b * 512:(b + 1) * 512],
                                                     start=(dc == 0), stop=(dc == 1)),
                                  reads=[mkT_b, mq_b], writes=[pb], signal=(dc == 1))
                        et, eb = bfring.next()
                        act.op(lambda e: e.activation(out=et[:], in_=pt[:], func=AF.Exp, scale=1.0 / 16.0),
                               reads=[pb], writes=[eb])
                        ps_tiles.append((et, eb))
                    Dt, Db = psa.next()
                    for mc in range(2):
                        et, eb = ps_tiles[mc]
                        pe.op(lambda e: e.matmul(Dt[:], lhsT=ones[:], rhs=et[:], start=(mc == 0), stop=(mc == 1)),
                              reads=[ones_b, eb], writes=[Db], signal=(mc == 1))
                    for dc in range(2):
                        Ot, Ob = psa.next()
                        for mc in range(2):
                            et, eb = ps_tiles[mc]
                            pe.op(lambda e: e.matmul(Ot[:], lhsT=mv[:, mc, hm * 256 + dc * 128: hm * 256 + (dc + 1) * 128],
                                                     rhs=et[:], start=(mc == 0), stop=(mc == 1)),
                                  reads=[mv_b, eb], writes=[Ob], signal=(mc == 1))
                        ch = 16 + 2 * hm + dc
                        if dc == 0:
                            lt, lb = f32ring.next()
                            act.op(lambda e: e.activation(out=lt[:], in_=Dt[:], func=AF.Ln), reads=[Db], writes=[lb])
                            act.op(lambda e: e.activation(out=lt[:], in_=lt[:], func=AF.Exp, scale=-1.0),
                                   reads=[lb], writes=[lb])
                        ot, ob = bfring.next()
                        dve.op(lambda e: e.tensor_tensor(out=ot[:], in0=Ot[:], in1=lt[:], op=ALU.mult),
                               reads=[Ob, lb], writes=[ob])
                        sp.dma(o_scr[ch, :, tok0 + b * 512: tok0 + (b + 1) * 512], ot[:], reads=[ob], writes=[o_b[ch]])

        def mq_gate(L, hT, hT_b, tok0, st, tb=TB):
            p = "L%d_" % L
            tbb = tb // 512
            mq, mq_b = K.sb(st, "mq", [128, 8, tb], BF16)

            def rhs_fn(kc, b):
                return hT[:, kc, b * 512:(b + 1) * 512], hT_b[4 * b:4 * b + 4]

            def epi_mq(c, b, pt, pb):
                act.op(lambda e: e.copy(out=mq[:, c, b * 512:(b + 1) * 512], in_=pt[:]), reads=[pb], writes=[mq_b])
            linear_fm(p + "mq", 8, 16, rhs_fn, tbb, epi_mq)
            chk("mq")
            mem_attention(mq, mq_b, tok0, tbb)
            chk("mematt")

            def epi_gate(c, b, pt, pb):
                ot, ob = bfring.next()
                act.op(lambda e: e.activation(out=ot[:], in_=pt[:], func=AF.Silu), reads=[pb], writes=[ob])
                sp.dma(sg_scr[c, :, tok0 + b * 512: tok0 + (b + 1) * 512], ot[:], reads=[ob], writes=[sg_b[c]])
            linear_fm(p + "gate", 24, 16, rhs_fn, tbb, epi_gate)

        def x_rows(L0):
            def f(t_abs):
                if L0:
                    return x_in[t_abs * 128:(t_abs + 1) * 128, :], []
                return xres[t_abs * 128:(t_abs + 1) * 128, :], xres_b[t_abs]
            return f

        def rope_fm(src32, src_b, cs, sn, cs_b, tcol0, dsts, P=64):
            pt, pb = psw.next()
            pe.op(lambda e: e.matmul(pt[0:P, :], lhsT=rotT[0:P, 0:P], rhs=src32, start=True, stop=True),
                  reads=[rotT_b, src_b], writes=[pb])
            t1, t1b = f32ring.next()
            dve.op(lambda e: e.tensor_tensor(out=t1[0:P, :], in0=src32, in1=cs[0:P, tcol0:tcol0 + 512], op=ALU.mult),
                   reads=[src_b, cs_b], writes=[t1b])
            t2, t2b = f32ring.next()
            dve.op(lambda e: e.tensor_tensor(out=t2[0:P, :], in0=pt[0:P, :], in1=sn[0:P, tcol0:tcol0 + 512], op=ALU.mult),
                   reads=[pb, cs_b], writes=[t2b])
            ot, ob = bfring.next()
            pool.op(lambda e: e.tensor_tensor(out=ot[0:P, :], in0=t1[0:P, :], in1=t2[0:P, :], op=ALU.add),
                    reads=[t1b, t2b], writes=[ob])
            for (dst_ap, dst_bufs, ps_) in dsts:
                sp.dma(dst_ap, ot[ps_, :], reads=[ob], writes=list(dst_bufs))

        def mla_pass(L, pi, first):
            p = "L%d_" % L
            tok0 = pi * TB
            with ExitStack() as so:
                cqg, cqg_b = K.sb(so, "cqg", [128, 12, TB], BF16)
                ckv, ckv_b = K.sb(so, "ckv", [128, 4, TB], BF16)
                rq, rq_b = K.sb(so, "rq", [128, TB], F32)
                rkv, rkv_b = K.sb(so, "rkv", [128, TB], F32)
                cs, cs_b = K.sb(so, "cs", [128, TB], F32)
                sn, sn_b = K.sb(so, "sn", [128, TB], F32)
                gq, gq_b = K.sb(so, "gq", [128, 12], F32)
                gkv, gkv_b = K.sb(so, "gkv", [128, 4], F32)
                qr32, qr32_b = K.sb(so, "qr32", [128, 512], F32)
                for hf in range(2):
                    sp.dma(cs[hf * 64:(hf + 1) * 64, :], cosT[:, tok0:tok0 + TB], writes=[cs_b])
                    sp.dma(sn[hf * 64:(hf + 1) * 64, :], sinT[:, tok0:tok0 + TB], writes=[cs_b])
                sp.dma(gq[:], Wd[p + "gq"], writes=[gq_b])
                sp.dma(gkv[:], Wd[p + "gkv"], writes=[gkv_b])
                with ExitStack() as sh:
                    hT, _ = K.sb(sh, "hT", [128, 16, TB], BF16)
                    hT_b = [Buf() for _ in range(TBT)]
                    with ExitStack() as sa:
                        xf = x_rows(first)
                        norm_transpose(lambda t: xf(pi * TBT + t), TBT, hT, hT_b, sa)
                    chk("A")
                    with ExitStack() as sb_:
                        kr32, kr32_b = K.sb(sb_, "kr32", [64, TB], F32)

                        def rhs_fn(kc, b):
                            return hT[:, kc, b * 512:(b + 1) * 512], hT_b[4 * b:4 * b + 4]

                        def make_epi(dst, dst_b, gcol, gcol_b, acc, nch):
                            def epi(c, b, pt, pb):
                                sq, sqb = bfring.next()
                                act.op(lambda e: e.activation(out=sq[:], in_=pt[:], func=AF.Square),
                                       reads=[pb], writes=[sqb])
                                dve.op(lambda e: e.tensor_scalar(out=dst[:, c, b * 512:(b + 1) * 512], in0=pt[:],
                                                                 scalar1=gcol[:, c:c + 1], scalar2=None, op0=ALU.mult),
                                       reads=[pb, gcol_b], writes=[dst_b])
                                at, ab = acc[b]
                                pe.op(lambda e: e.matmul(at[:], lhsT=ones[:], rhs=sq[:], start=(c == 0), stop=(c == nch - 1)),
                                      reads=[ones_b, sqb], writes=[ab])
                            return epi

                        def finish_rstd(acc, n, dst, dst_b):
                            for b in range(TBB):
                                at, ab = acc[b]
                                lt, lb = f32ring.next()
                                act.op(lambda e: e.activation(out=lt[:], in_=at[:], func=AF.Ln, bias=epsc[:], scale=1.0 / n),
                                       reads=[ab, epsc_b], writes=[lb])
                                act.op(lambda e: e.activation(out=dst[:, b * 512:(b + 1) * 512], in_=lt[:], func=AF.Exp, scale=-0.5),
                                       reads=[lb], writes=[dst_b])
                        accq = [psa.next() for _ in range(TBB)]
                        linear_fm(p + "cq", 12, 16, rhs_fn, TBB, make_epi(cqg, cqg_b, gq, gq_b, accq, 12))
                        chk("cq")
                        finish_rstd(accq, A_Q, rq, rq_b)
                        chk("cqr")
                        acck = [psa.next() for _ in range(TBB)]
                        linear_fm(p + "ckv", 4, 16, rhs_fn, TBB, make_epi(ckv, ckv_b, gkv, gkv_b, acck, 4))
                        finish_rstd(acck, A_KV, rkv, rkv_b)
                        for c in range(4):
                            dve.op(lambda e: e.tensor_tensor(out=ckv[:, c, :], in0=ckv[:, c, :], in1=rkv[:], op=ALU.mult),
                                   reads=[ckv_b, rkv_b], writes=[ckv_b])
                        chk("ckv")
                        wv, wb = load_w_tm(p + "kr", 16, 64)
                        for b in range(TBB):
                            pt, pb = psw.next()
                            for kc in range(16):
                                rap, rbufs = rhs_fn(kc, b)
                                pe.op(lambda e: e.matmul(pt[0:64, :], lhsT=wv[:, kc, :], rhs=rap, start=(kc == 0), stop=(kc == 15)),
                                      reads=[wb] + rbufs, writes=[pb], signal=(kc == 15))
                            dve.op(lambda e: e.tensor_copy(out=kr32[:, b * 512:(b + 1) * 512], in_=pt[0:64, :]),
                                   reads=[pb], writes=[kr32_b])
                            rope_fm(kr32[:, b * 512:(b + 1) * 512], kr32_b, cs, sn, cs_b, b * 512,
                                    [(kr_scr[:, tok0 + b * 512: tok0 + (b + 1) * 512], [kr_b], slice(0, 64))], P=64)
                        chk("kr")
                        mq_gate(L, hT, hT_b, tok0, sb_)
                chk("B1")
                with ExitStack() as s2:
                    for h0 in range(0, 16, 4):
                        wv, wb = load_w_fm(p + "uqn", h0, 4, 12)
                        for hh in range(4):
                            h = h0 + hh
                            for b in range(TBB):
                                pt, pb = psw.next()
                                for kc in range(12):
                                    pe.op(lambda e: e.matmul(pt[:], lhsT=wv[:, hh, kc, :], rhs=cqg[:, kc, b * 512:(b + 1) * 512],
                                                             start=(kc == 0), stop=(kc == 11)),
                                          reads=[wb, cqg_b], writes=[pb], signal=(kc == 11))
                                ot, ob = bfring.next()
                                dve.op(lambda e: e.tensor_tensor(out=ot[:], in0=pt[:], in1=rq[:, b * 512:(b + 1) * 512], op=ALU.mult),
                                       reads=[pb, rq_b], writes=[ob])
                                sp.dma(q_scr[h, 0:128, tok0 + b * 512: tok0 + (b + 1) * 512], ot[:], reads=[ob], writes=[q_b[h]])
                    for p0 in range(0, 8, 4):
                        wv, wb = load_w_fm(p + "uqr", p0, 4, 12)
                        for pp in range(4):
                            h = 2 * (p0 + pp)
                            for b in range(TBB):
                                pt, pb = psw.next()
                                for kc in range(12):
                                    pe.op(lambda e: e.matmul(pt[:], lhsT=wv[:, pp, kc, :], rhs=cqg[:, kc, b * 512:(b + 1) * 512],
                                                             start=(kc == 0), stop=(kc == 11)),
                                          reads=[wb, cqg_b], writes=[pb], signal=(kc == 11))
                                dve.op(lambda e: e.tensor_tensor(out=qr32[:], in0=pt[:], in1=rq[:, b * 512:(b + 1) * 512], op=ALU.mult),
                                       reads=[pb, rq_b], writes=[qr32_b])
                                tsl = slice(tok0 + b * 512, tok0 + (b + 1) * 512)
                                rope_fm(qr32[:], qr32_b, cs, sn, cs_b, b * 512,
                                        [(q_scr[h, 128:192, tsl], [q_b[h]], slice(0, 64)),
                                         (q_scr[h + 1, 128:192, tsl], [q_b[h + 1]], slice(64, 128))], P=128)
                    for h0 in range(0, 16, 8):
                        wv, wb = load_w_fm(p + "ukvk", h0, 8, 4)
                        for hh in range(8):
                            h = h0 + hh
                            for b in range(TBB):
                                pt, pb = psw.next()
                                for kc in range(4):
                                    pe.op(lambda e: e.matmul(pt[:], lhsT=wv[:, hh, kc, :], rhs=ckv[:, kc, b * 512:(b + 1) * 512],
                                                             start=(kc == 0), stop=(kc == 3)),
                                          reads=[wb, ckv_b], writes=[pb], signal=(kc == 3))
                                ot, ob = bfring.next()
                                act.op(lambda e: e.copy(out=ot[:], in_=pt[:]), reads=[pb], writes=[ob])
                                sp.dma(k_scr[h, :, tok0 + b * 512: tok0 + (b + 1) * 512], ot[:], reads=[ob], writes=[k_b[h]])
                    for hg in range(4):
                        wv, wb = load_w_fm(p + "ukvv", hg, 1, 4, ncol=512)
                        for t in range(TBT):
                            pt, pb = psw.next()
                            for kc in range(4):
                                pe.op(lambda e: e.matmul(pt[:], lhsT=ckv[:, kc, t * 128:(t + 1) * 128], rhs=wv[:, 0, kc, :],
                                                         start=(kc == 0), stop=(kc == 3)),
                                      reads=[wb, ckv_b], writes=[pb], signal=(kc == 3))
                            ot, ob = bfring.next()
                            act.op(lambda e: e.copy(out=ot[:], in_=pt[:]), reads=[pb], writes=[ob])
                            ta = pi * TBT + t
                            sp.dma(v_scr[hg * 4:(hg + 1) * 4, :, ta * 128:(ta + 1) * 128].rearrange("h p d -> p h d"),
                                   ot[:].rearrange("p (h d) -> p h d", h=4), reads=[ob], writes=v_b[hg * 4:(hg + 1) * 4])
            K.barrier()

        def mla_attention():
            scale = (128 + 64) ** -0.5
            with ExitStack() as st:
                krf, krf_b = K.sb(st, "krf", [64, S], BF16)
                sp.dma(krf[:], kr_scr, reads=[kr_b], writes=[krf_b])
                qn = Ring([K.sb(st, "qn%d" % i, [128, S], BF16) for i in range(2)])
                qr = Ring([K.sb(st, "qr%d" % i, [64, S], BF16) for i in range(2)])
                kn = Ring([K.sb(st, "kn%d" % i, [128, S], BF16) for i in range(2)])
                vv = Ring([K.sb(st, "vv%d" % i, [128, S], BF16) for i in range(2)])

                def load_head(h):
                    a = qn.next(); b_ = qr.next(); c = kn.next(); d = vv.next()
                    sp.dma(a[0][:], q_scr[h, 0:128, :], reads=[q_b[h]], writes=[a[1]])
                    sp.dma(b_[0][:], q_scr[h, 128:192, :], reads=[q_b[h]], writes=[b_[1]])
                    sp.dma(c[0][:], k_scr[h], reads=[k_b[h]], writes=[c[1]])
                    sp.dma(d[0][:], v_scr[h], reads=[v_b[h]], writes=[d[1]])
                    return a, b_, c, d
                nxt = load_head(0)
                for h in range(16):
                    (qnt, qnb), (qrt, qrb), (knt, knb), (vt, vb) = nxt
                    if h + 1 < 16:
                        nxt = load_head(h + 1)
                    for b in range(4):
                        Ot, Ob = psa.next()
                        Dt, Db = psa.next()
                        nj = 4 * (b + 1)

                        def emit_S(j):
                            jj = j - 4 * b
                            c0 = 128 * jj if jj > 0 else 0
                            St, Sb = psw.next()
                            js = slice(j * 128, (j + 1) * 128)
                            ts = slice(b * 512 + c0, (b + 1) * 512)
                            diag = jj >= 0
                            pe.op(lambda e: e.matmul(St[:, c0:512], lhsT=knt[:, js], rhs=qnt[:, ts], start=True, stop=False),
                                  reads=[knb, qnb], writes=[Sb], signal=False)
                            pe.op(lambda e: e.matmul(St[:, c0:512], lhsT=krf[:, js], rhs=qrt[:, ts], start=False, stop=not diag),
                                  reads=[krf_b, qrb], writes=[Sb], signal=not diag)
                            if diag:
                                pe.op(lambda e: e.matmul(St[:, c0:c0 + 128], lhsT=ident[:], rhs=cmask[:, 0, :], start=False, stop=True),
                                      reads=[ident_b, cmask_b], writes=[Sb])
                            return St, Sb, c0, js
                        pend = [emit_S(jx) for jx in range(min(2, nj))]
                        for j in range(nj):
                            St, Sb, c0, js = pend.pop(0)
                            if j + 2 < nj:
                                pend.append(emit_S(j + 2))
                            Pt, Pb = bfring.next()
                            act.op(lambda e: e.activation(out=Pt[:, c0:512], in_=St[:, c0:512], func=AF.Exp, scale=scale),
                                   reads=[Sb], writes=[Pb])
                            pe.op(lambda e: e.matmul(Ot[:, c0:512], lhsT=vt[:, js], rhs=Pt[:, c0:512], start=(j == 0), stop=(j == nj - 1)),
                                  reads=[vb, Pb], writes=[Ob], signal=(j == nj - 1))
                            pe.op(lambda e: e.matmul(Dt[:, c0:512], lhsT=ones[:], rhs=Pt[:, c0:512], start=(j == 0), stop=(j == nj - 1)),
                                  reads=[ones_b, Pb], writes=[Db], signal=True)
                        softmax_epilogue(Ot, Ob, Dt, Db, o_scr[h, :, b * 512:(b + 1) * 512], [o_b[h]])
            K.barrier()

        def gqa_pass(L, pi, first, dsa):
            p = "L%d_" % L
            TB_, TBT_, TBB_ = GTB, GTB // 128, GTB // 512
            tok0 = pi * TB_
            with ExitStack() as sh:
                hT, _ = K.sb(sh, "hT", [128, 16, TB_], BF16)
                hT_b = [Buf() for _ in range(TBT_)]
                with ExitStack() as sa:
                    xf = x_rows(first)
                    norm_transpose(lambda t: xf(pi * TBT_ + t), TBT_, hT, hT_b, sa)
                with ExitStack() as sb_:
                    def rhs_fn(kc, b):
                        return hT[:, kc, b * 512:(b + 1) * 512], hT_b[4 * b:4 * b + 4]

                    def epi_to(scr, bufs):
                        def epi(c, b, pt, pb):
                            ot, ob = bfring.next()
                            if (c + b) % 2 == 0:
                                act.op(lambda e: e.copy(out=ot[:], in_=pt[:]), reads=[pb], writes=[ob])
                            else:
                                dve.op(lambda e: e.tensor_copy(out=ot[:], in_=pt[:]), reads=[pb], writes=[ob])
                            sp.dma(scr[c, 0:128, tok0 + b * 512: tok0 + (b + 1) * 512], ot[:], reads=[ob], writes=[bufs[c]])
                        return epi
                    linear_fm(p + "q", 16, 16, rhs_fn, TBB_, epi_to(q_scr, q_b))
                    linear_fm(p + "k", 2, 16, rhs_fn, TBB_, epi_to(k_scr, k_b))
                    wv, wb = load_w_tm(p + "v", 16, 256)
                    vview = v_scr.rearrange("a p s -> (a p s)")[0:S * 256].rearrange("(t p d) -> t p d", p=128, d=256)
                    for t in range(TBT_):
                        pt, pb = psw.next()
                        for kc in range(16):
                            pe.op(lambda e: e.matmul(pt[:, 0:256], lhsT=hT[:, kc, t * 128:(t + 1) * 128], rhs=wv[:, kc, :],
                                                     start=(kc == 0), stop=(kc == 15)),
                                  reads=[wb, hT_b[t]], writes=[pb], signal=(kc == 15))
                        ot, ob = bfring.next()
                        act.op(lambda e: e.copy(out=ot[:, 0:256], in_=pt[:, 0:256]), reads=[pb], writes=[ob])
                        sp.dma(vview[pi * TBT_ + t], ot[:, 0:256], reads=[ob], writes=[v_b[0]])
                    if dsa:
                        linear_fm(p + "iq", 8, 16, rhs_fn, TBB_, epi_to(iq_scr, iq_b))
                        wv, wb = load_w_tm(p + "ik", 16, 64)
                        for b in range(TBB_):
                            pt, pb = psw.next()
                            for kc in range(16):
                                rap, rbufs = rhs_fn(kc, b)
                                pe.op(lambda e: e.matmul(pt[0:64, :], lhsT=wv[:, kc, :], rhs=rap, start=(kc == 0), stop=(kc == 15)),
                                      reads=[wb] + rbufs, writes=[pb], signal=(kc == 15))
                            ot, ob = bfring.next()
                            act.op(lambda e: e.copy(out=ot[0:64, :], in_=pt[0:64, :]), reads=[pb], writes=[ob])
                            sp.dma(ik_scr[:, tok0 + b * 512: tok0 + (b + 1) * 512], ot[0:64, :], reads=[ob], writes=[ik_b])
                        wv, wb = load_w_tm(p + "iw", 16, 16)
                        for t in range(TBT_):
                            pt, pb = psw.next()
                            for kc in range(16):
                                pe.op(lambda e: e.matmul(pt[:, 0:16], lhsT=hT[:, kc, t * 128:(t + 1) * 128], rhs=wv[:, kc, :],
                                                         start=(kc == 0), stop=(kc == 15)),
                                      reads=[wb, hT_b[t]], writes=[pb], signal=(kc == 15))
                            ft, fb = f32ring.next()
                            act.op(lambda e: e.copy(out=ft[:, 0:16], in_=pt[:, 0:16]), reads=[pb], writes=[fb])
                            ta = pi * TBT_ + t
                            sp.dma(iw_scr[ta * 128:(ta + 1) * 128, :], ft[:, 0:16], reads=[fb], writes=[iw_b])
                    mq_gate(L, hT, hT_b, tok0, sb_, tb=TB_)
            K.barrier()

        def dsa_indexer(st):
            iqr = Ring([K.sb(st, "iq%d" % i, [128, 8, 128], BF16) for i in range(2)])
            ik2, ik2_b = K.sb(st, "ik2", [128, S], BF16)
            accs = Ring([K.sb(st, "acc%d" % i, [128, S], F32) for i in range(4)])
            mtms = Ring([K.sb(st, "mtm%d" % i, [128, S], BF16) for i in range(4)])
            junks = [K.sb(st, "junkD%d" % i, [128, S], BF16) for i in range(2)]
            iwt = Ring([K.sb(st, "iwt%d" % i, [128, 16], F32) for i in range(2)])
            dgs = Ring([K.sb(st, "dg%d" % i, [128, 16, 128], F32) for i in range(2)])
            rls = Ring([K.sb(st, "rl%d" % i, [128, 512], F32) for i in range(6)])
            msts = Ring([K.sb(st, "mst%d" % i, [128, 1024], BF16) for i in range(2)])
            thrs = Ring([K.sb(st, "thr%d" % i, [128, 4], F32) for i in range(4)])
            id32, id32_b = K.sb(st, "id32", [128, 128], F32)
            sp.dma(id32[:], ident_d, writes=[id32_b])
            sp.dma(ik2[0:64, :], ik_scr, reads=[ik_b], writes=[ik2_b])
            sp.dma(ik2[64:128, :], ik_scr, reads=[ik_b], writes=[ik2_b])
            RNG = 512.0
            NIT = 28

            def accumulate(tt):
                Lk = (tt + 1) * 128
                acc, acc_b = accs.next()
                wt, wtb = iwt.next()
                sp.dma(wt[:], iw_scr[tt * 128:(tt + 1) * 128, :], reads=[iw_b], writes=[wtb])
                iq, iq_sb = iqr.next()
                sp.dma(iq[:], iq_scr[:, :, tt * 128:(tt + 1) * 128].rearrange("c p t -> p c t"), reads=iq_b, writes=[iq_sb])
                dg, dgb = dgs.next()
                for hi in range(16):
                    dve.op(lambda e: e.tensor_scalar(out=dg[:, hi, :], in0=id32[:], scalar1=wt[:, hi:hi + 1], scalar2=None, op0=ALU.mult),
                           reads=[id32_b, wtb], writes=[dgb], signal=(hi == 15))
                nsb = (Lk + 511) // 512
                for sbk in range(nsb):
                    c0 = sbk * 512
                    n = min(512, Lk - c0)
                    aP, aPb = psa.next()
                    for hp in range(8):
                        dA, dAb = psw.next()
                        dB, dBb = psw.next()
                        pe.op(lambda e: e.matmul(dA[:, 0:n], lhsT=iq[0:64, hp, :], rhs=ik2[0:64, c0:c0 + n],
                                                 start=True, stop=True), reads=[iq_sb, ik2_b], writes=[dAb])
                        pe.op(lambda e: e.matmul(dB[:, 0:n], lhsT=iq[64:128, hp, :], rhs=ik2[64:128, c0:c0 + n],
                                                 start=True, stop=True), reads=[iq_sb, ik2_b], writes=[dBb])
                        rts = []
                        for (dt_, db_) in ((dA, dAb), (dB, dBb)):
                            rt, rb = rls.next()
                            act.op(lambda e: e.activation(out=rt[:, 0:n], in_=dt_[:, 0:n], func=AF.Relu), reads=[db_], writes=[rb])
                            rts.append((rt, rb))
                        for e_, (rt, rb) in enumerate(rts):
                            hi = 2 * hp + e_
                            pe.op(lambda e: e.matmul(aP[:, 0:n], lhsT=dg[:, hi, :], rhs=rt[:, 0:n], start=(hi == 0), stop=(hi == 15)),
                                  reads=[dgb, rb], writes=[aPb], signal=True)
                    act.op(lambda e: e.copy(out=acc[:, c0:c0 + n], in_=aP[:, 0:n]), reads=[aPb], writes=[acc_b])
                pool.op(lambda e: e.tensor_tensor(out=acc[:, tt * 128:Lk], in0=acc[:, tt * 128:Lk], in1=negtm[:], op=ALU.add),
                        reads=[acc_b, negtm_b], writes=[acc_b])
                return acc, acc_b

            def bisect(group):
                outs = []
                chains = []
                for (tt, acc, acc_b) in group:
                    Lk = (tt + 1) * 128
                    mtm, mtm_b = mtms.next()
                    outs.append((tt, mtm, mtm_b))
                    if tt < 2:
                        dve.op(lambda e: e.tensor_scalar(out=mtm[:, 0:Lk], in0=acc[:, 0:Lk], scalar1=-1e29, scalar2=None,
                                                         op0=ALU.is_gt), reads=[acc_b], writes=[mtm_b])
                    else:
                        ct, cb = thrs.next()
                        dve.op(lambda e: e.memset(ct[:, 0:1], 0.0), writes=[cb])
                        junkD, junkD_b = junks[len(chains)]
                        chains.append((tt, acc, acc_b, mtm, mtm_b, ct, cb, Lk, junkD, junkD_b))
                step = RNG
                for k in range(NIT):
                    step = step * 0.5
                    for (tt, acc, acc_b, mtm, mtm_b, ct, cb, Lk, junkD, junkD_b) in chains:
                        dve.op(lambda e: e.tensor_scalar(out=junkD[:, 0:Lk], in0=acc[:, 0:Lk], scalar1=ct[:, 0:1], scalar2=None,
                                                         op0=ALU.is_ge, op1=ALU.add, accum_out=ct[:, 1:2]),
                               reads=[acc_b, cb], writes=[cb, junkD_b])
                    for (tt, acc, acc_b, mtm, mtm_b, ct, cb, Lk, junkD, junkD_b) in chains:
                        dve.op(lambda e: e.tensor_scalar(out=ct[:, 2:3], in0=ct[:, 1:2], scalar1=255.5, scalar2=2.0 * step,
                                                         op0=ALU.is_ge, op1=ALU.mult), reads=[cb], writes=[cb])
                    for (tt, acc, acc_b, mtm, mtm_b, ct, cb, Lk, junkD, junkD_b) in chains:
                        dve.op(lambda e: e.scalar_tensor_tensor(out=ct[:, 0:1], in0=ct[:, 2:3], scalar=-step, in1=ct[:, 0:1],
                                                                op0=ALU.add, op1=ALU.add), reads=[cb], writes=[cb])
                for (tt, acc, acc_b, mtm, mtm_b, ct, cb, Lk, junkD, junkD_b) in chains:
                    dve.op(lambda e: e.tensor_scalar(out=ct[:, 0:1], in0=ct[:, 0:1], scalar1=-step, scalar2=None, op0=ALU.add),
                           reads=[cb], writes=[cb])
                    dve.op(lambda e: e.tensor_scalar(out=mtm[:, 0:Lk], in0=acc[:, 0:Lk], scalar1=ct[:, 0:1], scalar2=None,
                                                     op0=ALU.is_ge), reads=[acc_b, cb], writes=[mtm_b])
                return outs

            def transposes(group):
                for (tt, mtm, mtm_b) in group:
                    for j0 in range(0, tt + 1, 8):
                        nb_ = min(8, tt + 1 - j0)
                        pt, pb = psw.next()
                        pv = pt[:].bitcast(BF16)
                        for q in range(nb_):
                            j = j0 + q
                            pe.op(lambda e: e.transpose(out=pv[:, q * 128:(q + 1) * 128], in_=mtm[:, j * 128:(j + 1) * 128], identity=ident[:]),
                                  reads=[mtm_b, ident_b], writes=[pb], signal=(q == nb_ - 1))
                        ms, msb = msts.next()
                        act.op(lambda e: e.copy(out=ms[:, 0:nb_ * 128], in_=pv[:, 0:nb_ * 128]), reads=[pb], writes=[msb])
                        sp.dma(mask_scr[j0:j0 + nb_, :, tt * 128:(tt + 1) * 128].rearrange("j p t -> p j t"),
                               ms[:, 0:nb_ * 128].rearrange("p (j t) -> p j t", j=nb_), reads=[msb], writes=[mask_b])
            groups = [[2 * i + 1, 2 * i] for i in range(7, -1, -1)]
            accd = {}
            bis = {}
            for i in range(len(groups) + 2):
                if i < len(groups):
                    accd[i] = [(tt,) + accumulate(tt) for tt in groups[i]]
                if 0 <= i - 1 < len(groups):
                    bis[i - 1] = bisect(accd.pop(i - 1))
                if 0 <= i - 2 < len(groups):
                    transposes(bis.pop(i - 2))

        def gqa_attention(L, dsa):
            with ExitStack() as st:
                if dsa:
                    with ExitStack() as si:
                        dsa_indexer(si)
                    K.barrier()
                    maskT, maskT_b = K.sb(st, "maskT", [128, 16, S], BF16)
                    for j in range(16):
                        sp.dma(maskT[:, j, :], mask_scr[j], reads=[mask_b], writes=[maskT_b])
                vall, vall_b = K.sb(st, "vall", [128, 16, 256], BF16)
                vview = v_scr.rearrange("a p s -> (a p s)")[0:S * 256].rearrange("(t p d) -> p t d", p=128, d=256)
                sp.dma(vall[:], vview, reads=[v_b[0]], writes=[vall_b])
                cbt, cbt_b = K.sb(st, "cbt", [128, 32], F32)
                sp.dma(cbt[:], cb_d.partition_broadcast(128), writes=[cbt_b])
                sk, sk_b = K.sb(st, "sk", [128, 16], F32)
                if not dsa:
                    sp.dma(sk[:], sink_d, writes=[sk_b])
                    act.op(lambda e: e.activation(out=sk[:], in_=sk[:], func=AF.Exp), reads=[sk_b], writes=[sk_b])
                psS = Ring(banks[0:6])
                psOD = Ring(banks[6:8])
                k2s = Ring([K.sb(st, "k2_%d" % i, [128, S], BF16) for i in range(2)])
                qcs = Ring([K.sb(st, "qc%d" % i, [128, S], BF16) for i in range(2)])
                wbts = Ring([K.sb(st, "wbt%d" % i, [128, 2, 640], F32) for i in range(2)])
                qv = q_scr

                def load_chunk(c):
                    a = qcs.next(); w_ = wbts.next()
                    sp.dma(a[0][:], qv[c, 0:128, :], reads=[q_b[c]], writes=[a[1]])
                    sp.dma(w_[0][:], wbias_d[c], writes=[w_[1]])
                    return a, w_

                def load_k(g):
                    kk = k2s.next()
                    src = k_scr[g // 2, (g % 2) * 64:(g % 2) * 64 + 64, :]
                    sp.dma(kk[0][0:64, :], src, reads=[k_b[g // 2]], writes=[kk[1]])
                    sp.dma(kk[0][64:128, :], src, reads=[k_b[g // 2]], writes=[kk[1]])
                    return kk
                nxt = load_chunk(0)
                kcur = None
                for c in range(16):
                    g = c // 4
                    if c % 4 == 0:
                        kcur = load_k(g)
                    k2, k2b = kcur
                    (qc, qcb), (wbt, wbtb) = nxt
                    if c + 1 < 16:
                        nxt = load_chunk(c + 1)
                    for b in range(4):
                        Ot, Ob = psOD.next()
                        Dt, Db = psOD.next()
                        if dsa:
                            jl = list(range(0, 4 * (b + 1)))
                        else:
                            jl = [j for j in range(4 * b - 1, 4 * b + 4) if j >= 0]
                        def emit_S(j):
                            jj = j - 4 * b
                            if dsa:
                                lo = max(jj, 0) * 128
                                hi_ = 512
                            else:
                                lo = max(128 * jj, 0)
                                hi_ = min(128 * jj + 256, 512)
                            n = hi_ - lo
                            woff = lo - 128 * jj if jj >= -1 else None
                            js = slice(j * 128, (j + 1) * 128)
                            ts = slice(b * 512 + lo, b * 512 + hi_)
                            SA, SAb = psS.next()
                            SB, SBb = psS.next()
                            only = dsa and jj < -1
                            pe.op(lambda e: e.matmul(SA[:, lo:hi_], lhsT=k2[0:64, js], rhs=qc[0:64, ts], start=True, stop=only),
                                  reads=[k2b, qcb], writes=[SAb], signal=False)
                            pe.op(lambda e: e.matmul(SB[:, lo:hi_], lhsT=k2[64:128, js], rhs=qc[64:128, ts], start=True, stop=only),
                                  reads=[k2b, qcb], writes=[SBb], signal=only)
                            if dsa:
                                mrhs = maskT[:, j, ts]
                                mb_ = [maskT_b]
                                mid, midb = identC, identC_b
                            else:
                                mrhs = cmask[:, 0:2, :].rearrange("p a n -> p (a n)")[:, woff:woff + n]
                                mb_ = [cmask_b]
                                mid, midb = ident, ident_b
                            if dsa and jj < -1:
                                return (SA, SAb), (SB, SBb), lo, hi_, n, woff, jj, mrhs
                            pe.op(lambda e: e.matmul(SA[:, lo:hi_], lhsT=mid[:], rhs=mrhs, start=False, stop=True),
                                  reads=[midb] + mb_, writes=[SAb])
                            pe.op(lambda e: e.matmul(SB[:, lo:hi_], lhsT=mid[:], rhs=mrhs, start=False, stop=True),
                                  reads=[midb] + mb_, writes=[SBb])
                            return (SA, SAb), (SB, SBb), lo, hi_, n, woff, jj, mrhs
                        pend = [emit_S(jx) for jx in jl[0:2]]
                        for ji, j in enumerate(jl):
                            (SA, SAb), (SB, SBb), lo, hi_, n, woff, jj, mrhs = pend.pop(0)
                            if ji + 2 < len(jl):
                                pend.append(emit_S(jl[ji + 2]))
                            special = jj >= -1
                            for e_, (St, Sb) in enumerate(((SA, SAb), (SB, SBb))):
                                h = 2 * c + e_
                                Pt, Pb = bfring.next()
                                if special:
                                    ft, fb = f32ring.next()
                                    dve.op(lambda e: e.scalar_tensor_tensor(out=ft[:, lo:hi_], in0=St[:, lo:hi_], scalar=0.125,
                                                                            in1=wbt[:, e_, woff:woff + n], op0=ALU.mult, op1=ALU.add),
                                           reads=[Sb, wbtb], writes=[fb])
                                    if dsa:
                                        act.op(lambda e: e.activation(out=Pt[:, lo:hi_], in_=ft[:, lo:hi_], func=AF.Exp, bias=negC8[:]),
                                               reads=[fb, negC8_b], writes=[Pb])
                                    else:
                                        act.op(lambda e: e.activation(out=Pt[:, lo:hi_], in_=ft[:, lo:hi_], func=AF.Exp),
                                               reads=[fb], writes=[Pb])
                                else:
                                    act.op(lambda e: e.activation(out=Pt[:, lo:hi_], in_=St[:, lo:hi_], func=AF.Exp, scale=0.125,
                                                                  bias=cbt[:, h:h + 1]),
                                           reads=[Sb, cbt_b], writes=[Pb])
                                    dve.op(lambda e: e.tensor_tensor(out=Pt[:, lo:hi_], in0=Pt[:, lo:hi_], in1=mrhs, op=ALU.mult),
                                           reads=[Pb, maskT_b], writes=[Pb])
                                first = (j == jl[0])
                                last = (j == jl[-1])
                                pe.op(lambda e: e.matmul(Ot[e_ * 64:(e_ + 1) * 64, lo:hi_], lhsT=vall[:, j, g * 64:(g + 1) * 64],
                                                         rhs=Pt[:, lo:hi_], start=first, stop=last, skip_group_check=True),
                                      reads=[vall_b, Pb], writes=[Ob], signal=(last and e_ == 1))
                                pe.op(lambda e: e.matmul(Dt[e_ * 64:(e_ + 1) * 64, lo:hi_], lhsT=ones[:, 0:64],
                                                         rhs=Pt[:, lo:hi_], start=first, stop=last, skip_group_check=True),
                                      reads=[ones_b, Pb], writes=[Db], signal=True)
                        if dsa:
                            softmax_epilogue(Ot, Ob, Dt, Db, o_scr[c, :, b * 512:(b + 1) * 512], [o_b[c]])
                        else:
                            softmax_epilogue(Ot, Ob, Dt, Db, o_scr[c, :, b * 512:(b + 1) * 512], [o_b[c]],
                                             bias_ap=sk[:, c:c + 1], bias_bufs=[sk_b])
            K.barrier()

        def out_proj(L, first):
            p = "L%d_" % L
            with ExitStack() as st:
                yT, _ = K.sb(st, "yT", [128, 24, S], BF16)
                ybufs = [[Buf() for _ in range(24)] for _ in range(4)]
                ost = Ring([K.sb(st, "ost%d" % i, [128, 512], BF16) for i in range(3)])
                sgt = Ring([K.sb(st, "sgt%d" % i, [128, 512], BF16) for i in range(3)])
                xin = Ring([K.sb(st, "xin%d" % i, [128, 512], F32) for i in range(3)])
                xo = Ring([K.sb(st, "xo%d" % i, [128, 512], F32) for i in range(3)])
                cnt = [0]

                def build_chunk(q, c):
                    a = ost.next(); s_ = sgt.next()
                    sp.dma(a[0][:], o_scr[c, :, q * 512:(q + 1) * 512], reads=[o_b[c]], writes=[a[1]])
                    sp.dma(s_[0][:], sg_scr[c, :, q * 512:(q + 1) * 512], reads=[sg_b[c]], writes=[s_[1]])
                    cnt[0] += 1
                    eng = dve if cnt[0] % 4 else pool
                    eng.op(lambda e: e.tensor_tensor(out=yT[:, c, q * 512:(q + 1) * 512], in0=a[0][:], in1=s_[0][:], op=ALU.mult),
                           reads=[a[1], s_[1]], writes=[ybufs[q][c]])
                for c in range(24):
                    build_chunk(0, c)
                pending = [(q, c) for q in range(1, 4) for c in range(24)]
                for nb in range(4):
                    wv, wb = load_w_tm(p + "out", 24, 512, idx=nb)
                    for ta in range(NT):
                        for _ in range(6):
                            if pending:
                                build_chunk(*pending.pop(0))
                        xi, xib = xin.next()
                        if first:
                            sp.dma(xi[:], x_in[ta * 128:(ta + 1) * 128, nb * 512:(nb + 1) * 512], writes=[xib])
                        else:
                            sp.dma(xi[:], xres[ta * 128:(ta + 1) * 128, nb * 512:(nb + 1) * 512],
                                   reads=[xres_b[ta][nb]], writes=[xib])
                        pt, pb = psw.next()
                        for c in range(24):
                            pe.op(lambda e: e.matmul(pt[:], lhsT=yT[:, c, ta * 128:(ta + 1) * 128], rhs=wv[:, c, :],
                                                     start=(c == 0), stop=(c == 23)),
                                  reads=[wb, ybufs[ta // 4][c]], writes=[pb], signal=(c == 23))
                        xot, xob = xo.next()
                        dve.op(lambda e: e.tensor_tensor(out=xot[:], in0=pt[:], in1=xi[:], op=ALU.add),
                               reads=[pb, xib], writes=[xob])
                        sp.dma(xres[ta * 128:(ta + 1) * 128, nb * 512:(nb + 1) * 512], xot[:], reads=[xob],
                               writes=[xres_b[ta][nb]])
            K.barrier()

        def final_pass():
            with ExitStack() as st:
                xts = Ring([K.sb(st, "fx%d" % i, [128, D], F32) for i in range(4)])
                ots = Ring([K.sb(st, "fo%d" % i, [128, D], F32) for i in range(4)])
                junk, junk_b = K.sb(st, "fjunk", [128, D], BF16)
                if final_norm:
                    load_gain(4)
                fins = []
                for t in range(NT):
                    xt, xb = xts.next()
                    sp.dma(xt[:], xres[t * 128:(t + 1) * 128, :], reads=xres_b[t], writes=[xb])
                    if final_norm:
                        ct, cb = colring.next()
                        act.op(lambda e: e.activation(out=junk[:], in_=xt[:], func=AF.Square, accum_out=ct[:, 0:1]),
                               reads=[xb], writes=[junk_b, cb])
                        rs, rb = rstd_col(ct[:, 0:1], cb, D)
                        ot, ob = ots.next()
                        dve.op(lambda e: e.scalar_tensor_tensor(out=ot[:], in0=xt[:], scalar=rs, in1=gbc[:],
                                                                op0=ALU.mult, op1=ALU.mult),
                               reads=[xb, rb, gbc_b], writes=[ob])
                        fins.append(sp.dma(out_d[t * 128:(t + 1) * 128, :], ot[:], reads=[ob]))
                    else:
                        fins.append(sp.dma(out_d[t * 128:(t + 1) * 128, :], xt[:], reads=[xb]))
                for tk in fins:
                    sp.wait(tk)

        NEGM_col, NEGM_col_b = K.sb(top, "negmcol", [128, 1], F32)
        try:
            load_consts()
            dve.op(lambda e: e.memset(NEGM_col[:], NEGM), writes=[NEGM_col_b])
            chk("consts")
            mem_prep()
            chk("memprep")
            first = True
            for L in layers:
                kind = L % 3
                mem_kv(L)
                load_gain(L)
                chk("memkv")
                for pi in range(NPASS if kind == 0 else S // GTB):
                    if kind == 0:
                        mla_pass(L, pi, first)
                    else:
                        gqa_pass(L, pi, first, dsa=(kind == 1))
                    chk("pass%d" % pi)
                if kind == 0:
                    mla_attention()
                else:
                    gqa_attention(L, dsa=(kind == 1))
                chk("attn")
                out_proj(L, first)
                first = False
            final_pass()
        except _Stop:
            pass
        K.barrier()
    nc._wlog = wlog
    return nc


def host_prepare(inputs, layers):
    f = np.float32
    shared = {}
    gains = np.concatenate([inputs["norm_in"], inputs["final_norm"][None], inputs["mem_norm"][None]], 0).astype(f)
    shared["gains"] = np.ascontiguousarray(gains)
    inv = (1.0 / (10000.0 ** (np.arange(0, 64, 2, dtype=np.float32) / np.float32(64)))).astype(np.float32)
    ang = np.arange(S, dtype=np.float32)[:, None] * inv[None, :]
    cos = np.cos(ang).astype(f).T
    sin = np.sin(ang).astype(f).T
    shared["cosT"] = np.ascontiguousarray(np.concatenate([cos, cos], 0))
    shared["sinT"] = np.ascontiguousarray(np.concatenate([sin, sin], 0))
    R = np.zeros((64, 64), f)
    for d in range(32):
        R[d, d + 32] = -1.0
        R[d + 32, d] = 1.0
    shared["rotT"] = np.ascontiguousarray(R.T)
    sl = np.arange(128)[:, None]
    tl = np.arange(128)[None, :]
    cm = np.zeros((128, 3, 128), f)
    cm[:, 0, :] = np.where(sl <= tl, 0.0, NEGM)
    cm[:, 1, :] = np.where(tl < sl, 0.0, NEGM)
    cm[:, 2, :] = np.where(tl <= sl, 0.0, -1e30)
    shared["cmask"] = cm
    shared["ident"] = np.eye(128, dtype=f)
    rb = inputs["rel_bias"].astype(f)
    shared["cbias"] = np.ascontiguousarray(rb[31:32, :])
    delta_diag = tl - sl
    delta_off = tl - sl + 128
    bd = t5_bucket_np(delta_diag)
    bo = t5_bucket_np(delta_off)
    wb = np.zeros((32, 128, 640), f)
    for h in range(32):
        wb[h, :, 0:128] = rb[bd, h]
        wb[h, :, 128:256] = rb[bo, h]
        wb[h, :, 256:640] = rb[31, h]
    shared["wbias"] = np.ascontiguousarray(wb.reshape(16, 2, 128, 640).transpose(0, 2, 1, 3))
    sinks = inputs["c_sinks"][0].astype(f)
    spc = np.zeros((128, 16), f)
    for c in range(16):
        spc[0:64, c] = sinks[2 * c]
        spc[64:128, c] = sinks[2 * c + 1]
    shared["sinkpc"] = spc
    for L in layers:
        kind, j = L % 3, L // 3
        p = "L%d_" % L
        if kind == 0:
            W = inputs["w_in_a"][j]
            o = 0
            shared[p + "cq"] = chunk_fm(W[:, o:o + A_Q]); o += A_Q
            shared[p + "ckv"] = chunk_fm(W[:, o:o + A_KV]); o += A_KV
            shared[p + "kr"] = chunk_tm(W[:, o:o + A_ROPE]); o += A_ROPE
            shared[p + "mq"] = chunk_fm(W[:, o:o + MEMW]); o += MEMW
            shared[p + "gate"] = chunk_fm(W[:, o:o + BRW]); o += BRW
            shared[p + "gq"] = col_pc(inputs["a_q_norm"][j])
            shared[p + "gkv"] = col_pc(inputs["a_kv_norm"][j])
            wuq = inputs["w_uq"][j].reshape(12, 128, 16, 192)
            wq_h = wuq.transpose(2, 1, 0, 3)
            shared[p + "uqn"] = np.ascontiguousarray(wq_h[:, :, :, 0:128])
            wr = wq_h[:, :, :, 128:192].reshape(8, 2, 128, 12, 64)
            shared[p + "uqr"] = np.ascontiguousarray(wr.transpose(0, 2, 3, 1, 4).reshape(8, 128, 12, 128))
            wukv = inputs["w_ukv"][j].reshape(4, 128, 16, 256)
            shared[p + "ukvk"] = np.ascontiguousarray(wukv[:, :, :, 0:128].transpose(2, 1, 0, 3))
            wv = wukv[:, :, :, 128:256].reshape(4, 128, 4, 512)
            shared[p + "ukvv"] = np.ascontiguousarray(wv.transpose(2, 1, 0, 3))
        else:
            W = inputs["w_in_b"][j] if kind == 1 else inputs["w_in_c"][j]
            o = 0
            shared[p + "q"] = chunk_fm(W[:, o:o + 2048]); o += 2048
            shared[p + "k"] = chunk_fm(W[:, o:o + 256]); o += 256
            shared[p + "v"] = chunk_tm(W[:, o:o + 256]); o += 256
            if kind == 1:
                shared[p + "iq"] = chunk_fm(W[:, o:o + 1024]); o += 1024
                shared[p + "ik"] = chunk_tm(W[:, o:o + 64]); o += 64
                shared[p + "iw"] = chunk_tm(W[:, o:o + 16]); o += 16
            shared[p + "mq"] = chunk_fm(W[:, o:o + MEMW]); o += MEMW
            shared[p + "gate"] = chunk_fm(W[:, o:o + BRW]); o += BRW
            assert o == W.shape[1]
        Wm = inputs["w_mem_kv"][L]
        shared[p + "mk"] = chunk_fm(Wm[:, 0:1024])
        mvw = chunk_tm(Wm[:, 1024:2048])
        shared[p + "mv"] = np.ascontiguousarray(mvw.reshape(128, 16, 2, 512).transpose(2, 0, 1, 3))
        Wo = chunk_tm(inputs["w_out"][L])
        shared[p + "out"] = np.ascontiguousarray(Wo.reshape(128, 24, 4, 512).transpose(2, 0, 1, 3))
    return shared


_PROG_CACHE = {}


def build_full(layers, final_norm=True, dbg=(), stop_after=None):
    nc0 = build_program(list(layers), final_norm=final_norm, dbg=dbg, stop_after=stop_after)
    return build_program(list(layers), final_norm=final_norm, dbg=dbg, stop_after=stop_after, wplan=list(nc0._wlog))


def run_layers(inputs, layers, final_norm=True, x_override=None, dbg=(), stop_after=None, ncores=8):
    key = (tuple(layers), final_norm, tuple(dbg), stop_after)
    if key not in _PROG_CACHE:
        _PROG_CACHE[key] = build_full(layers, final_norm=final_norm, dbg=dbg, stop_after=stop_after)
    nc = _PROG_CACHE[key]
    shared = host_prepare(inputs, layers)
    x = inputs["x"] if x_override is None else x_override
    in_maps = []
    for b in range(ncores):
        m = dict(shared)
        m["x"] = np.ascontiguousarray(x[b], dtype=np.float32)
        m["mem"] = np.ascontiguousarray(inputs["mem"][b], dtype=np.float32)
        in_maps.append(m)
    res = run_bass_kernel_spmd(nc, in_maps, core_ids=list(range(ncores)))
    return res


def kernel(**inputs):
    inputs = {k: np.asarray(v) for k, v in inputs.items()}
    res = run_layers(inputs, [0, 1, 2, 3], final_norm=True)
    return np.stack([np.asarray(r["out"], dtype=np.float32) for r in res.results], 0)
```

```python
import math
import numpy as np
import concourse.bass as bass
import concourse.mybir as mybir
from concourse.bass_utils import run_bass_kernel_spmd

F32 = mybir.dt.float32
BF16 = mybir.dt.bfloat16
AF = mybir.ActivationFunctionType
ALU = mybir.AluOpType

S = 2048
D = 2048
NT = 16
EPS = 1e-6
TB = 1024
GTB = 2048
NPASS = S // TB
TBT = TB // 128
TBB = TB // 512
NEGM = -30000.0
MASKC = 2992.0

A_Q, A_KV, A_ROPE = 1536, 512, 64
MEMW = 1024
BRW = 3072


def chunk_fm(W):
    Kd, N = W.shape
    return np.ascontiguousarray(W.reshape(Kd // 128, 128, N // 128, 128).transpose(2, 1, 0, 3))


def chunk_tm(W):
    Kd, N = W.shape
    return np.ascontiguousarray(W.reshape(Kd // 128, 128, N).transpose(1, 0, 2))


def col_pc(g):
    return np.ascontiguousarray(g.reshape(-1, 128).T)


def t5_bucket_np(n):
    n = np.maximum(n, 0)
    nf = np.maximum(n, 1).astype(np.float32)
    large = 16 + (np.log(nf / np.float32(16)) / np.float32(math.log(128 / 16)) * np.float32(16)).astype(np.int32)
    large = np.minimum(large, 31)
    return np.where(n < 16, n, large)


class Buf:
    __slots__ = ("name", "w", "r", "excl")

    def __init__(self, name="", excl=False):
        self.name = name
        self.w = None
        self.r = {}
        self.excl = excl


class Eng:
    def __init__(self, K, eng, name, nring=0, selfwait=True):
        self.K = K
        self.e = eng
        self.name = name
        self.selfwait = selfwait
        self.sem = K.new_sem("s_" + name)
        self.cnt = 0
        self.waited = {}
        self.rsem = [K.new_sem("d_%s%d" % (name, i)) for i in range(nring)]
        self.rcnt = [0] * nring
        self.ri = 0
        self.pr = []
        self.pw = []

    def wait(self, tok):
        if tok is None or self.K.stopped:
            return
        sem, val = tok
        if (not self.selfwait) and sem is self.sem:
            return
        k = id(sem)
        if self.waited.get(k, 0) >= val:
            return
        self.e.wait_ge(sem, val)
        self.waited[k] = val

    def deps(self, reads, writes):
        for b in reads:
            self.wait(b.w)
            if b.excl:
                for k, t in b.r.items():
                    if k != self.name:
                        self.wait(t)
        for b in writes:
            self.wait(b.w)
            for t in b.r.values():
                self.wait(t)

    def _commit(self, tok, key, reads, writes):
        for b in writes:
            b.w = tok
            b.r = {}
        for b in reads:
            b.r[key] = tok

    def op(self, fn, reads=(), writes=(), signal=True):
        if self.K.stopped:
            return None
        self.deps(reads, writes)
        inst = fn(self.e)
        if signal:
            self.cnt += 1
            inst.then_inc(self.sem, 1)
            tok = (self.sem, self.cnt)
            self._commit(tok, self.name, list(reads) + self.pr, list(writes) + self.pw)
            self.pr = []
            self.pw = []
            return tok
        self.pr += list(reads)
        self.pw += list(writes)
        return None

    def dma(self, out, in_, reads=(), writes=(), **kw):
        if self.K.stopped:
            return None
        self.deps(reads, writes)
        k = self.ri % len(self.rsem)
        self.ri += 1
        if self.rcnt[k] > 0:
            self.wait((self.rsem[k], 16 * self.rcnt[k]))
        inst = self.e.dma_start(out=out, in_=in_, **kw)
        self.rcnt[k] += 1
        inst.then_inc(self.rsem[k], 16)
        tok = (self.rsem[k], 16 * self.rcnt[k])
        self._commit(tok, "%s_d%d" % (self.name, k), reads, writes)
        return tok

    def all_tokens(self):
        toks = []
        if self.cnt:
            toks.append((self.sem, self.cnt))
        for s, c in zip(self.rsem, self.rcnt):
            if c:
                toks.append((s, 16 * c))
        return toks


class Ring:
    def __init__(self, items):
        self.items = items
        self.i = 0

    def next(self):
        it = self.items[self.i % len(self.items)]
        self.i += 1
        return it


class Kern:
    def __init__(self, nc, stack):
        self.nc = nc
        self.stack = stack
        self.nsem = 0
        self.stopped = False
        self.pe = Eng(self, nc.tensor, "pe", selfwait=False)
        self.act = Eng(self, nc.scalar, "act")
        self.dve = Eng(self, nc.vector, "dve")
        self.pool = Eng(self, nc.gpsimd, "pool", nring=8)
        self.sp = Eng(self, nc.sync, "sp", nring=12)
        self.engs = [self.pe, self.act, self.dve, self.pool, self.sp]

    def new_sem(self, name):
        self.nsem += 1
        return self.stack.enter_context(self.nc.semaphore(name))

    def barrier(self):
        toks = []
        for e in self.engs:
            toks += e.all_tokens()
        for e in self.engs:
            for t in toks:
                if t[0] is e.sem and not e.selfwait:
                    continue
                e.wait(t)

    def sb(self, stack, name, shape, dt, nbuf=None):
        self.nsb = getattr(self, "nsb", 0) + 1
        name = "sb%d_%s" % (self.nsb, name)
        t = stack.enter_context(self.nc.sbuf_tensor(name, shape, dt))
        if nbuf is None:
            return t, Buf(name)
        return t, [Buf("%s%d" % (name, i)) for i in range(nbuf)]


def build_program(layers, final_norm=True, dbg=(), stop_after=None, wplan=None, wlog=None):
    from contextlib import ExitStack

    nc = bass.Bass("TRN2", target_bir_lowering=False)
    top = ExitStack()
    with top:
        K = Kern(nc, top)
        pe, act, dve, pool, sp = K.pe, K.act, K.dve, K.pool, K.sp

        class _Stop(Exception):
            pass

        def chk(name):
            if stop_after == name:
                K.barrier()
                K.stopped = True

        def din(name, shape, dt=F32):
            return nc.dram_tensor(name, list(shape), dt, kind="ExternalInput").ap()

        def dscr(name, shape, dt):
            kind = "ExternalOutput" if name in dbg else "Internal"
            return nc.dram_tensor(name, list(shape), dt, kind=kind).ap()

        x_in = din("x", [S, D])
        mem_in = din("mem", [256, D])
        out_d = nc.dram_tensor("out", [S, D], F32, kind="ExternalOutput").ap()
        gains = din("gains", [6, D])
        cosT = din("cosT", [64, S])
        sinT = din("sinT", [64, S])
        rotT_d = din("rotT", [64, 64])
        cmask_d = din("cmask", [128, 3, 128])
        ident_d = din("ident", [128, 128])
        cb_d = din("cbias", [1, 32])
        sink_d = din("sinkpc", [128, 16])
        wbias_d = din("wbias", [16, 128, 2, 640])

        Wd = {}
        for L in layers:
            kind, j = L % 3, L // 3
            p = "L%d_" % L
            if kind == 0:
                Wd[p + "cq"] = din(p + "cq", [12, 128, 16, 128])
                Wd[p + "ckv"] = din(p + "ckv", [4, 128, 16, 128])
                Wd[p + "kr"] = din(p + "kr", [128, 16, 64])
                Wd[p + "gq"] = din(p + "gq", [128, 12])
                Wd[p + "gkv"] = din(p + "gkv", [128, 4])
                Wd[p + "uqn"] = din(p + "uqn", [16, 128, 12, 128])
                Wd[p + "uqr"] = din(p + "uqr", [8, 128, 12, 128])
                Wd[p + "ukvk"] = din(p + "ukvk", [16, 128, 4, 128])
                Wd[p + "ukvv"] = din(p + "ukvv", [4, 128, 4, 512])
            else:
                Wd[p + "q"] = din(p + "q", [16, 128, 16, 128])
                Wd[p + "k"] = din(p + "k", [2, 128, 16, 128])
                Wd[p + "v"] = din(p + "v", [128, 16, 256])
                if kind == 1:
                    Wd[p + "iq"] = din(p + "iq", [8, 128, 16, 128])
                    Wd[p + "ik"] = din(p + "ik", [128, 16, 64])
                    Wd[p + "iw"] = din(p + "iw", [128, 16, 16])
            Wd[p + "mq"] = din(p + "mq", [8, 128, 16, 128])
            Wd[p + "gate"] = din(p + "gate", [24, 128, 16, 128])
            Wd[p + "mk"] = din(p + "mk", [8, 128, 16, 128])
            Wd[p + "mv"] = din(p + "mv", [2, 128, 16, 512])
            Wd[p + "out"] = din(p + "out", [4, 128, 24, 512])

        xres = dscr("xres", [S, D], F32)
        o_scr = dscr("o_scr", [24, 128, S], BF16)
        sg_scr = dscr("sg_scr", [24, 128, S], BF16)
        q_scr = dscr("q_scr", [16, 192, S], BF16)
        k_scr = dscr("k_scr", [16, 128, S], BF16)
        kr_scr = dscr("kr_scr", [64, S], BF16)
        v_scr = dscr("v_scr", [16, 128, S], BF16)
        iq_scr = dscr("iq_scr", [8, 128, S], BF16)
        ik_scr = dscr("ik_scr", [64, S], BF16)
        iw_scr = dscr("iw_scr", [S, 16], F32)
        mask_scr = dscr("mask_scr", [16, 128, S], BF16)
        mask_b = Buf()
        xres_b = [[Buf() for _ in range(4)] for _ in range(NT)]
        o_b = [Buf() for _ in range(24)]
        sg_b = [Buf() for _ in range(24)]
        q_b = [Buf() for _ in range(16)]
        k_b = [Buf() for _ in range(16)]
        kr_b = Buf()
        v_b = [Buf() for _ in range(16)]
        iq_b = [Buf() for _ in range(8)]
        ik_b = Buf()
        iw_b = Buf()

        ident, ident_b = K.sb(top, "ident", [128, 128], BF16)
        ones, ones_b = K.sb(top, "ones", [128, 128], BF16)
        cmask, cmask_b = K.sb(top, "cmask", [128, 3, 128], BF16)
        negtm, negtm_b = K.sb(top, "negtm", [128, 128], F32)
        rotT, rotT_b = K.sb(top, "rotT", [128, 128], F32)
        identC, identC_b = K.sb(top, "identC", [128, 128], BF16)
        negC8, negC8_b = K.sb(top, "negC8", [128, 1], F32)
        epsc, epsc_b = K.sb(top, "epsc", [128, 1], F32)
        gbc, gbc_b = K.sb(top, "gbc", [128, D], F32)
        memT, memT_b = K.sb(top, "memT", [128, 16, 256], BF16)
        mkT, mkT_b = K.sb(top, "mkT", [128, 8, 256], BF16)
        mv, mv_b = K.sb(top, "mv", [128, 2, 1024], BF16)
        wbs = []
        for i in range(2):
            t, b = K.sb(top, "wb%d" % i, [128, 12288], BF16)
            wbs.append((t, b))
        wring = Ring(wbs)
        f32ring = Ring([K.sb(top, "tf%d" % i, [128, 512], F32) for i in range(4)])
        bfring = Ring([K.sb(top, "tb%d" % i, [128, 512], BF16) for i in range(6)])
        colring = Ring([K.sb(top, "tc%d" % i, [128, 4], F32) for i in range(4)])
        banks = [(top.enter_context(nc.psum_tensor("ps%d" % i, [128, 512], F32)), Buf(excl=True)) for i in range(8)]
        psw = Ring(banks[0:4])
        psa = Ring(banks[4:8])

        def load_consts():
            with ExitStack() as st:
                stg, stg_b = K.sb(st, "cstg", [128, 3, 128], F32)
                sp.dma(stg[:, 0, :], ident_d, writes=[stg_b])
                dve.op(lambda e: e.tensor_copy(out=ident[:], in_=stg[:, 0, :]), reads=[stg_b], writes=[ident_b])
                dve.op(lambda e: e.tensor_scalar(out=identC[:], in0=stg[:, 0, :], scalar1=MASKC, scalar2=None, op0=ALU.mult),
                       reads=[stg_b], writes=[identC_b])
                dve.op(lambda e: e.memset(negC8[:], -MASKC * 0.125), writes=[negC8_b])
                sp.dma(stg[:], cmask_d, writes=[stg_b])
                dve.op(lambda e: e.tensor_copy(out=cmask[:], in_=stg[:]), reads=[stg_b], writes=[cmask_b])
                dve.op(lambda e: e.tensor_copy(out=negtm[:], in_=stg[:, 2, :]), reads=[stg_b], writes=[negtm_b])
                dve.op(lambda e: e.memset(ones[:], 1.0), writes=[ones_b])
                dve.op(lambda e: e.memset(epsc[:], EPS), writes=[epsc_b])
                dve.op(lambda e: e.memset(rotT[:], 0.0), writes=[rotT_b])
                sp.dma(rotT[0:64, 0:64], rotT_d, writes=[rotT_b])
                sp.dma(rotT[64:128, 64:128], rotT_d, writes=[rotT_b])
                K.barrier()

        def rstd_col(ssq_ap, ssq_buf, n):
            ct, cb = colring.next()
            act.op(lambda e: e.activation(out=ct[:, 1:2], in_=ssq_ap, func=AF.Ln, bias=epsc[:], scale=1.0 / n),
                   reads=[ssq_buf, epsc_b], writes=[cb])
            act.op(lambda e: e.activation(out=ct[:, 2:3], in_=ct[:, 1:2], func=AF.Exp, scale=-0.5),
                   reads=[cb], writes=[cb])
            return ct[:, 2:3], cb

        def load_gain(row):
            sp.dma(gbc[:], gains[row:row + 1, :].partition_broadcast(128), writes=[gbc_b])

        def norm_transpose(src_rows_fn, ntiles, dstT, dst_bufs, st):
            xts = Ring([K.sb(st, "xt%d" % i, [128, D], F32) for i in range(2)])
            hbs = Ring([K.sb(st, "hb%d" % i, [128, D], BF16) for i in range(2)])
            junk, junk_b = K.sb(st, "junk", [128, D], BF16)
            for t in range(ntiles):
                xt, xb = xts.next()
                src_ap, src_bufs = src_rows_fn(t)
                sp.dma(xt[:], src_ap, reads=src_bufs, writes=[xb])
                ct, cb = colring.next()
                act.op(lambda e: e.activation(out=junk[:], in_=xt[:], func=AF.Square, accum_out=ct[:, 0:1]),
                       reads=[xb], writes=[junk_b, cb])
                rs, rb = rstd_col(ct[:, 0:1], cb, D)
                hb, hbb = hbs.next()
                dve.op(lambda e: e.scalar_tensor_tensor(out=hb[:], in0=xt[:], scalar=rs, in1=gbc[:],
                                                        op0=ALU.mult, op1=ALU.mult),
                       reads=[xb, rb, gbc_b], writes=[hbb])
                for half in range(2):
                    pt, pb = psw.next()
                    pv = pt[:].bitcast(BF16)
                    for q in range(8):
                        kc = half * 8 + q
                        pe.op(lambda e: e.transpose(out=pv[:, q * 128:(q + 1) * 128], in_=hb[:, kc * 128:(kc + 1) * 128],
                                                    identity=ident[:]),
                              reads=[hbb, ident_b], writes=[pb], signal=(q == 7))
                    eng = act if half == 0 else dve
                    outv = dstT[:, half * 8:(half + 1) * 8, t * 128:(t + 1) * 128]
                    inv = pv[:, 0:1024].rearrange("p (k n) -> p k n", k=8)
                    if eng is act:
                        act.op(lambda e: e.copy(out=outv, in_=inv), reads=[pb], writes=[dst_bufs[t]])
                    else:
                        dve.op(lambda e: e.tensor_copy(out=outv, in_=inv), reads=[pb], writes=[dst_bufs[t]])

        if wlog is None:
            wlog = []
        wstate = {"i": 0, "issued": {}}

        def w_issue(desc):
            wt, wb = wring.next()
            kind = desc[0]
            if kind == "fm":
                _, name, c0, g, KC, ncol = desc
                src = Wd[name]
                n = g * KC * ncol
                assert n <= 12288
                if KC * ncol <= 2048:
                    pool.dma(wt[:, 0:n].rearrange("p (g x) -> p g x", g=g),
                             src[c0:c0 + g].rearrange("g p k n -> p g (k n)"), writes=[wb], max_dma_last_dim=8192)
                else:
                    m = KC * ncol
                    for gg in range(g):
                        pool.dma(wt[:, gg * m:(gg + 1) * m], src[c0 + gg].rearrange("p k n -> p (k n)"), writes=[wb],
                                 max_dma_last_dim=8192)
                return wt[:, 0:n].rearrange("p (g k n) -> p g k n", g=g, k=KC), wb
            _, name, idx, KC, ncol = desc
            src = Wd[name] if idx is None else Wd[name][idx]
            n = KC * ncol
            assert n <= 12288
            pool.dma(wt[:, 0:n], src.rearrange("p k n -> p (k n)"), writes=[wb], max_dma_last_dim=8192)
            return wt[:, 0:n].rearrange("p (k n) -> p k n", k=KC), wb

        def wload(desc):
            i = wstate["i"]
            wstate["i"] += 1
            wlog.append(desc)
            if wplan is None:
                return w_issue(desc)
            assert wplan[i] == desc, (i, wplan[i], desc)
            for k in (i, i + 1):
                if k < len(wplan) and k not in wstate["issued"]:
                    wstate["issued"][k] = w_issue(wplan[k])
            return wstate["issued"].pop(i)

        def load_w_fm(name, c0, g, KC, ncol=128):
            return wload(("fm", name, c0, g, KC, ncol))

        def load_w_tm(name, KC, ncol, idx=None):
            return wload(("tm", name, idx, KC, ncol))

        def linear_fm(src, nch, KC, rhs_fn, nblk, epi, G=4, M=128, ncol=128):
            c = 0
            pending = None
            groups = []
            while c < nch:
                g = min(G, nch - c)
                groups.append((c, g))
                c += g
            for gi, (c0, g) in enumerate(groups):
                wv, wb = load_w_fm(src, c0, g, KC, ncol)
                for cc in range(g):
                    for b in range(nblk):
                        pt, pb = psw.next()
                        for kc in range(KC):
                            rap, rbufs = rhs_fn(kc, b)
                            pe.op(lambda e: e.matmul(pt[0:M, :], lhsT=wv[:, cc, kc, 0:M], rhs=rap,
                                                     start=(kc == 0), stop=(kc == KC - 1)),
                                  reads=[wb] + rbufs, writes=[pb], signal=(kc == KC - 1))
                        epi(c0 + cc, b, pt, pb)

        def softmax_epilogue(Ot, Ob, Dt, Db, dst_ap, dst_bufs, bias_ap=None, bias_bufs=()):
            lt, lb = f32ring.next()
            if bias_ap is None:
                act.op(lambda e: e.activation(out=lt[:], in_=Dt[:], func=AF.Ln), reads=[Db], writes=[lb])
            else:
                act.op(lambda e: e.activation(out=lt[:], in_=Dt[:], func=AF.Ln, bias=bias_ap),
                       reads=[Db] + list(bias_bufs), writes=[lb])
            act.op(lambda e: e.activation(out=lt[:], in_=lt[:], func=AF.Exp, scale=-1.0), reads=[lb], writes=[lb])
            ot, ob = bfring.next()
            dve.op(lambda e: e.tensor_tensor(out=ot[:], in0=Ot[:], in1=lt[:], op=ALU.mult),
                   reads=[Ob, lb], writes=[ob])
            sp.dma(dst_ap, ot[:], reads=[ob], writes=list(dst_bufs))

        def mem_prep():
            with ExitStack() as st:
                load_gain(5)
                mb = [Buf() for _ in range(2)]
                norm_transpose(lambda t: (mem_in[t * 128:(t + 1) * 128, :], []), 2, memT, mb, st)
                K.barrier()

        def mem_kv(L):
            p = "L%d_" % L

            def epi(c, b, pt, pb):
                act.op(lambda e: e.copy(out=mkT[:, c, :], in_=pt[:, 0:256]), reads=[pb], writes=[mkT_b])
            c = 0
            for c0 in range(0, 8, 4):
                wv, wb = load_w_fm(p + "mk", c0, 4, 16)
                for cc in range(4):
                    pt, pb = psw.next()
                    for kc in range(16):
                        pe.op(lambda e: e.matmul(pt[:, 0:256], lhsT=wv[:, cc, kc, :], rhs=memT[:, kc, :],
                                                 start=(kc == 0), stop=(kc == 15)),
                              reads=[wb, memT_b], writes=[pb], signal=(kc == 15))
                    epi(c0 + cc, 0, pt, pb)
            for nb in range(2):
                wv, wb = load_w_tm(p + "mv", 16, 512, idx=nb)
                for mc in range(2):
                    pt, pb = psw.next()
                    for kc in range(16):
                        pe.op(lambda e: e.matmul(pt[:], lhsT=memT[:, kc, mc * 128:(mc + 1) * 128], rhs=wv[:, kc, :],
                                                 start=(kc == 0), stop=(kc == 15)),
                              reads=[wb, memT_b], writes=[pb], signal=(kc == 15))
                    act.op(lambda e: e.copy(out=mv[:, mc, nb * 512:(nb + 1) * 512], in_=pt[:]),
                           reads=[pb], writes=[mv_b])

        def mem_attention(mq, mq_b, tok0, tbb=TBB):
            for hm in range(4):
                for b in range(tbb):
                    ps_tiles = []
                    for mc in range(2):
                        pt, pb = psw.next()
                        for dc in range(2):
                            pe.op(lambda e: e.matmul(pt[:], lhsT=mkT[:, 2 * hm + dc, mc * 128:(mc + 1) * 128],
                                                     rhs=mq[:, 2 * hm + dc, b * 512:(b + 1) * 512],
                                                     start=(dc == 0), stop=(dc == 1)),
                                  reads=[mkT_b, mq_b], writes=[pb], signal=(dc == 1))
                        et, eb = bfring.next()
                        act.op(lambda e: e.activation(out=et[:], in_=pt[:], func=AF.Exp, scale=1.0 / 16.0),
                               reads=[pb], writes=[eb])
                        ps_tiles.append((et, eb))
                    Dt, Db = psa.next()
                    for mc in range(2):
                        et, eb = ps_tiles[mc]
                        pe.op(lambda e: e.matmul(Dt[:], lhsT=ones[:], rhs=et[:], start=(mc == 0), stop=(mc == 1)),
                              reads=[ones_b, eb], writes=[Db], signal=(mc == 1))
                    for dc in range(2):
                        Ot, Ob = psa.next()
                        for mc in range(2):
                            et, eb = ps_tiles[mc]
                            pe.op(lambda e: e.matmul(Ot[:], lhsT=mv[:, mc, hm * 256 + dc * 128: hm * 256 + (dc + 1) * 128],
                                                     rhs=et[:], start=(mc == 0), stop=(mc == 1)),
                                  reads=[mv_b, eb], writes=[Ob], signal=(mc == 1))
                        ch = 16 + 2 * hm + dc
                        if dc == 0:
                            lt, lb = f32ring.next()
                            act.op(lambda e: e.activation(out=lt[:], in_=Dt[:], func=AF.Ln), reads=[Db], writes=[lb])
                            act.op(lambda e: e.activation(out=lt[:], in_=lt[:], func=AF.Exp, scale=-1.0),
                                   reads=[lb], writes=[lb])
                        ot, ob = bfring.next()
                        dve.op(lambda e: e.tensor_tensor(out=ot[:], in0=Ot[:], in1=lt[:], op=ALU.mult),
                               reads=[Ob, lb], writes=[ob])
                        sp.dma(o_scr[ch, :, tok0 + b * 512: tok0 + (b + 1) * 512], ot[:], reads=[ob], writes=[o_b[ch]])

        def mq_gate(L, hT, hT_b, tok0, st, tb=TB):
            p = "L%d_" % L
            tbb = tb // 512
            mq, mq_b = K.sb(st, "mq", [128, 8, tb], BF16)

            def rhs_fn(kc, b):
                return hT[:, kc, b * 512:(b + 1) * 512], hT_b[4 * b:4 * b + 4]

            def epi_mq(c, b, pt, pb):
                act.op(lambda e: e.copy(out=mq[:, c, b * 512:(b + 1) * 512], in_=pt[:]), reads=[pb], writes=[mq_b])
            linear_fm(p + "mq", 8, 16, rhs_fn, tbb, epi_mq)
            chk("mq")
            mem_attention(mq, mq_b, tok0, tbb)
            chk("mematt")

            def epi_gate(c, b, pt, pb):
                ot, ob = bfring.next()
                act.op(lambda e: e.activation(out=ot[:], in_=pt[:], func=AF.Silu), reads=[pb], writes=[ob])
                sp.dma(sg_scr[c, :, tok0 + b * 512: tok0 + (b + 1) * 512], ot[:], reads=[ob], writes=[sg_b[c]])
            linear_fm(p + "gate", 24, 16, rhs_fn, tbb, epi_gate)

        def x_rows(L0):
            def f(t_abs):
                if L0:
                    return x_in[t_abs * 128:(t_abs + 1) * 128, :], []
                return xres[t_abs * 128:(t_abs + 1) * 128, :], xres_b[t_abs]
            return f

        def rope_fm(src32, src_b, cs, sn, cs_b, tcol0, dsts, P=64):
            pt, pb = psw.next()
            pe.op(lambda e: e.matmul(pt[0:P, :], lhsT=rotT[0:P, 0:P], rhs=src32, start=True, stop=True),
                  reads=[rotT_b, src_b], writes=[pb])
            t1, t1b = f32ring.next()
            dve.op(lambda e: e.tensor_tensor(out=t1[0:P, :], in0=src32, in1=cs[0:P, tcol0:tcol0 + 512], op=ALU.mult),
                   reads=[src_b, cs_b], writes=[t1b])
            t2, t2b = f32ring.next()
            dve.op(lambda e: e.tensor_tensor(out=t2[0:P, :], in0=pt[0:P, :], in1=sn[0:P, tcol0:tcol0 + 512], op=ALU.mult),
                   reads=[pb, cs_b], writes=[t2b])
            ot, ob = bfring.next()
            pool.op(lambda e: e.tensor_tensor(out=ot[0:P, :], in0=t1[0:P, :], in1=t2[0:P, :], op=ALU.add),
                    reads=[t1b, t2b], writes=[ob])
            for (dst_ap, dst_bufs, ps_) in dsts:
                sp.dma(dst_ap, ot[ps_, :], reads=[ob], writes=list(dst_bufs))

        def mla_pass(L, pi, first):
            p = "L%d_" % L
            tok0 = pi * TB
            with ExitStack() as so:
                cqg, cqg_b = K.sb(so, "cqg", [128, 12, TB], BF16)
                ckv, ckv_b = K.sb(so, "ckv", [128, 4, TB], BF16)
                rq, rq_b = K.sb(so, "rq", [128, TB], F32)
                rkv, rkv_b = K.sb(so, "rkv", [128, TB], F32)
                cs, cs_b = K.sb(so, "cs", [128, TB], F32)
                sn, sn_b = K.sb(so, "sn", [128, TB], F32)
                gq, gq_b = K.sb(so, "gq", [128, 12], F32)
                gkv, gkv_b = K.sb(so, "gkv", [128, 4], F32)
                qr32, qr32_b = K.sb(so, "qr32", [128, 512], F32)
                for hf in range(2):
                    sp.dma(cs[hf * 64:(hf + 1) * 64, :], cosT[:, tok0:tok0 + TB], writes=[cs_b])
                    sp.dma(sn[hf * 64:(hf + 1) * 64, :], sinT[:, tok0:tok0 + TB], writes=[cs_b])
                sp.dma(gq[:], Wd[p + "gq"], writes=[gq_b])
                sp.dma(gkv[:], Wd[p + "gkv"], writes=[gkv_b])
                with ExitStack() as sh:
                    hT, _ = K.sb(sh, "hT", [128, 16, TB], BF16)
                    hT_b = [Buf() for _ in range(TBT)]
                    with ExitStack() as sa:
                        xf = x_rows(first)
                        norm_transpose(lambda t: xf(pi * TBT + t), TBT, hT, hT_b, sa)
                    chk("A")
                    with ExitStack() as sb_:
                        kr32, kr32_b = K.sb(sb_, "kr32", [64, TB], F32)

                        def rhs_fn(kc, b):
                            return hT[:, kc, b * 512:(b + 1) * 512], hT_b[4 * b:4 * b + 4]

                        def make_epi(dst, dst_b, gcol, gcol_b, acc, nch):
                            def epi(c, b, pt, pb):
                                sq, sqb = bfring.next()
                                act.op(lambda e: e.activation(out=sq[:], in_=pt[:], func=AF.Square),
                                       reads=[pb], writes=[sqb])
                                dve.op(lambda e: e.tensor_scalar(out=dst[:, c, b * 512:(b + 1) * 512], in0=pt[:],
                                                                 scalar1=gcol[:, c:c + 1], scalar2=None, op0=ALU.mult),
                                       reads=[pb, gcol_b], writes=[dst_b])
                                at, ab = acc[b]
                                pe.op(lambda e: e.matmul(at[:], lhsT=ones[:], rhs=sq[:], start=(c == 0), stop=(c == nch - 1)),
                                      reads=[ones_b, sqb], writes=[ab])
                            return epi

                        def finish_rstd(acc, n, dst, dst_b):
                            for b in range(TBB):
                                at, ab = acc[b]
                                lt, lb = f32ring.next()
                                act.op(lambda e: e.activation(out=lt[:], in_=at[:], func=AF.Ln, bias=epsc[:], scale=1.0 / n),
                                       reads=[ab, epsc_b], writes=[lb])
                                act.op(lambda e: e.activation(out=dst[:, b * 512:(b + 1) * 512], in_=lt[:], func=AF.Exp, scale=-0.5),
                                       reads=[lb], writes=[dst_b])
                        accq = [psa.next() for _ in range(TBB)]
                        linear_fm(p + "cq", 12, 16, rhs_fn, TBB, make_epi(cqg, cqg_b, gq, gq_b, accq, 12))
                        chk("cq")
                        finish_rstd(accq, A_Q, rq, rq_b)
                        chk("cqr")
                        acck = [psa.next() for _ in range(TBB)]
                        linear_fm(p + "ckv", 4, 16, rhs_fn, TBB, make_epi(ckv, ckv_b, gkv, gkv_b, acck, 4))
                        finish_rstd(acck, A_KV, rkv, rkv_b)
                        for c in range(4):
                            dve.op(lambda e: e.tensor_tensor(out=ckv[:, c, :], in0=ckv[:, c, :], in1=rkv[:], op=ALU.mult),
                                   reads=[ckv_b, rkv_b], writes=[ckv_b])
                        chk("ckv")
                        wv, wb = load_w_tm(p + "kr", 16, 64)
                        for b in range(TBB):
                            pt, pb = psw.next()
                            for kc in range(16):
                                rap, rbufs = rhs_fn(kc, b)
                                pe.op(lambda e: e.matmul(pt[0:64, :], lhsT=wv[:, kc, :], rhs=rap, start=(kc == 0), stop=(kc == 15)),
                                      reads=[wb] + rbufs, writes=[pb], signal=(kc == 15))
                            dve.op(lambda e: e.tensor_copy(out=kr32[:, b * 512:(b + 1) * 512], in_=pt[0:64, :]),
                                   reads=[pb], writes=[kr32_b])
                            rope_fm(kr32[:, b * 512:(b + 1) * 512], kr32_b, cs, sn, cs_b, b * 512,
                                    [(kr_scr[:, tok0 + b * 512: tok0 + (b + 1) * 512], [kr_b], slice(0, 64))], P=64)
                        chk("kr")
                        mq_gate(L, hT, hT_b, tok0, sb_)
                chk("B1")
                with ExitStack() as s2:
                    for h0 in range(0, 16, 4):
                        wv, wb = load_w_fm(p + "uqn", h0, 4, 12)
                        for hh in range(4):
                            h = h0 + hh
                            for b in range(TBB):
                                pt, pb = psw.next()
                                for kc in range(12):
                                    pe.op(lambda e: e.matmul(pt[:], lhsT=wv[:, hh, kc, :], rhs=cqg[:, kc, b * 512:(b + 1) * 512],
                                                             start=(kc == 0), stop=(kc == 11)),
                                          reads=[wb, cqg_b], writes=[pb], signal=(kc == 11))
                                ot, ob = bfring.next()
                                dve.op(lambda e: e.tensor_tensor(out=ot[:], in0=pt[:], in1=rq[:, b * 512:(b + 1) * 512], op=ALU.mult),
                                       reads=[pb, rq_b], writes=[ob])
                                sp.dma(q_scr[h, 0:128, tok0 + b * 512: tok0 + (b + 1) * 512], ot[:], reads=[ob], writes=[q_b[h]])
                    for p0 in range(0, 8, 4):
                        wv, wb = load_w_fm(p + "uqr", p0, 4, 12)
                        for pp in range(4):
                            h = 2 * (p0 + pp)
                            for b in range(TBB):
                                pt, pb = psw.next()
                                for kc in range(12):
                                    pe.op(lambda e: e.matmul(pt[:], lhsT=wv[:, pp, kc, :], rhs=cqg[:, kc, b * 512:(b + 1) * 512],
                                                             start=(kc == 0), stop=(kc == 11)),
                                          reads=[wb, cqg_b], writes=[pb], signal=(kc == 11))
                                dve.op(lambda e: e.tensor_tensor(out=qr32[:], in0=pt[:], in1=rq[:, b * 512:(b + 1) * 512], op=ALU.mult),
                                       reads=[pb, rq_b], writes=[qr32_b])
                                tsl = slice(tok0 + b * 512, tok0 + (b + 1) * 512)
                                rope_fm(qr32[:], qr32_b, cs, sn, cs_b, b * 512,
                                        [(q_scr[h, 128:192, tsl], [q_b[h]], slice(0, 64)),
                                         (q_scr[h + 1, 128:192, tsl], [q_b[h + 1]], slice(64, 128))], P=128)
                    for h0 in range(0, 16, 8):
                        wv, wb = load_w_fm(p + "ukvk", h0, 8, 4)
                        for hh in range(8):
                            h = h0 + hh
                            for b in range(TBB):
                                pt, pb = psw.next()
                                for kc in range(4):
                                    pe.op(lambda e: e.matmul(pt[:], lhsT=wv[:, hh, kc, :], rhs=ckv[:, kc, b * 512:(b + 1) * 512],
                                                             start=(kc == 0), stop=(kc == 3)),
                                          reads=[wb, ckv_b], writes=[pb], signal=(kc == 3))
                                ot, ob = bfring.next()
                                act.op(lambda e: e.copy(out=ot[:], in_=pt[:]), reads=[pb], writes=[ob])
                                sp.dma(k_scr[h, :, tok0 + b * 512: tok0 + (b + 1) * 512], ot[:], reads=[ob], writes=[k_b[h]])
                    for hg in range(4):
                        wv, wb = load_w_fm(p + "ukvv", hg, 1, 4, ncol=512)
                        for t in range(TBT):
                            pt, pb = psw.next()
                            for kc in range(4):
                                pe.op(lambda e: e.matmul(pt[:], lhsT=ckv[:, kc, t * 128:(t + 1) * 128], rhs=wv[:, 0, kc, :],
                                                         start=(kc == 0), stop=(kc == 3)),
                                      reads=[wb, ckv_b], writes=[pb], signal=(kc == 3))
                            ot, ob = bfring.next()
                            act.op(lambda e: e.copy(out=ot[:], in_=pt[:]), reads=[pb], writes=[ob])
                            ta = pi * TBT + t
                            sp.dma(v_scr[hg * 4:(hg + 1) * 4, :, ta * 128:(ta + 1) * 128].rearrange("h p d -> p h d"),
                                   ot[:].rearrange("p (h d) -> p h d", h=4), reads=[ob], writes=v_b[hg * 4:(hg + 1) * 4])
            K.barrier()

        def mla_attention():
            scale = (128 + 64) ** -0.5
            with ExitStack() as st:
                krf, krf_b = K.sb(st, "krf", [64, S], BF16)
                sp.dma(krf[:], kr_scr, reads=[kr_b], writes=[krf_b])
                qn = Ring([K.sb(st, "qn%d" % i, [128, S], BF16) for i in range(2)])
                qr = Ring([K.sb(st, "qr%d" % i, [64, S], BF16) for i in range(2)])
                kn = Ring([K.sb(st, "kn%d" % i, [128, S], BF16) for i in range(2)])
                vv = Ring([K.sb(st, "vv%d" % i, [128, S], BF16) for i in range(2)])

                def load_head(h):
                    a = qn.next(); b_ = qr.next(); c = kn.next(); d = vv.next()
                    sp.dma(a[0][:], q_scr[h, 0:128, :], reads=[q_b[h]], writes=[a[1]])
                    sp.dma(b_[0][:], q_scr[h, 128:192, :], reads=[q_b[h]], writes=[b_[1]])
                    sp.dma(c[0][:], k_scr[h], reads=[k_b[h]], writes=[c[1]])
                    sp.dma(d[0][:], v_scr[h], reads=[v_b[h]], writes=[d[1]])
                    return a, b_, c, d
                nxt = load_head(0)
                for h in range(16):
                    (qnt, qnb), (qrt, qrb), (knt, knb), (vt, vb) = nxt
                    if h + 1 < 16:
                        nxt = load_head(h + 1)
                    for b in range(4):
                        Ot, Ob = psa.next()
                        Dt, Db = psa.next()
                        nj = 4 * (b + 1)

                        def emit_S(j):
                            jj = j - 4 * b
                            c0 = 128 * jj if jj > 0 else 0
                            St, Sb = psw.next()
                            js = slice(j * 128, (j + 1) * 128)
                            ts = slice(b * 512 + c0, (b + 1) * 512)
                            diag = jj >= 0
                            pe.op(lambda e: e.matmul(St[:, c0:512], lhsT=knt[:, js], rhs=qnt[:, ts], start=True, stop=False),
                                  reads=[knb, qnb], writes=[Sb], signal=False)
                            pe.op(lambda e: e.matmul(St[:, c0:512], lhsT=krf[:, js], rhs=qrt[:, ts], start=False, stop=not diag),
                                  reads=[krf_b, qrb], writes=[Sb], signal=not diag)
                            if diag:
                                pe.op(lambda e: e.matmul(St[:, c0:c0 + 128], lhsT=ident[:], rhs=cmask[:, 0, :], start=False, stop=True),
                                      reads=[ident_b, cmask_b], writes=[Sb])
                            return St, Sb, c0, js
                        pend = [emit_S(jx) for jx in range(min(2, nj))]
                        for j in range(nj):
                            St, Sb, c0, js = pend.pop(0)
                            if j + 2 < nj:
                                pend.append(emit_S(j + 2))
                            Pt, Pb = bfring.next()
                            act.op(lambda e: e.activation(out=Pt[:, c0:512], in_=St[:, c0:512], func=AF.Exp, scale=scale),
                                   reads=[Sb], writes=[Pb])
                            pe.op(lambda e: e.matmul(Ot[:, c0:512], lhsT=vt[:, js], rhs=Pt[:, c0:512], start=(j == 0), stop=(j == nj - 1)),
                                  reads=[vb, Pb], writes=[Ob], signal=(j == nj - 1))
                            pe.op(lambda e: e.matmul(Dt[:, c0:512], lhsT=ones[:], rhs=Pt[:, c0:512], start=(j == 0), stop=(j == nj - 1)),
                                  reads=[ones_b, Pb], writes=[Db], signal=True)
                        softmax_epilogue(Ot, Ob, Dt, Db, o_scr[h, :, b * 512:(b + 1) * 512], [o_b[h]])
            K.barrier()

        def gqa_pass(L, pi, first, dsa):
            p = "L%d_" % L
            TB_, TBT_, TBB_ = GTB, GTB // 128, GTB // 512
            tok0 = pi * TB_
            with ExitStack() as sh:
                hT, _ = K.sb(sh, "hT", [128, 16, TB_], BF16)
                hT_b = [Buf() for _ in range(TBT_)]
                with ExitStack() as sa:
                    xf = x_rows(first)
                    norm_transpose(lambda t: xf(pi * TBT_ + t), TBT_, hT, hT_b, sa)
                with ExitStack() as sb_:
                    def rhs_fn(kc, b):
                        return hT[:, kc, b * 512:(b + 1) * 512], hT_b[4 * b:4 * b + 4]

                    def epi_to(scr, bufs):
                        def epi(c, b, pt, pb):
                            ot, ob = bfring.next()
                            if (c + b) % 2 == 0:
                                act.op(lambda e: e.copy(out=ot[:], in_=pt[:]), reads=[pb], writes=[ob])
                            else:
                                dve.op(lambda e: e.tensor_copy(out=ot[:], in_=pt[:]), reads=[pb], writes=[ob])
                            sp.dma(scr[c, 0:128, tok0 + b * 512: tok0 + (b + 1) * 512], ot[:], reads=[ob], writes=[bufs[c]])
                        return epi
                    linear_fm(p + "q", 16, 16, rhs_fn, TBB_, epi_to(q_scr, q_b))
                    linear_fm(p + "k", 2, 16, rhs_fn, TBB_, epi_to(k_scr, k_b))
                    wv, wb = load_w_tm(p + "v", 16, 256)
                    vview = v_scr.rearrange("a p s -> (a p s)")[0:S * 256].rearrange("(t p d) -> t p d", p=128, d=256)
                    for t in range(TBT_):
                        pt, pb = psw.next()
                        for kc in range(16):
                            pe.op(lambda e: e.matmul(pt[:, 0:256], lhsT=hT[:, kc, t * 128:(t + 1) * 128], rhs=wv[:, kc, :],
                                                     start=(kc == 0), stop=(kc == 15)),
                                  reads=[wb, hT_b[t]], writes=[pb], signal=(kc == 15))
                        ot, ob = bfring.next()
                        act.op(lambda e: e.copy(out=ot[:, 0:256], in_=pt[:, 0:256]), reads=[pb], writes=[ob])
                        sp.dma(vview[pi * TBT_ + t], ot[:, 0:256], reads=[ob], writes=[v_b[0]])
                    if dsa:
                        linear_fm(p + "iq", 8, 16, rhs_fn, TBB_, epi_to(iq_scr, iq_b))
                        wv, wb = load_w_tm(p + "ik", 16, 64)
                        for b in range(TBB_):
                            pt, pb = psw.next()
                            for kc in range(16):
                                rap, rbufs = rhs_fn(kc, b)
                                pe.op(lambda e: e.matmul(pt[0:64, :], lhsT=wv[:, kc, :], rhs=rap, start=(kc == 0), stop=(kc == 15)),
                                      reads=[wb] + rbufs, writes=[pb], signal=(kc == 15))
                            ot, ob = bfring.next()
                            act.op(lambda e: e.copy(out=ot[0:64, :], in_=pt[0:64, :]), reads=[pb], writes=[ob])
                            sp.dma(ik_scr[:, tok0 + b * 512: tok0 + (b + 1) * 512], ot[0:64, :], reads=[ob], writes=[ik_b])
                        wv, wb = load_w_tm(p + "iw", 16, 16)
                        for t in range(TBT_):
                            pt, pb = psw.next()
                            for kc in range(16):
                                pe.op(lambda e: e.matmul(pt[:, 0:16], lhsT=hT[:, kc, t * 128:(t + 1) * 128], rhs=wv[:, kc, :],
                                                         start=(kc == 0), stop=(kc == 15)),
                                      reads=[wb, hT_b[t]], writes=[pb], signal=(kc == 15))
                            ft, fb = f32ring.next()
                            act.op(lambda e: e.copy(out=ft[:, 0:16], in_=pt[:, 0:16]), reads=[pb], writes=[fb])
                            ta = pi * TBT_ + t
                            sp.dma(iw_scr[ta * 128:(ta + 1) * 128, :], ft[:, 0:16], reads=[fb], writes=[iw_b])
                    mq_gate(L, hT, hT_b, tok0, sb_, tb=TB_)
            K.barrier()

        def dsa_indexer(st):
            iqr = Ring([K.sb(st, "iq%d" % i, [128, 8, 128], BF16) for i in range(2)])
            ik2, ik2_b = K.sb(st, "ik2", [128, S], BF16)
            accs = Ring([K.sb(st, "acc%d" % i, [128, S], F32) for i in range(4)])
            mtms = Ring([K.sb(st, "mtm%d" % i, [128, S], BF16) for i in range(4)])
            junks = [K.sb(st, "junkD%d" % i, [128, S], BF16) for i in range(2)]
            iwt = Ring([K.sb(st, "iwt%d" % i, [128, 16], F32) for i in range(2)])
            dgs = Ring([K.sb(st, "dg%d" % i, [128, 16, 128], F32) for i in range(2)])
            rls = Ring([K.sb(st, "rl%d" % i, [128, 512], F32) for i in range(6)])
            msts = Ring([K.sb(st, "mst%d" % i, [128, 1024], BF16) for i in range(2)])
            thrs = Ring([K.sb(st, "thr%d" % i, [128, 4], F32) for i in range(4)])
            id32, id32_b = K.sb(st, "id32", [128, 128], F32)
            sp.dma(id32[:], ident_d, writes=[id32_b])
            sp.dma(ik2[0:64, :], ik_scr, reads=[ik_b], writes=[ik2_b])
            sp.dma(ik2[64:128, :], ik_scr, reads=[ik_b], writes=[ik2_b])
            RNG = 512.0
            NIT = 28

            def accumulate(tt):
                Lk = (tt + 1) * 128
                acc, acc_b = accs.next()
                wt, wtb = iwt.next()
                sp.dma(wt[:], iw_scr[tt * 128:(tt + 1) * 128, :], reads=[iw_b], writes=[wtb])
                iq, iq_sb = iqr.next()
                sp.dma(iq[:], iq_scr[:, :, tt * 128:(tt + 1) * 128].rearrange("c p t -> p c t"), reads=iq_b, writes=[iq_sb])
                dg, dgb = dgs.next()
                for hi in range(16):
                    dve.op(lambda e: e.tensor_scalar(out=dg[:, hi, :], in0=id32[:], scalar1=wt[:, hi:hi + 1], scalar2=None, op0=ALU.mult),
                           reads=[id32_b, wtb], writes=[dgb], signal=(hi == 15))
                nsb = (Lk + 511) // 512
                for sbk in range(nsb):
                    c0 = sbk * 512
                    n = min(512, Lk - c0)
                    aP, aPb = psa.next()
                    for hp in range(8):
                        dA, dAb = psw.next()
                        dB, dBb = psw.next()
                        pe.op(lambda e: e.matmul(dA[:, 0:n], lhsT=iq[0:64, hp, :], rhs=ik2[0:64, c0:c0 + n],
                                                 start=True, stop=True), reads=[iq_sb, ik2_b], writes=[dAb])
                        pe.op(lambda e: e.matmul(dB[:, 0:n], lhsT=iq[64:128, hp, :], rhs=ik2[64:128, c0:c0 + n],
                                                 start=True, stop=True), reads=[iq_sb, ik2_b], writes=[dBb])
                        rts = []
                        for (dt_, db_) in ((dA, dAb), (dB, dBb)):
                            rt, rb = rls.next()
                            act.op(lambda e: e.activation(out=rt[:, 0:n], in_=dt_[:, 0:n], func=AF.Relu), reads=[db_], writes=[rb])
                            rts.append((rt, rb))
                        for e_, (rt, rb) in enumerate(rts):
                            hi = 2 * hp + e_
                            pe.op(lambda e: e.matmul(aP[:, 0:n], lhsT=dg[:, hi, :], rhs=rt[:, 0:n], start=(hi == 0), stop=(hi == 15)),
                                  reads=[dgb, rb], writes=[aPb], signal=True)
                    act.op(lambda e: e.copy(out=acc[:, c0:c0 + n], in_=aP[:, 0:n]), reads=[aPb], writes=[acc_b])
                pool.op(lambda e: e.tensor_tensor(out=acc[:, tt * 128:Lk], in0=acc[:, tt * 128:Lk], in1=negtm[:], op=ALU.add),
                        reads=[acc_b, negtm_b], writes=[acc_b])
                return acc, acc_b

            def bisect(group):
                outs = []
                chains = []
                for (tt, acc, acc_b) in group:
                    Lk = (tt + 1) * 128
                    mtm, mtm_b = mtms.next()
                    outs.append((tt, mtm, mtm_b))
                    if tt < 2:
                        dve.op(lambda e: e.tensor_scalar(out=mtm[:, 0:Lk], in0=acc[:, 0:Lk], scalar1=-1e29, scalar2=None,
                                                         op0=ALU.is_gt), reads=[acc_b], writes=[mtm_b])
                    else:
                        ct, cb = thrs.next()
                        dve.op(lambda e: e.memset(ct[:, 0:1], 0.0), writes=[cb])
                        junkD, junkD_b = junks[len(chains)]
                        chains.append((tt, acc, acc_b, mtm, mtm_b, ct, cb, Lk, junkD, junkD_b))
                step = RNG
                for k in range(NIT):
                    step = step * 0.5
                    for (tt, acc, acc_b, mtm, mtm_b, ct, cb, Lk, junkD, junkD_b) in chains:
                        dve.op(lambda e: e.tensor_scalar(out=junkD[:, 0:Lk], in0=acc[:, 0:Lk], scalar1=ct[:, 0:1], scalar2=None,
                                                         op0=ALU.is_ge, op1=ALU.add, accum_out=ct[:, 1:2]),
                               reads=[acc_b, cb], writes=[cb, junkD_b])
                    for (tt, acc, acc_b, mtm, mtm_b, ct, cb, Lk, junkD, junkD_b) in chains:
                        dve.op(lambda e: e.tensor_scalar(out=ct[:, 2:3], in0=ct[:, 1:2], scalar1=255.5, scalar2=2.0 * step,
                                                         op0=ALU.is_ge, op1=ALU.mult), reads=[cb], writes=[cb])
                    for (tt, acc, acc_b, mtm, mtm_b, ct, cb, Lk, junkD, junkD_b) in chains:
                        dve.op(lambda e: e.scalar_tensor_tensor(out=ct[:, 0:1], in0=ct[:, 2:3], scalar=-step, in1=ct[:, 0:1],
                                                                op0=ALU.add, op1=ALU.add), reads=[cb], writes=[cb])
                for (tt, acc, acc_b, mtm, mtm_b, ct, cb, Lk, junkD, junkD_b) in chains:
                    dve.op(lambda e: e.tensor_scalar(out=ct[:, 0:1], in0=ct[:, 0:1], scalar1=-step, scalar2=None, op0=ALU.add),
                           reads=[cb], writes=[cb])
                    dve.op(lambda e: e.tensor_scalar(out=mtm[:, 0:Lk], in0=acc[:, 0:Lk], scalar1=ct[:, 0:1], scalar2=None,
                                                     op0=ALU.is_ge), reads=[acc_b, cb], writes=[mtm_b])
                return outs

            def transposes(group):
                for (tt, mtm, mtm_b) in group:
                    for j0 in range(0, tt + 1, 8):
                        nb_ = min(8, tt + 1 - j0)
                        pt, pb = psw.next()
                        pv = pt[:].bitcast(BF16)
                        for q in range(nb_):
                            j = j0 + q
                            pe.op(lambda e: e.transpose(out=pv[:, q * 128:(q + 1) * 128], in_=mtm[:, j * 128:(j + 1) * 128], identity=ident[:]),
                                  reads=[mtm_b, ident_b], writes=[pb], signal=(q == nb_ - 1))
                        ms, msb = msts.next()
                        act.op(lambda e: e.copy(out=ms[:, 0:nb_ * 128], in_=pv[:, 0:nb_ * 128]), reads=[pb], writes=[msb])
                        sp.dma(mask_scr[j0:j0 + nb_, :, tt * 128:(tt + 1) * 128].rearrange("j p t -> p j t"),
                               ms[:, 0:nb_ * 128].rearrange("p (j t) -> p j t", j=nb_), reads=[msb], writes=[mask_b])
            groups = [[2 * i + 1, 2 * i] for i in range(7, -1, -1)]
            accd = {}
            bis = {}
            for i in range(len(groups) + 2):
                if i < len(groups):
                    accd[i] = [(tt,) + accumulate(tt) for tt in groups[i]]
                if 0 <= i - 1 < len(groups):
                    bis[i - 1] = bisect(accd.pop(i - 1))
                if 0 <= i - 2 < len(groups):
                    transposes(bis.pop(i - 2))

        def gqa_attention(L, dsa):
            with ExitStack() as st:
                if dsa:
                    with ExitStack() as si:
                        dsa_indexer(si)
                    K.barrier()
                    maskT, maskT_b = K.sb(st, "maskT", [128, 16, S], BF16)
                    for j in range(16):
                        sp.dma(maskT[:, j, j * 128:S], mask_scr[j, :, j * 128:S], reads=[mask_b], writes=[maskT_b])
                vall, vall_b = K.sb(st, "vall", [128, 16, 256], BF16)
                vview = v_scr.rearrange("a p s -> (a p s)")[0:S * 256].rearrange("(t p d) -> p t d", p=128, d=256)
                sp.dma(vall[:], vview, reads=[v_b[0]], writes=[vall_b])
                cbt, cbt_b = K.sb(st, "cbt", [128, 32], F32)
                sp.dma(cbt[:], cb_d.partition_broadcast(128), writes=[cbt_b])
                sk, sk_b = K.sb(st, "sk", [128, 16], F32)
                if not dsa:
                    sp.dma(sk[:], sink_d, writes=[sk_b])
                    act.op(lambda e: e.activation(out=sk[:], in_=sk[:], func=AF.Exp), reads=[sk_b], writes=[sk_b])
                psS = Ring(banks[0:6])
                psOD = Ring(banks[6:8])
                k2s = Ring([K.sb(st, "k2_%d" % i, [128, S], BF16) for i in range(2)])
                qcs = Ring([K.sb(st, "qc%d" % i, [128, S], BF16) for i in range(2)])
                wbts = Ring([K.sb(st, "wbt%d" % i, [128, 2, 640], F32) for i in range(2)])
                qv = q_scr

                def load_chunk(c):
                    a = qcs.next(); w_ = wbts.next()
                    sp.dma(a[0][:], qv[c, 0:128, :], reads=[q_b[c]], writes=[a[1]])
                    sp.dma(w_[0][:], wbias_d[c], writes=[w_[1]])
                    return a, w_

                def load_k(g):
                    kk = k2s.next()
                    src = k_scr[g // 2, (g % 2) * 64:(g % 2) * 64 + 64, :]
                    sp.dma(kk[0][0:64, :], src, reads=[k_b[g // 2]], writes=[kk[1]])
                    sp.dma(kk[0][64:128, :], src, reads=[k_b[g // 2]], writes=[kk[1]])
                    return kk
                nxt = load_chunk(0)
                kcur = None
                for c in range(16):
                    g = c // 4
                    if c % 4 == 0:
                        kcur = load_k(g)
                    k2, k2b = kcur
                    (qc, qcb), (wbt, wbtb) = nxt
                    if c + 1 < 16:
                        nxt = load_chunk(c + 1)
                    for b in range(4):
                        Ot, Ob = psOD.next()
                        Dt, Db = psOD.next()
                        if dsa:
                            jl = list(range(0, 4 * (b + 1)))
                        else:
                            jl = [j for j in range(4 * b - 1, 4 * b + 4) if j >= 0]
                        def emit_S(j):
                            jj = j - 4 * b
                            if dsa:
                                lo = max(jj, 0) * 128
                                hi_ = 512
                            else:
                                lo = max(128 * jj, 0)
                                hi_ = min(128 * jj + 256, 512)
                            n = hi_ - lo
                            woff = lo - 128 * jj if jj >= -1 else None
                            js = slice(j * 128, (j + 1) * 128)
                            ts = slice(b * 512 + lo, b * 512 + hi_)
                            SA, SAb = psS.next()
                            SB, SBb = psS.next()
                            only = dsa and jj < -1
                            pe.op(lambda e: e.matmul(SA[:, lo:hi_], lhsT=k2[0:64, js], rhs=qc[0:64, ts], start=True, stop=only),
                                  reads=[k2b, qcb], writes=[SAb], signal=False)
                            pe.op(lambda e: e.matmul(SB[:, lo:hi_], lhsT=k2[64:128, js], rhs=qc[64:128, ts], start=True, stop=only),
                                  reads=[k2b, qcb], writes=[SBb], signal=only)
                            if dsa:
                                mrhs = maskT[:, j, ts]
                                mb_ = [maskT_b]
                                mid, midb = identC, identC_b
                            else:
                                mrhs = cmask[:, 0:2, :].rearrange("p a n -> p (a n)")[:, woff:woff + n]
                                mb_ = [cmask_b]
                                mid, midb = ident, ident_b
                            if dsa and jj < -1:
                                return (SA, SAb), (SB, SBb), lo, hi_, n, woff, jj, mrhs
                            pe.op(lambda e: e.matmul(SA[:, lo:hi_], lhsT=mid[:], rhs=mrhs, start=False, stop=True),
                                  reads=[midb] + mb_, writes=[SAb])
                            pe.op(lambda e: e.matmul(SB[:, lo:hi_], lhsT=mid[:], rhs=mrhs, start=False, stop=True),
                                  reads=[midb] + mb_, writes=[SBb])
                            return (SA, SAb), (SB, SBb), lo, hi_, n, woff, jj, mrhs
                        pend = [emit_S(jx) for jx in jl[0:2]]
                        for ji, j in enumerate(jl):
                            (SA, SAb), (SB, SBb), lo, hi_, n, woff, jj, mrhs = pend.pop(0)
                            if ji + 2 < len(jl):
                                pend.append(emit_S(jl[ji + 2]))
                            special = jj >= -1
                            for e_, (St, Sb) in enumerate(((SA, SAb), (SB, SBb))):
                                h = 2 * c + e_
                                Pt, Pb = bfring.next()
                                if special:
                                    ft, fb = f32ring.next()
                                    dve.op(lambda e: e.scalar_tensor_tensor(out=ft[:, lo:hi_], in0=St[:, lo:hi_], scalar=0.125,
                                                                            in1=wbt[:, e_, woff:woff + n], op0=ALU.mult, op1=ALU.add),
                                           reads=[Sb, wbtb], writes=[fb])
                                    if dsa:
                                        act.op(lambda e: e.activation(out=Pt[:, lo:hi_], in_=ft[:, lo:hi_], func=AF.Exp, bias=negC8[:]),
                                               reads=[fb, negC8_b], writes=[Pb])
                                    else:
                                        act.op(lambda e: e.activation(out=Pt[:, lo:hi_], in_=ft[:, lo:hi_], func=AF.Exp),
                                               reads=[fb], writes=[Pb])
                                else:
                                    act.op(lambda e: e.activation(out=Pt[:, lo:hi_], in_=St[:, lo:hi_], func=AF.Exp, scale=0.125,
                                                                  bias=cbt[:, h:h + 1]),
                                           reads=[Sb, cbt_b], writes=[Pb])
                                    dve.op(lambda e: e.tensor_tensor(out=Pt[:, lo:hi_], in0=Pt[:, lo:hi_], in1=mrhs, op=ALU.mult),
                                           reads=[Pb, maskT_b], writes=[Pb])
                                first = (j == jl[0])
                                last = (j == jl[-1])
                                pe.op(lambda e: e.matmul(Ot[e_ * 64:(e_ + 1) * 64, lo:hi_], lhsT=vall[:, j, g * 64:(g + 1) * 64],
                                                         rhs=Pt[:, lo:hi_], start=first, stop=last, skip_group_check=True),
                                      reads=[vall_b, Pb], writes=[Ob], signal=(last and e_ == 1))
                                pe.op(lambda e: e.matmul(Dt[e_ * 64:(e_ + 1) * 64, lo:hi_], lhsT=ones[:, 0:64],
                                                         rhs=Pt[:, lo:hi_], start=first, stop=last, skip_group_check=True),
                                      reads=[ones_b, Pb], writes=[Db], signal=True)
                        if dsa:
                            softmax_epilogue(Ot, Ob, Dt, Db, o_scr[c, :, b * 512:(b + 1) * 512], [o_b[c]])
                        else:
                            softmax_epilogue(Ot, Ob, Dt, Db, o_scr[c, :, b * 512:(b + 1) * 512], [o_b[c]],
                                             bias_ap=sk[:, c:c + 1], bias_bufs=[sk_b])
            K.barrier()

        def out_proj(L, first):
            p = "L%d_" % L
            with ExitStack() as st:
                yT, _ = K.sb(st, "yT", [128, 24, S], BF16)
                ybufs = [[Buf() for _ in range(24)] for _ in range(4)]
                ost = Ring([K.sb(st, "ost%d" % i, [128, 512], BF16) for i in range(3)])
                sgt = Ring([K.sb(st, "sgt%d" % i, [128, 512], BF16) for i in range(3)])
                xin = Ring([K.sb(st, "xin%d" % i, [128, 512], F32) for i in range(3)])
                xo = Ring([K.sb(st, "xo%d" % i, [128, 512], F32) for i in range(3)])
                cnt = [0]

                def build_chunk(q, c):
                    a = ost.next(); s_ = sgt.next()
                    sp.dma(a[0][:], o_scr[c, :, q * 512:(q + 1) * 512], reads=[o_b[c]], writes=[a[1]])
                    sp.dma(s_[0][:], sg_scr[c, :, q * 512:(q + 1) * 512], reads=[sg_b[c]], writes=[s_[1]])
                    cnt[0] += 1
                    eng = dve if cnt[0] % 4 else pool
                    eng.op(lambda e: e.tensor_tensor(out=yT[:, c, q * 512:(q + 1) * 512], in0=a[0][:], in1=s_[0][:], op=ALU.mult),
                           reads=[a[1], s_[1]], writes=[ybufs[q][c]])
                for c in range(24):
                    build_chunk(0, c)
                pending = [(q, c) for q in range(1, 4) for c in range(24)]
                for nb in range(4):
                    wv, wb = load_w_tm(p + "out", 24, 512, idx=nb)
                    for ta in range(NT):
                        for _ in range(6):
                            if pending:
                                build_chunk(*pending.pop(0))
                        xi, xib = xin.next()
                        if first:
                            sp.dma(xi[:], x_in[ta * 128:(ta + 1) * 128, nb * 512:(nb + 1) * 512], writes=[xib])
                        else:
                            sp.dma(xi[:], xres[ta * 128:(ta + 1) * 128, nb * 512:(nb + 1) * 512],
                                   reads=[xres_b[ta][nb]], writes=[xib])
                        pt, pb = psw.next()
                        for c in range(24):
                            pe.op(lambda e: e.matmul(pt[:], lhsT=yT[:, c, ta * 128:(ta + 1) * 128], rhs=wv[:, c, :],
                                                     start=(c == 0), stop=(c == 23)),
                                  reads=[wb, ybufs[ta // 4][c]], writes=[pb], signal=(c == 23))
                        xot, xob = xo.next()
                        dve.op(lambda e: e.tensor_tensor(out=xot[:], in0=pt[:], in1=xi[:], op=ALU.add),
                               reads=[pb, xib], writes=[xob])
                        sp.dma(xres[ta * 128:(ta + 1) * 128, nb * 512:(nb + 1) * 512], xot[:], reads=[xob],
                               writes=[xres_b[ta][nb]])
            K.barrier()

        def final_pass():
            with ExitStack() as st:
                xts = Ring([K.sb(st, "fx%d" % i, [128, D], F32) for i in range(4)])
                ots = Ring([K.sb(st, "fo%d" % i, [128, D], F32) for i in range(4)])
                junk, junk_b = K.sb(st, "fjunk", [128, D], BF16)
                if final_norm:
                    load_gain(4)
                fins = []
                for t in range(NT):
                    xt, xb = xts.next()
                    sp.dma(xt[:], xres[t * 128:(t + 1) * 128, :], reads=xres_b[t], writes=[xb])
                    if final_norm:
                        ct, cb = colring.next()
                        act.op(lambda e: e.activation(out=junk[:], in_=xt[:], func=AF.Square, accum_out=ct[:, 0:1]),
                               reads=[xb], writes=[junk_b, cb])
                        rs, rb = rstd_col(ct[:, 0:1], cb, D)
                        ot, ob = ots.next()
                        dve.op(lambda e: e.scalar_tensor_tensor(out=ot[:], in0=xt[:], scalar=rs, in1=gbc[:],
                                                                op0=ALU.mult, op1=ALU.mult),
                               reads=[xb, rb, gbc_b], writes=[ob])
                        fins.append(sp.dma(out_d[t * 128:(t + 1) * 128, :], ot[:], reads=[ob]))
                    else:
                        fins.append(sp.dma(out_d[t * 128:(t + 1) * 128, :], xt[:], reads=[xb]))
                for tk in fins:
                    sp.wait(tk)

        NEGM_col, NEGM_col_b = K.sb(top, "negmcol", [128, 1], F32)
        try:
            load_consts()
            dve.op(lambda e: e.memset(NEGM_col[:], NEGM), writes=[NEGM_col_b])
            chk("consts")
            mem_prep()
            chk("memprep")
            first = True
            for L in layers:
                kind = L % 3
                mem_kv(L)
                load_gain(L)
                chk("memkv")
                for pi in range(NPASS if kind == 0 else S // GTB):
                    if kind == 0:
                        mla_pass(L, pi, first)
                    else:
                        gqa_pass(L, pi, first, dsa=(kind == 1))
                    chk("pass%d" % pi)
                if kind == 0:
                    mla_attention()
                else:
                    gqa_attention(L, dsa=(kind == 1))
                chk("attn")
                out_proj(L, first)
                first = False
            final_pass()
        except _Stop:
            pass
        K.barrier()
    nc._wlog = wlog
    return nc


def host_prepare(inputs, layers):
    f = np.float32
    shared = {}
    gains = np.concatenate([inputs["norm_in"], inputs["final_norm"][None], inputs["mem_norm"][None]], 0).astype(f)
    shared["gains"] = np.ascontiguousarray(gains)
    inv = (1.0 / (10000.0 ** (np.arange(0, 64, 2, dtype=np.float32) / np.float32(64)))).astype(np.float32)
    ang = np.arange(S, dtype=np.float32)[:, None] * inv[None, :]
    cos = np.cos(ang).astype(f).T
    sin = np.sin(ang).astype(f).T
    shared["cosT"] = np.ascontiguousarray(np.concatenate([cos, cos], 0))
    shared["sinT"] = np.ascontiguousarray(np.concatenate([sin, sin], 0))
    R = np.zeros((64, 64), f)
    for d in range(32):
        R[d, d + 32] = -1.0
        R[d + 32, d] = 1.0
    shared["rotT"] = np.ascontiguousarray(R.T)
    sl = np.arange(128)[:, None]
    tl = np.arange(128)[None, :]
    cm = np.zeros((128, 3, 128), f)
    cm[:, 0, :] = np.where(sl <= tl, 0.0, NEGM)
    cm[:, 1, :] = np.where(tl < sl, 0.0, NEGM)
    cm[:, 2, :] = np.where(tl <= sl, 0.0, -1e30)
    shared["cmask"] = cm
    shared["ident"] = np.eye(128, dtype=f)
    rb = inputs["rel_bias"].astype(f)
    shared["cbias"] = np.ascontiguousarray(rb[31:32, :])
    delta_diag = tl - sl
    delta_off = tl - sl + 128
    bd = t5_bucket_np(delta_diag)
    bo = t5_bucket_np(delta_off)
    wb = np.zeros((32, 128, 640), f)
    for h in range(32):
        wb[h, :, 0:128] = rb[bd, h]
        wb[h, :, 128:256] = rb[bo, h]
        wb[h, :, 256:640] = rb[31, h]
    shared["wbias"] = np.ascontiguousarray(wb.reshape(16, 2, 128, 640).transpose(0, 2, 1, 3))
    sinks = inputs["c_sinks"][0].astype(f)
    spc = np.zeros((128, 16), f)
    for c in range(16):
        spc[0:64, c] = sinks[2 * c]
        spc[64:128, c] = sinks[2 * c + 1]
    shared["sinkpc"] = spc
    for L in layers:
        kind, j = L % 3, L // 3
        p = "L%d_" % L
        if kind == 0:
            W = inputs["w_in_a"][j]
            o = 0
            shared[p + "cq"] = chunk_fm(W[:, o:o + A_Q]); o += A_Q
            shared[p + "ckv"] = chunk_fm(W[:, o:o + A_KV]); o += A_KV
            shared[p + "kr"] = chunk_tm(W[:, o:o + A_ROPE]); o += A_ROPE
            shared[p + "mq"] = chunk_fm(W[:, o:o + MEMW]); o += MEMW
            shared[p + "gate"] = chunk_fm(W[:, o:o + BRW]); o += BRW
            shared[p + "gq"] = col_pc(inputs["a_q_norm"][j])
            shared[p + "gkv"] = col_pc(inputs["a_kv_norm"][j])
            wuq = inputs["w_uq"][j].reshape(12, 128, 16, 192)
            wq_h = wuq.transpose(2, 1, 0, 3)
            shared[p + "uqn"] = np.ascontiguousarray(wq_h[:, :, :, 0:128])
            wr = wq_h[:, :, :, 128:192].reshape(8, 2, 128, 12, 64)
            shared[p + "uqr"] = np.ascontiguousarray(wr.transpose(0, 2, 3, 1, 4).reshape(8, 128, 12, 128))
            wukv = inputs["w_ukv"][j].reshape(4, 128, 16, 256)
            shared[p + "ukvk"] = np.ascontiguousarray(wukv[:, :, :, 0:128].transpose(2, 1, 0, 3))
            wv = wukv[:, :, :, 128:256].reshape(4, 128, 4, 512)
            shared[p + "ukvv"] = np.ascontiguousarray(wv.transpose(2, 1, 0, 3))
        else:
            W = inputs["w_in_b"][j] if kind == 1 else inputs["w_in_c"][j]
            o = 0
            shared[p + "q"] = chunk_fm(W[:, o:o + 2048]); o += 2048
            shared[p + "k"] = chunk_fm(W[:, o:o + 256]); o += 256
            shared[p + "v"] = chunk_tm(W[:, o:o + 256]); o += 256
            if kind == 1:
                shared[p + "iq"] = chunk_fm(W[:, o:o + 1024]); o += 1024
                shared[p + "ik"] = chunk_tm(W[:, o:o + 64]); o += 64
                shared[p + "iw"] = chunk_tm(W[:, o:o + 16]); o += 16
            shared[p + "mq"] = chunk_fm(W[:, o:o + MEMW]); o += MEMW
            shared[p + "gate"] = chunk_fm(W[:, o:o + BRW]); o += BRW
            assert o == W.shape[1]
        Wm = inputs["w_mem_kv"][L]
        shared[p + "mk"] = chunk_fm(Wm[:, 0:1024])
        mvw = chunk_tm(Wm[:, 1024:2048])
        shared[p + "mv"] = np.ascontiguousarray(mvw.reshape(128, 16, 2, 512).transpose(2, 0, 1, 3))
        Wo = chunk_tm(inputs["w_out"][L])
        shared[p + "out"] = np.ascontiguousarray(Wo.reshape(128, 24, 4, 512).transpose(2, 0, 1, 3))
    return shared


_PROG_CACHE = {}


def build_full(layers, final_norm=True, dbg=(), stop_after=None):
    nc0 = build_program(list(layers), final_norm=final_norm, dbg=dbg, stop_after=stop_after)
    return build_program(list(layers), final_norm=final_norm, dbg=dbg, stop_after=stop_after, wplan=list(nc0._wlog))


def run_layers(inputs, layers, final_norm=True, x_override=None, dbg=(), stop_after=None, ncores=8):
    key = (tuple(layers), final_norm, tuple(dbg), stop_after)
    if key not in _PROG_CACHE:
        _PROG_CACHE[key] = build_full(layers, final_norm=final_norm, dbg=dbg, stop_after=stop_after)
    nc = _PROG_CACHE[key]
    shared = host_prepare(inputs, layers)
    x = inputs["x"] if x_override is None else x_override
    in_maps = []
    for b in range(ncores):
        m = dict(shared)
        m["x"] = np.ascontiguousarray(x[b], dtype=np.float32)
        m["mem"] = np.ascontiguousarray(inputs["mem"][b], dtype=np.float32)
        in_maps.append(m)
    res = run_bass_kernel_spmd(nc, in_maps, core_ids=list(range(ncores)))
    return res


def kernel(**inputs):
    inputs = {k: np.asarray(v) for k, v in inputs.items()}
    res = run_layers(inputs, [0, 1, 2, 3], final_norm=True)
    return np.stack([np.asarray(r["out"], dtype=np.float32) for r in res.results], 0)
```

```python
import math
import numpy as np
import concourse.bass as bass
import concourse.mybir as mybir
from concourse.bass_utils import run_bass_kernel_spmd

F32 = mybir.dt.float32
BF16 = mybir.dt.bfloat16
AF = mybir.ActivationFunctionType
ALU = mybir.AluOpType

S = 2048
D = 2048
NT = 16
EPS = 1e-6
TB = 1024
GTB = 2048
NPASS = S // TB
TBT = TB // 128
TBB = TB // 512
NEGM = -30000.0
MASKC = 2992.0

A_Q, A_KV, A_ROPE = 1536, 512, 64
MEMW = 1024
BRW = 3072


def chunk_fm(W):
    Kd, N = W.shape
    return np.ascontiguousarray(W.reshape(Kd // 128, 128, N // 128, 128).transpose(2, 1, 0, 3))


def chunk_tm(W):
    Kd, N = W.shape
    return np.ascontiguousarray(W.reshape(Kd // 128, 128, N).transpose(1, 0, 2))


def col_pc(g):
    return np.ascontiguousarray(g.reshape(-1, 128).T)


def t5_bucket_np(n):
    n = np.maximum(n, 0)
    nf = np.maximum(n, 1).astype(np.float32)
    large = 16 + (np.log(nf / np.float32(16)) / np.float32(math.log(128 / 16)) * np.float32(16)).astype(np.int32)
    large = np.minimum(large, 31)
    return np.where(n < 16, n, large)


class Buf:
    __slots__ = ("name", "w", "r", "excl")

    def __init__(self, name="", excl=False):
        self.name = name
        self.w = None
        self.r = {}
        self.excl = excl


class Eng:
    def __init__(self, K, eng, name, nring=0, selfwait=True):
        self.K = K
        self.e = eng
        self.name = name
        self.selfwait = selfwait
        self.sem = K.new_sem("s_" + name)
        self.cnt = 0
        self.waited = {}
        self.rsem = [K.new_sem("d_%s%d" % (name, i)) for i in range(nring)]
        self.rcnt = [0] * nring
        self.ri = 0
        self.pr = []
        self.pw = []

    def wait(self, tok):
        if tok is None or self.K.stopped:
            return
        sem, val = tok
        if (not self.selfwait) and sem is self.sem:
            return
        k = id(sem)
        if self.waited.get(k, 0) >= val:
            return
        self.e.wait_ge(sem, val)
        self.waited[k] = val

    def deps(self, reads, writes):
        for b in reads:
            self.wait(b.w)
            if b.excl:
                for k, t in b.r.items():
                    if k != self.name:
                        self.wait(t)
        for b in writes:
            self.wait(b.w)
            for t in b.r.values():
                self.wait(t)

    def _commit(self, tok, key, reads, writes):
        for b in writes:
            b.w = tok
            b.r = {}
        for b in reads:
            b.r[key] = tok

    def op(self, fn, reads=(), writes=(), signal=True):
        if self.K.stopped:
            return None
        self.deps(reads, writes)
        inst = fn(self.e)
        if signal:
            self.cnt += 1
            inst.then_inc(self.sem, 1)
            tok = (self.sem, self.cnt)
            self._commit(tok, self.name, list(reads) + self.pr, list(writes) + self.pw)
            self.pr = []
            self.pw = []
            return tok
        self.pr += list(reads)
        self.pw += list(writes)
        return None

    def dma(self, out, in_, reads=(), writes=(), **kw):
        if self.K.stopped:
            return None
        self.deps(reads, writes)
        k = self.ri % len(self.rsem)
        self.ri += 1
        if self.rcnt[k] > 0:
            self.wait((self.rsem[k], 16 * self.rcnt[k]))
        inst = self.e.dma_start(out=out, in_=in_, **kw)
        self.rcnt[k] += 1
        inst.then_inc(self.rsem[k], 16)
        tok = (self.rsem[k], 16 * self.rcnt[k])
        self._commit(tok, "%s_d%d" % (self.name, k), reads, writes)
        return tok

    def all_tokens(self):
        toks = []
        if self.cnt:
            toks.append((self.sem, self.cnt))
        for s, c in zip(self.rsem, self.rcnt):
            if c:
                toks.append((s, 16 * c))
        return toks


class Ring:
    def __init__(self, items):
        self.items = items
        self.i = 0

    def next(self):
        it = self.items[self.i % len(self.items)]
        self.i += 1
        return it


class Kern:
    def __init__(self, nc, stack):
        self.nc = nc
        self.stack = stack
        self.nsem = 0
        self.stopped = False
        self.pe = Eng(self, nc.tensor, "pe", selfwait=False)
        self.act = Eng(self, nc.scalar, "act")
        self.dve = Eng(self, nc.vector, "dve")
        self.pool = Eng(self, nc.gpsimd, "pool", nring=8)
        self.sp = Eng(self, nc.sync, "sp", nring=12)
        self.engs = [self.pe, self.act, self.dve, self.pool, self.sp]

    def new_sem(self, name):
        self.nsem += 1
        return self.stack.enter_context(self.nc.semaphore(name))

    def barrier(self):
        toks = []
        for e in self.engs:
            toks += e.all_tokens()
        for e in self.engs:
            for t in toks:
                if t[0] is e.sem and not e.selfwait:
                    continue
                e.wait(t)

    def sb(self, stack, name, shape, dt, nbuf=None):
        self.nsb = getattr(self, "nsb", 0) + 1
        name = "sb%d_%s" % (self.nsb, name)
        t = stack.enter_context(self.nc.sbuf_tensor(name, shape, dt))
        if nbuf is None:
            return t, Buf(name)
        return t, [Buf("%s%d" % (name, i)) for i in range(nbuf)]


def build_program(layers, final_norm=True, dbg=(), stop_after=None, wplan=None, wlog=None):
    from contextlib import ExitStack

    nc = bass.Bass("TRN2", target_bir_lowering=False)
    top = ExitStack()
    with top:
        K = Kern(nc, top)
        pe, act, dve, pool, sp = K.pe, K.act, K.dve, K.pool, K.sp

        class _Stop(Exception):
            pass

        def chk(name):
            if stop_after == name:
                K.barrier()
                K.stopped = True

        def din(name, shape, dt=F32):
            return nc.dram_tensor(name, list(shape), dt, kind="ExternalInput").ap()

        def dscr(name, shape, dt):
            kind = "ExternalOutput" if name in dbg else "Internal"
            return nc.dram_tensor(name, list(shape), dt, kind=kind).ap()

        x_in = din("x", [S, D])
        mem_in = din("mem", [256, D])
        out_d = nc.dram_tensor("out", [S, D], F32, kind="ExternalOutput").ap()
        gains = din("gains", [6, D])
        cosT = din("cosT", [64, S])
        sinT = din("sinT", [64, S])
        rotT_d = din("rotT", [64, 64])
        cmask_d = din("cmask", [128, 3, 128])
        ident_d = din("ident", [128, 128])
        cb_d = din("cbias", [1, 32])
        sink_d = din("sinkpc", [128, 16])
        wbias_d = din("wbias", [16, 128, 2, 640])

        Wd = {}
        for L in layers:
            kind, j = L % 3, L // 3
            p = "L%d_" % L
            if kind == 0:
                Wd[p + "cq"] = din(p + "cq", [12, 128, 16, 128])
                Wd[p + "ckv"] = din(p + "ckv", [4, 128, 16, 128])
                Wd[p + "kr"] = din(p + "kr", [128, 16, 64])
                Wd[p + "gq"] = din(p + "gq", [128, 12])
                Wd[p + "gkv"] = din(p + "gkv", [128, 4])
                Wd[p + "uqn"] = din(p + "uqn", [16, 128, 12, 128])
                Wd[p + "uqr"] = din(p + "uqr", [8, 128, 12, 128])
                Wd[p + "ukvk"] = din(p + "ukvk", [16, 128, 4, 128])
                Wd[p + "ukvv"] = din(p + "ukvv", [4, 128, 4, 512])
            else:
                Wd[p + "q"] = din(p + "q", [16, 128, 16, 128])
                Wd[p + "k"] = din(p + "k", [2, 128, 16, 128])
                Wd[p + "v"] = din(p + "v", [128, 16, 256])
                if kind == 1:
                    Wd[p + "iq"] = din(p + "iq", [8, 128, 16, 128])
                    Wd[p + "ik"] = din(p + "ik", [128, 16, 64])
                    Wd[p + "iw"] = din(p + "iw", [128, 16, 16])
            Wd[p + "mq"] = din(p + "mq", [8, 128, 16, 128])
            Wd[p + "gate"] = din(p + "gate", [24, 128, 16, 128])
            Wd[p + "mk"] = din(p + "mk", [8, 128, 16, 128])
            Wd[p + "mv"] = din(p + "mv", [2, 128, 16, 512])
            Wd[p + "out"] = din(p + "out", [4, 128, 24, 512])

        xres = dscr("xres", [S, D], F32)
        o_scr = dscr("o_scr", [24, 128, S], BF16)
        sg_scr = dscr("sg_scr", [24, 128, S], BF16)
        q_scr = dscr("q_scr", [16, 192, S], BF16)
        k_scr = dscr("k_scr", [16, 128, S], BF16)
        kr_scr = dscr("kr_scr", [64, S], BF16)
        v_scr = dscr("v_scr", [16, 128, S], BF16)
        iq_scr = dscr("iq_scr", [8, 128, S], BF16)
        ik_scr = dscr("ik_scr", [64, S], BF16)
        iw_scr = dscr("iw_scr", [S, 16], F32)
        mask_scr = dscr("mask_scr", [16, 128, S], BF16)
        mask_b = Buf()
        xres_b = [[Buf() for _ in range(4)] for _ in range(NT)]
        o_b = [Buf() for _ in range(24)]
        sg_b = [Buf() for _ in range(24)]
        q_b = [Buf() for _ in range(16)]
        k_b = [Buf() for _ in range(16)]
        kr_b = Buf()
        v_b = [Buf() for _ in range(16)]
        iq_b = [Buf() for _ in range(8)]
        ik_b = Buf()
        iw_b = Buf()

        ident, ident_b = K.sb(top, "ident", [128, 128], BF16)
        ones, ones_b = K.sb(top, "ones", [128, 128], BF16)
        cmask, cmask_b = K.sb(top, "cmask", [128, 3, 128], BF16)
        negtm, negtm_b = K.sb(top, "negtm", [128, 128], F32)
        rotT, rotT_b = K.sb(top, "rotT", [128, 128], F32)
        identC, identC_b = K.sb(top, "identC", [128, 128], BF16)
        negC8, negC8_b = K.sb(top, "negC8", [128, 1], F32)
        epsc, epsc_b = K.sb(top, "epsc", [128, 1], F32)
        gbc, gbc_b = K.sb(top, "gbc", [128, D], F32)
        memT, memT_b = K.sb(top, "memT", [128, 16, 256], BF16)
        mkT, mkT_b = K.sb(top, "mkT", [128, 8, 256], BF16)
        mv, mv_b = K.sb(top, "mv", [128, 2, 1024], BF16)
        wbs = []
        for i in range(2):
            t, b = K.sb(top, "wb%d" % i, [128, 12288], BF16)
            wbs.append((t, b))
        wring = Ring(wbs)
        f32ring = Ring([K.sb(top, "tf%d" % i, [128, 512], F32) for i in range(4)])
        bfring = Ring([K.sb(top, "tb%d" % i, [128, 512], BF16) for i in range(6)])
        colring = Ring([K.sb(top, "tc%d" % i, [128, 4], F32) for i in range(4)])
        banks = [(top.enter_context(nc.psum_tensor("ps%d" % i, [128, 512], F32)), Buf(excl=True)) for i in range(8)]
        psw = Ring(banks[0:4])
        psa = Ring(banks[4:8])

        def load_consts():
            with ExitStack() as st:
                stg, stg_b = K.sb(st, "cstg", [128, 3, 128], F32)
                sp.dma(stg[:, 0, :], ident_d, writes=[stg_b])
                dve.op(lambda e: e.tensor_copy(out=ident[:], in_=stg[:, 0, :]), reads=[stg_b], writes=[ident_b])
                dve.op(lambda e: e.tensor_scalar(out=identC[:], in0=stg[:, 0, :], scalar1=MASKC, scalar2=None, op0=ALU.mult),
                       reads=[stg_b], writes=[identC_b])
                dve.op(lambda e: e.memset(negC8[:], -MASKC * 0.125), writes=[negC8_b])
                sp.dma(stg[:], cmask_d, writes=[stg_b])
                dve.op(lambda e: e.tensor_copy(out=cmask[:], in_=stg[:]), reads=[stg_b], writes=[cmask_b])
                dve.op(lambda e: e.tensor_copy(out=negtm[:], in_=stg[:, 2, :]), reads=[stg_b], writes=[negtm_b])
                dve.op(lambda e: e.memset(ones[:], 1.0), writes=[ones_b])
                dve.op(lambda e: e.memset(epsc[:], EPS), writes=[epsc_b])
                dve.op(lambda e: e.memset(rotT[:], 0.0), writes=[rotT_b])
                sp.dma(rotT[0:64, 0:64], rotT_d, writes=[rotT_b])
                sp.dma(rotT[64:128, 64:128], rotT_d, writes=[rotT_b])
                K.barrier()

        def rstd_col(ssq_ap, ssq_buf, n):
            ct, cb = colring.next()
            act.op(lambda e: e.activation(out=ct[:, 1:2], in_=ssq_ap, func=AF.Ln, bias=epsc[:], scale=1.0 / n),
                   reads=[ssq_buf, epsc_b], writes=[cb])
            act.op(lambda e: e.activation(out=ct[:, 2:3], in_=ct[:, 1:2], func=AF.Exp, scale=-0.5),
                   reads=[cb], writes=[cb])
            return ct[:, 2:3], cb

        def load_gain(row):
            sp.dma(gbc[:], gains[row:row + 1, :].partition_broadcast(128), writes=[gbc_b])

        def norm_transpose(src_rows_fn, ntiles, dstT, dst_bufs, st):
            xts = Ring([K.sb(st, "xt%d" % i, [128, D], F32) for i in range(2)])
            hbs = Ring([K.sb(st, "hb%d" % i, [128, D], BF16) for i in range(2)])
            junk, junk_b = K.sb(st, "junk", [128, D], BF16)
            for t in range(ntiles):
                xt, xb = xts.next()
                src_ap, src_bufs = src_rows_fn(t)
                sp.dma(xt[:], src_ap, reads=src_bufs, writes=[xb])
                ct, cb = colring.next()
                act.op(lambda e: e.activation(out=junk[:], in_=xt[:], func=AF.Square, accum_out=ct[:, 0:1]),
                       reads=[xb], writes=[junk_b, cb])
                rs, rb = rstd_col(ct[:, 0:1], cb, D)
                hb, hbb = hbs.next()
                dve.op(lambda e: e.scalar_tensor_tensor(out=hb[:], in0=xt[:], scalar=rs, in1=gbc[:],
                                                        op0=ALU.mult, op1=ALU.mult),
                       reads=[xb, rb, gbc_b], writes=[hbb])
                for half in range(2):
                    pt, pb = psw.next()
                    pv = pt[:].bitcast(BF16)
                    for q in range(8):
                        kc = half * 8 + q
                        pe.op(lambda e: e.transpose(out=pv[:, q * 128:(q + 1) * 128], in_=hb[:, kc * 128:(kc + 1) * 128],
                                                    identity=ident[:]),
                              reads=[hbb, ident_b], writes=[pb], signal=(q == 7))
                    eng = act if half == 0 else dve
                    outv = dstT[:, half * 8:(half + 1) * 8, t * 128:(t + 1) * 128]
                    inv = pv[:, 0:1024].rearrange("p (k n) -> p k n", k=8)
                    if eng is act:
                        act.op(lambda e: e.copy(out=outv, in_=inv), reads=[pb], writes=[dst_bufs[t]])
                    else:
                        dve.op(lambda e: e.tensor_copy(out=outv, in_=inv), reads=[pb], writes=[dst_bufs[t]])

        if wlog is None:
            wlog = []
        wstate = {"i": 0, "issued": {}}

        def w_issue(desc):
            wt, wb = wring.next()
            kind = desc[0]
            if kind == "fm":
                _, name, c0, g, KC, ncol = desc
                src = Wd[name]
                n = g * KC * ncol
                assert n <= 12288
                if KC * ncol <= 2048:
                    pool.dma(wt[:, 0:n].rearrange("p (g x) -> p g x", g=g),
                             src[c0:c0 + g].rearrange("g p k n -> p g (k n)"), writes=[wb], max_dma_last_dim=8192)
                else:
                    m = KC * ncol
                    for gg in range(g):
                        pool.dma(wt[:, gg * m:(gg + 1) * m], src[c0 + gg].rearrange("p k n -> p (k n)"), writes=[wb],
                                 max_dma_last_dim=8192)
                return wt[:, 0:n].rearrange("p (g k n) -> p g k n", g=g, k=KC), wb
            _, name, idx, KC, ncol = desc
            src = Wd[name] if idx is None else Wd[name][idx]
            n = KC * ncol
            assert n <= 12288
            pool.dma(wt[:, 0:n], src.rearrange("p k n -> p (k n)"), writes=[wb], max_dma_last_dim=8192)
            return wt[:, 0:n].rearrange("p (k n) -> p k n", k=KC), wb

        def wload(desc):
            i = wstate["i"]
            wstate["i"] += 1
            wlog.append(desc)
            if wplan is None:
                return w_issue(desc)
            assert wplan[i] == desc, (i, wplan[i], desc)
            for k in (i, i + 1):
                if k < len(wplan) and k not in wstate["issued"]:
                    wstate["issued"][k] = w_issue(wplan[k])
            return wstate["issued"].pop(i)

        def load_w_fm(name, c0, g, KC, ncol=128):
            return wload(("fm", name, c0, g, KC, ncol))

        def load_w_tm(name, KC, ncol, idx=None):
            return wload(("tm", name, idx, KC, ncol))

        def linear_fm(src, nch, KC, rhs_fn, nblk, epi, G=4, M=128, ncol=128):
            c = 0
            pending = None
            groups = []
            while c < nch:
                g = min(G, nch - c)
                groups.append((c, g))
                c += g
            for gi, (c0, g) in enumerate(groups):
                wv, wb = load_w_fm(src, c0, g, KC, ncol)
                for cc in range(g):
                    for b in range(nblk):
                        pt, pb = psw.next()
                        for kc in range(KC):
                            rap, rbufs = rhs_fn(kc, b)
                            pe.op(lambda e: e.matmul(pt[0:M, :], lhsT=wv[:, cc, kc, 0:M], rhs=rap,
                                                     start=(kc == 0), stop=(kc == KC - 1)),
                                  reads=[wb] + rbufs, writes=[pb], signal=(kc == KC - 1))
                        epi(c0 + cc, b, pt, pb)

        def softmax_epilogue(Ot, Ob, Dt, Db, dst_ap, dst_bufs, bias_ap=None, bias_bufs=()):
            lt, lb = f32ring.next()
            if bias_ap is None:
                act.op(lambda e: e.activation(out=lt[:], in_=Dt[:], func=AF.Ln), reads=[Db], writes=[lb])
            else:
                act.op(lambda e: e.activation(out=lt[:], in_=Dt[:], func=AF.Ln, bias=bias_ap),
                       reads=[Db] + list(bias_bufs), writes=[lb])
            act.op(lambda e: e.activation(out=lt[:], in_=lt[:], func=AF.Exp, scale=-1.0), reads=[lb], writes=[lb])
            ot, ob = bfring.next()
            dve.op(lambda e: e.tensor_tensor(out=ot[:], in0=Ot[:], in1=lt[:], op=ALU.mult),
                   reads=[Ob, lb], writes=[ob])
            sp.dma(dst_ap, ot[:], reads=[ob], writes=list(dst_bufs))

        def mem_prep():
            with ExitStack() as st:
                load_gain(5)
                mb = [Buf() for _ in range(2)]
                norm_transpose(lambda t: (mem_in[t * 128:(t + 1) * 128, :], []), 2, memT, mb, st)
                K.barrier()

        def mem_kv(L):
            p = "L%d_" % L

            def epi(c, b, pt, pb):
                act.op(lambda e: e.copy(out=mkT[:, c, :], in_=pt[:, 0:256]), reads=[pb], writes=[mkT_b])
            c = 0
            for c0 in range(0, 8, 4):
                wv, wb = load_w_fm(p + "mk", c0, 4, 16)
                for cc in range(4):
                    pt, pb = psw.next()
                    for kc in range(16):
                        pe.op(lambda e: e.matmul(pt[:, 0:256], lhsT=wv[:, cc, kc, :], rhs=memT[:, kc, :],
                                                 start=(kc == 0), stop=(kc == 15)),
                              reads=[wb, memT_b], writes=[pb], signal=(kc == 15))
                    epi(c0 + cc, 0, pt, pb)
            for nb in range(2):
                wv, wb = load_w_tm(p + "mv", 16, 512, idx=nb)
                for mc in range(2):
                    pt, pb = psw.next()
                    for kc in range(16):
                        pe.op(lambda e: e.matmul(pt[:], lhsT=memT[:, kc, mc * 128:(mc + 1) * 128], rhs=wv[:, kc, :],
                                                 start=(kc == 0), stop=(kc == 15)),
                              reads=[wb, memT_b], writes=[pb], signal=(kc == 15))
                    act.op(lambda e: e.copy(out=mv[:, mc, nb * 512:(nb + 1) * 512], in_=pt[:]),
                           reads=[pb], writes=[mv_b])

        def mem_attention(mq, mq_b, tok0, tbb=TBB):
            for hm in range(4):
                for b in range(tbb):
                    ps_tiles = []
                    for mc in range(2):
                        pt, pb = psw.next()
                        for dc in range(2):
                            pe.op(lambda e: e.matmul(pt[:], lhsT=mkT[:, 2 * hm + dc, mc * 128:(mc + 1) * 128],
                                                     rhs=mq[:, 2 * hm + dc, b * 512:(b + 1) * 512],
                                                     start=(dc == 0), stop=(dc == 1)),
                                  reads=[mkT_b, mq_b], writes=[pb], signal=(dc == 1))
                        et, eb = bfring.next()
                        act.op(lambda e: e.activation(out=et[:], in_=pt[:], func=AF.Exp, scale=1.0 / 16.0),
                               reads=[pb], writes=[eb])
                        ps_tiles.append((et, eb))
                    Dt, Db = psa.next()
                    for mc in range(2):
                        et, eb = ps_tiles[mc]
                        pe.op(lambda e: e.matmul(Dt[:], lhsT=ones[:], rhs=et[:], start=(mc == 0), stop=(mc == 1)),
                              reads=[ones_b, eb], writes=[Db], signal=(mc == 1))
                    for dc in range(2):
                        Ot, Ob = psa.next()
                        for mc in range(2):
                            et, eb = ps_tiles[mc]
                            pe.op(lambda e: e.matmul(Ot[:], lhsT=mv[:, mc, hm * 256 + dc * 128: hm * 256 + (dc + 1) * 128],
                                                     rhs=et[:], start=(mc == 0), stop=(mc == 1)),
                                  reads=[mv_b, eb], writes=[Ob], signal=(mc == 1))
                        ch = 16 + 2 * hm + dc
                        if dc == 0:
                            lt, lb = f32ring.next()
                            act.op(lambda e: e.activation(out=lt[:], in_=Dt[:], func=AF.Ln), reads=[Db], writes=[lb])
                            act.op(lambda e: e.activation(out=lt[:], in_=lt[:], func=AF.Exp, scale=-1.0),
                                   reads=[lb], writes=[lb])
                        ot, ob = bfring.next()
                        dve.op(lambda e: e.tensor_tensor(out=ot[:], in0=Ot[:], in1=lt[:], op=ALU.mult),
                               reads=[Ob, lb], writes=[ob])
                        sp.dma(o_scr[ch, :, tok0 + b * 512: tok0 + (b + 1) * 512], ot[:], reads=[ob], writes=[o_b[ch]])

        def mq_gate(L, hT, hT_b, tok0, st, tb=TB):
            p = "L%d_" % L
            tbb = tb // 512
            mq, mq_b = K.sb(st, "mq", [128, 8, tb], BF16)

            def rhs_fn(kc, b):
                return hT[:, kc, b * 512:(b + 1) * 512], hT_b[4 * b:4 * b + 4]

            def epi_mq(c, b, pt, pb):
                act.op(lambda e: e.copy(out=mq[:, c, b * 512:(b + 1) * 512], in_=pt[:]), reads=[pb], writes=[mq_b])
            linear_fm(p + "mq", 8, 16, rhs_fn, tbb, epi_mq)
            chk("mq")
            mem_attention(mq, mq_b, tok0, tbb)
            chk("mematt")

            def epi_gate(c, b, pt, pb):
                ot, ob = bfring.next()
                act.op(lambda e: e.activation(out=ot[:], in_=pt[:], func=AF.Silu), reads=[pb], writes=[ob])
                sp.dma(sg_scr[c, :, tok0 + b * 512: tok0 + (b + 1) * 512], ot[:], reads=[ob], writes=[sg_b[c]])
            linear_fm(p + "gate", 24, 16, rhs_fn, tbb, epi_gate)

        def x_rows(L0):
            def f(t_abs):
                if L0:
                    return x_in[t_abs * 128:(t_abs + 1) * 128, :], []
                return xres[t_abs * 128:(t_abs + 1) * 128, :], xres_b[t_abs]
            return f

        def rope_fm(src32, src_b, cs, sn, cs_b, tcol0, dsts, P=64):
            pt, pb = psw.next()
            pe.op(lambda e: e.matmul(pt[0:P, :], lhsT=rotT[0:P, 0:P], rhs=src32, start=True, stop=True),
                  reads=[rotT_b, src_b], writes=[pb])
            t1, t1b = f32ring.next()
            dve.op(lambda e: e.tensor_tensor(out=t1[0:P, :], in0=src32, in1=cs[0:P, tcol0:tcol0 + 512], op=ALU.mult),
                   reads=[src_b, cs_b], writes=[t1b])
            t2, t2b = f32ring.next()
            dve.op(lambda e: e.tensor_tensor(out=t2[0:P, :], in0=pt[0:P, :], in1=sn[0:P, tcol0:tcol0 + 512], op=ALU.mult),
                   reads=[pb, cs_b], writes=[t2b])
            ot, ob = bfring.next()
            pool.op(lambda e: e.tensor_tensor(out=ot[0:P, :], in0=t1[0:P, :], in1=t2[0:P, :], op=ALU.add),
                    reads=[t1b, t2b], writes=[ob])
            for (dst_ap, dst_bufs, ps_) in dsts:
                sp.dma(dst_ap, ot[ps_, :], reads=[ob], writes=list(dst_bufs))

        def mla_pass(L, pi, first):
            p = "L%d_" % L
            tok0 = pi * TB
            with ExitStack() as so:
                cqg, cqg_b = K.sb(so, "cqg", [128, 12, TB], BF16)
                ckv, ckv_b = K.sb(so, "ckv", [128, 4, TB], BF16)
                rq, rq_b = K.sb(so, "rq", [128, TB], F32)
                rkv, rkv_b = K.sb(so, "rkv", [128, TB], F32)
                cs, cs_b = K.sb(so, "cs", [128, TB], F32)
                sn, sn_b = K.sb(so, "sn", [128, TB], F32)
                gq, gq_b = K.sb(so, "gq", [128, 12], F32)
                gkv, gkv_b = K.sb(so, "gkv", [128, 4], F32)
                qr32, qr32_b = K.sb(so, "qr32", [128, 512], F32)
                for hf in range(2):
                    sp.dma(cs[hf * 64:(hf + 1) * 64, :], cosT[:, tok0:tok0 + TB], writes=[cs_b])
                    sp.dma(sn[hf * 64:(hf + 1) * 64, :], sinT[:, tok0:tok0 + TB], writes=[cs_b])
                sp.dma(gq[:], Wd[p + "gq"], writes=[gq_b])
                sp.dma(gkv[:], Wd[p + "gkv"], writes=[gkv_b])
                with ExitStack() as sh:
                    hT, _ = K.sb(sh, "hT", [128, 16, TB], BF16)
                    hT_b = [Buf() for _ in range(TBT)]
                    with ExitStack() as sa:
                        xf = x_rows(first)
                        norm_transpose(lambda t: xf(pi * TBT + t), TBT, hT, hT_b, sa)
                    chk("A")
                    with ExitStack() as sb_:
                        kr32, kr32_b = K.sb(sb_, "kr32", [64, TB], F32)

                        def rhs_fn(kc, b):
                            return hT[:, kc, b * 512:(b + 1) * 512], hT_b[4 * b:4 * b + 4]

                        def make_epi(dst, dst_b, gcol, gcol_b, acc, nch):
                            def epi(c, b, pt, pb):
                                sq, sqb = bfring.next()
                                act.op(lambda e: e.activation(out=sq[:], in_=pt[:], func=AF.Square),
                                       reads=[pb], writes=[sqb])
                                dve.op(lambda e: e.tensor_scalar(out=dst[:, c, b * 512:(b + 1) * 512], in0=pt[:],
                                                                 scalar1=gcol[:, c:c + 1], scalar2=None, op0=ALU.mult),
                                       reads=[pb, gcol_b], writes=[dst_b])
                                at, ab = acc[b]
                                pe.op(lambda e: e.matmul(at[:], lhsT=ones[:], rhs=sq[:], start=(c == 0), stop=(c == nch - 1)),
                                      reads=[ones_b, sqb], writes=[ab])
                            return epi

                        def finish_rstd(acc, n, dst, dst_b):
                            for b in range(TBB):
                                at, ab = acc[b]
                                lt, lb = f32ring.next()
                                act.op(lambda e: e.activation(out=lt[:], in_=at[:], func=AF.Ln, bias=epsc[:], scale=1.0 / n),
                                       reads=[ab, epsc_b], writes=[lb])
                                act.op(lambda e: e.activation(out=dst[:, b * 512:(b + 1) * 512], in_=lt[:], func=AF.Exp, scale=-0.5),
                                       reads=[lb], writes=[dst_b])
                        accq = [psa.next() for _ in range(TBB)]
                        linear_fm(p + "cq", 12, 16, rhs_fn, TBB, make_epi(cqg, cqg_b, gq, gq_b, accq, 12))
                        chk("cq")
                        finish_rstd(accq, A_Q, rq, rq_b)
                        chk("cqr")
                        acck = [psa.next() for _ in range(TBB)]
                        linear_fm(p + "ckv", 4, 16, rhs_fn, TBB, make_epi(ckv, ckv_b, gkv, gkv_b, acck, 4))
                        finish_rstd(acck, A_KV, rkv, rkv_b)
                        for c in range(4):
                            dve.op(lambda e: e.tensor_tensor(out=ckv[:, c, :], in0=ckv[:, c, :], in1=rkv[:], op=ALU.mult),
                                   reads=[ckv_b, rkv_b], writes=[ckv_b])
                        chk("ckv")
                        wv, wb = load_w_tm(p + "kr", 16, 64)
                        for b in range(TBB):
                            pt, pb = psw.next()
                            for kc in range(16):
                                rap, rbufs = rhs_fn(kc, b)
                                pe.op(lambda e: e.matmul(pt[0:64, :], lhsT=wv[:, kc, :], rhs=rap, start=(kc == 0), stop=(kc == 15)),
                                      reads=[wb] + rbufs, writes=[pb], signal=(kc == 15))
                            dve.op(lambda e: e.tensor_copy(out=kr32[:, b * 512:(b + 1) * 512], in_=pt[0:64, :]),
                                   reads=[pb], writes=[kr32_b])
                            rope_fm(kr32[:, b * 512:(b + 1) * 512], kr32_b, cs, sn, cs_b, b * 512,
                                    [(kr_scr[:, tok0 + b * 512: tok0 + (b + 1) * 512], [kr_b], slice(0, 64))], P=64)
                        chk("kr")
                        mq_gate(L, hT, hT_b, tok0, sb_)
                chk("B1")
                with ExitStack() as s2:
                    for h0 in range(0, 16, 4):
                        wv, wb = load_w_fm(p + "uqn", h0, 4, 12)
                        for hh in range(4):
                            h = h0 + hh
                            for b in range(TBB):
                                pt, pb = psw.next()
                                for kc in range(12):
                                    pe.op(lambda e: e.matmul(pt[:], lhsT=wv[:, hh, kc, :], rhs=cqg[:, kc, b * 512:(b + 1) * 512],
                                                             start=(kc == 0), stop=(kc == 11)),
                                          reads=[wb, cqg_b], writes=[pb], signal=(kc == 11))
                                ot, ob = bfring.next()
                                dve.op(lambda e: e.tensor_tensor(out=ot[:], in0=pt[:], in1=rq[:, b * 512:(b + 1) * 512], op=ALU.mult),
                                       reads=[pb, rq_b], writes=[ob])
                                sp.dma(q_scr[h, 0:128, tok0 + b * 512: tok0 + (b + 1) * 512], ot[:], reads=[ob], writes=[q_b[h]])
                    for p0 in range(0, 8, 4):
                        wv, wb = load_w_fm(p + "uqr", p0, 4, 12)
                        for pp in range(4):
                            h = 2 * (p0 + pp)
                            for b in range(TBB):
                                pt, pb = psw.next()
                                for kc in range(12):
                                    pe.op(lambda e: e.matmul(pt[:], lhsT=wv[:, pp, kc, :], rhs=cqg[:, kc, b * 512:(b + 1) * 512],
                                                             start=(kc == 0), stop=(kc == 11)),
                                          reads=[wb, cqg_b], writes=[pb], signal=(kc == 11))
                                dve.op(lambda e: e.tensor_tensor(out=qr32[:], in0=pt[:], in1=rq[:, b * 512:(b + 1) * 512], op=ALU.mult),
                                       reads=[pb, rq_b], writes=[qr32_b])
                                tsl = slice(tok0 + b * 512, tok0 + (b + 1) * 512)
                                rope_fm(qr32[:], qr32_b, cs, sn, cs_b, b * 512,
                                        [(q_scr[h, 128:192, tsl], [q_b[h]], slice(0, 64)),
                                         (q_scr[h + 1, 128:192, tsl], [q_b[h + 1]], slice(64, 128))], P=128)
                    for h0 in range(0, 16, 8):
                        wv, wb = load_w_fm(p + "ukvk", h0, 8, 4)
                        for hh in range(8):
                            h = h0 + hh
                            for b in range(TBB):
                                pt, pb = psw.next()
                                for kc in range(4):
                                    pe.op(lambda e: e.matmul(pt[:], lhsT=wv[:, hh, kc, :], rhs=ckv[:, kc, b * 512:(b + 1) * 512],
                                                             start=(kc == 0), stop=(kc == 3)),
                                          reads=[wb, ckv_b], writes=[pb], signal=(kc == 3))
                                ot, ob = bfring.next()
                                act.op(lambda e: e.copy(out=ot[:], in_=pt[:]), reads=[pb], writes=[ob])
                                sp.dma(k_scr[h, :, tok0 + b * 512: tok0 + (b + 1) * 512], ot[:], reads=[ob], writes=[k_b[h]])
                    for hg in range(4):
                        wv, wb = load_w_fm(p + "ukvv", hg, 1, 4, ncol=512)
                        for t in range(TBT):
                            pt, pb = psw.next()
                            for kc in range(4):
                                pe.op(lambda e: e.matmul(pt[:], lhsT=ckv[:, kc, t * 128:(t + 1) * 128], rhs=wv[:, 0, kc, :],
                                                         start=(kc == 0), stop=(kc == 3)),
                                      reads=[wb, ckv_b], writes=[pb], signal=(kc == 3))
                            ot, ob = bfring.next()
                            act.op(lambda e: e.copy(out=ot[:], in_=pt[:]), reads=[pb], writes=[ob])
                            ta = pi * TBT + t
                            sp.dma(v_scr[hg * 4:(hg + 1) * 4, :, ta * 128:(ta + 1) * 128].rearrange("h p d -> p h d"),
                                   ot[:].rearrange("p (h d) -> p h d", h=4), reads=[ob], writes=v_b[hg * 4:(hg + 1) * 4])
            K.barrier()

        def mla_attention():
            scale = (128 + 64) ** -0.5
            with ExitStack() as st:
                krf, krf_b = K.sb(st, "krf", [128, S], BF16)
                sp.dma(krf[0:64, :], kr_scr, reads=[kr_b], writes=[krf_b])
                sp.dma(krf[64:128, :], kr_scr, reads=[kr_b], writes=[krf_b])
                qn = Ring([K.sb(st, "qn%d" % i, [128, S], BF16) for i in range(2)])
                qr = Ring([K.sb(st, "qr%d" % i, [128, S], BF16) for i in range(2)])
                kn = Ring([K.sb(st, "kn%d" % i, [128, S], BF16) for i in range(2)])
                vv = Ring([K.sb(st, "vv%d" % i, [128, S], BF16) for i in range(2)])

                def load_head(h):
                    a = qn.next(); b_ = qr.next(); c = kn.next(); d = vv.next()
                    sp.dma(a[0][:], q_scr[h, 0:128, :], reads=[q_b[h]], writes=[a[1]])
                    sp.dma(b_[0][0:64, :], q_scr[h, 128:192, :], reads=[q_b[h]], writes=[b_[1]])
                    sp.dma(b_[0][64:128, :], q_scr[h, 128:192, :], reads=[q_b[h]], writes=[b_[1]])
                    sp.dma(c[0][:], k_scr[h], reads=[k_b[h]], writes=[c[1]])
                    sp.dma(d[0][:], v_scr[h], reads=[v_b[h]], writes=[d[1]])
                    return a, b_, c, d
                nxt = load_head(0)
                for h in range(16):
                    (qnt, qnb), (qrt, qrb), (knt, knb), (vt, vb) = nxt
                    if h + 1 < 16:
                        nxt = load_head(h + 1)
                    for b in range(4):
                        Ot, Ob = psa.next()
                        Dt, Db = psa.next()
                        nj = 4 * (b + 1)

                        def emit_pair(j0):
                            tl = []
                            for e_ in range(2):
                                j = j0 + e_
                                jj = j - 4 * b
                                c0 = 128 * jj if jj > 0 else 0
                                St, Sb = psw.next()
                                js = slice(j * 128, (j + 1) * 128)
                                ts = slice(b * 512 + c0, (b + 1) * 512)
                                pe.op(lambda e: e.matmul(St[:, c0:512], lhsT=knt[:, js], rhs=qnt[:, ts], start=True, stop=False),
                                      reads=[knb, qnb], writes=[Sb], signal=False)
                                tl.append((St, Sb, c0, js, ts, jj >= 0))
                            for e_, (St, Sb, c0, js, ts, diag) in enumerate(tl):
                                ps_ = slice(e_ * 64, (e_ + 1) * 64)
                                pe.op(lambda e: e.matmul(St[:, c0:512], lhsT=krf[ps_, js], rhs=qrt[ps_, ts], start=False, stop=not diag),
                                      reads=[krf_b, qrb], writes=[Sb], signal=not diag)
                            for (St, Sb, c0, js, ts, diag) in tl:
                                if diag:
                                    pe.op(lambda e: e.matmul(St[:, c0:c0 + 128], lhsT=ident[:], rhs=cmask[:, 0, :], start=False, stop=True),
                                          reads=[ident_b, cmask_b], writes=[Sb])
                            return [(St, Sb, c0, js) for (St, Sb, c0, js, ts, diag) in tl]
                        pend = emit_pair(0)
                        for j in range(nj):
                            St, Sb, c0, js = pend.pop(0)
                            if j % 2 == 0 and j + 2 < nj:
                                pend += emit_pair(j + 2)
                            Pt, Pb = bfring.next()
                            act.op(lambda e: e.activation(out=Pt[:, c0:512], in_=St[:, c0:512], func=AF.Exp, scale=scale),
                                   reads=[Sb], writes=[Pb])
                            pe.op(lambda e: e.matmul(Ot[:, c0:512], lhsT=vt[:, js], rhs=Pt[:, c0:512], start=(j == 0), stop=(j == nj - 1)),
                                  reads=[vb, Pb], writes=[Ob], signal=(j == nj - 1))
                            pe.op(lambda e: e.matmul(Dt[:, c0:512], lhsT=ones[:], rhs=Pt[:, c0:512], start=(j == 0), stop=(j == nj - 1)),
                                  reads=[ones_b, Pb], writes=[Db], signal=True)
                        softmax_epilogue(Ot, Ob, Dt, Db, o_scr[h, :, b * 512:(b + 1) * 512], [o_b[h]])
            K.barrier()

        def gqa_pass(L, pi, first, dsa):
            p = "L%d_" % L
            TB_, TBT_, TBB_ = GTB, GTB // 128, GTB // 512
            tok0 = pi * TB_
            with ExitStack() as sh:
                hT, _ = K.sb(sh, "hT", [128, 16, TB_], BF16)
                hT_b = [Buf() for _ in range(TBT_)]
                with ExitStack() as sa:
                    xf = x_rows(first)
                    norm_transpose(lambda t: xf(pi * TBT_ + t), TBT_, hT, hT_b, sa)
                with ExitStack() as sb_:
                    def rhs_fn(kc, b):
                        return hT[:, kc, b * 512:(b + 1) * 512], hT_b[4 * b:4 * b + 4]

                    def epi_to(scr, bufs):
                        def epi(c, b, pt, pb):
                            ot, ob = bfring.next()
                            if (c + b) % 2 == 0:
                                act.op(lambda e: e.copy(out=ot[:], in_=pt[:]), reads=[pb], writes=[ob])
                            else:
                                dve.op(lambda e: e.tensor_copy(out=ot[:], in_=pt[:]), reads=[pb], writes=[ob])
                            sp.dma(scr[c, 0:128, tok0 + b * 512: tok0 + (b + 1) * 512], ot[:], reads=[ob], writes=[bufs[c]])
                        return epi
                    linear_fm(p + "q", 16, 16, rhs_fn, TBB_, epi_to(q_scr, q_b))
                    linear_fm(p + "k", 2, 16, rhs_fn, TBB_, epi_to(k_scr, k_b))
                    wv, wb = load_w_tm(p + "v", 16, 256)
                    vview = v_scr.rearrange("a p s -> (a p s)")[0:S * 256].rearrange("(t p d) -> t p d", p=128, d=256)
                    for t in range(TBT_):
                        pt, pb = psw.next()
                        for kc in range(16):
                            pe.op(lambda e: e.matmul(pt[:, 0:256], lhsT=hT[:, kc, t * 128:(t + 1) * 128], rhs=wv[:, kc, :],
                                                     start=(kc == 0), stop=(kc == 15)),
                                  reads=[wb, hT_b[t]], writes=[pb], signal=(kc == 15))
                        ot, ob = bfring.next()
                        act.op(lambda e: e.copy(out=ot[:, 0:256], in_=pt[:, 0:256]), reads=[pb], writes=[ob])
                        sp.dma(vview[pi * TBT_ + t], ot[:, 0:256], reads=[ob], writes=[v_b[0]])
                    if dsa:
                        linear_fm(p + "iq", 8, 16, rhs_fn, TBB_, epi_to(iq_scr, iq_b))
                        wv, wb = load_w_tm(p + "ik", 16, 64)
                        for b in range(TBB_):
                            pt, pb = psw.next()
                            for kc in range(16):
                                rap, rbufs = rhs_fn(kc, b)
                                pe.op(lambda e: e.matmul(pt[0:64, :], lhsT=wv[:, kc, :], rhs=rap, start=(kc == 0), stop=(kc == 15)),
                                      reads=[wb] + rbufs, writes=[pb], signal=(kc == 15))
                            ot, ob = bfring.next()
                            act.op(lambda e: e.copy(out=ot[0:64, :], in_=pt[0:64, :]), reads=[pb], writes=[ob])
                            sp.dma(ik_scr[:, tok0 + b * 512: tok0 + (b + 1) * 512], ot[0:64, :], reads=[ob], writes=[ik_b])
                        wv, wb = load_w_tm(p + "iw", 16, 16)
                        for t in range(TBT_):
                            pt, pb = psw.next()
                            for kc in range(16):
                                pe.op(lambda e: e.matmul(pt[:, 0:16], lhsT=hT[:, kc, t * 128:(t + 1) * 128], rhs=wv[:, kc, :],
                                                         start=(kc == 0), stop=(kc == 15)),
                                      reads=[wb, hT_b[t]], writes=[pb], signal=(kc == 15))
                            ft, fb = f32ring.next()
                            act.op(lambda e: e.copy(out=ft[:, 0:16], in_=pt[:, 0:16]), reads=[pb], writes=[fb])
                            ta = pi * TBT_ + t
                            sp.dma(iw_scr[ta * 128:(ta + 1) * 128, :], ft[:, 0:16], reads=[fb], writes=[iw_b])
                    mq_gate(L, hT, hT_b, tok0, sb_, tb=TB_)
            K.barrier()

        def dsa_indexer(st):
            iqr = Ring([K.sb(st, "iq%d" % i, [128, 8, 128], BF16) for i in range(2)])
            ik2, ik2_b = K.sb(st, "ik2", [128, S], BF16)
            accs = Ring([K.sb(st, "acc%d" % i, [128, S], F32) for i in range(4)])
            mtms = Ring([K.sb(st, "mtm%d" % i, [128, S], BF16) for i in range(4)])
            junks = [K.sb(st, "junkD%d" % i, [128, S], BF16) for i in range(2)]
            iwt = Ring([K.sb(st, "iwt%d" % i, [128, 16], F32) for i in range(2)])
            dgs = Ring([K.sb(st, "dg%d" % i, [128, 16, 128], F32) for i in range(2)])
            rls = Ring([K.sb(st, "rl%d" % i, [128, 512], F32) for i in range(6)])
            msts = Ring([K.sb(st, "mst%d" % i, [128, 1024], BF16) for i in range(2)])
            thrs = Ring([K.sb(st, "thr%d" % i, [128, 4], F32) for i in range(4)])
            id32, id32_b = K.sb(st, "id32", [128, 128], F32)
            sp.dma(id32[:], ident_d, writes=[id32_b])
            sp.dma(ik2[0:64, :], ik_scr, reads=[ik_b], writes=[ik2_b])
            sp.dma(ik2[64:128, :], ik_scr, reads=[ik_b], writes=[ik2_b])
            RNG = 512.0
            NIT = 28

            def accumulate(tt):
                Lk = (tt + 1) * 128
                acc, acc_b = accs.next()
                wt, wtb = iwt.next()
                sp.dma(wt[:], iw_scr[tt * 128:(tt + 1) * 128, :], reads=[iw_b], writes=[wtb])
                iq, iq_sb = iqr.next()
                sp.dma(iq[:], iq_scr[:, :, tt * 128:(tt + 1) * 128].rearrange("c p t -> p c t"), reads=iq_b, writes=[iq_sb])
                dg, dgb = dgs.next()
                for hi in range(16):
                    dve.op(lambda e: e.tensor_scalar(out=dg[:, hi, :], in0=id32[:], scalar1=wt[:, hi:hi + 1], scalar2=None, op0=ALU.mult),
                           reads=[id32_b, wtb], writes=[dgb], signal=(hi == 15))
                nsb = (Lk + 511) // 512
                for sbk in range(nsb):
                    c0 = sbk * 512
                    n = min(512, Lk - c0)
                    aP, aPb = psa.next()
                    for hp in range(8):
                        dA, dAb = psw.next()
                        dB, dBb = psw.next()
                        pe.op(lambda e: e.matmul(dA[:, 0:n], lhsT=iq[0:64, hp, :], rhs=ik2[0:64, c0:c0 + n],
                                                 start=True, stop=True), reads=[iq_sb, ik2_b], writes=[dAb])
                        pe.op(lambda e: e.matmul(dB[:, 0:n], lhsT=iq[64:128, hp, :], rhs=ik2[64:128, c0:c0 + n],
                                                 start=True, stop=True), reads=[iq_sb, ik2_b], writes=[dBb])
                        rts = []
                        for (dt_, db_) in ((dA, dAb), (dB, dBb)):
                            rt, rb = rls.next()
                            act.op(lambda e: e.activation(out=rt[:, 0:n], in_=dt_[:, 0:n], func=AF.Relu), reads=[db_], writes=[rb])
                            rts.append((rt, rb))
                        for e_, (rt, rb) in enumerate(rts):
                            hi = 2 * hp + e_
                            pe.op(lambda e: e.matmul(aP[:, 0:n], lhsT=dg[:, hi, :], rhs=rt[:, 0:n], start=(hi == 0), stop=(hi == 15)),
                                  reads=[dgb, rb], writes=[aPb], signal=True)
                    act.op(lambda e: e.copy(out=acc[:, c0:c0 + n], in_=aP[:, 0:n]), reads=[aPb], writes=[acc_b])
                pool.op(lambda e: e.tensor_tensor(out=acc[:, tt * 128:Lk], in0=acc[:, tt * 128:Lk], in1=negtm[:], op=ALU.add),
                        reads=[acc_b, negtm_b], writes=[acc_b])
                return acc, acc_b

            def bisect(group):
                outs = []
                chains = []
                for (tt, acc, acc_b) in group:
                    Lk = (tt + 1) * 128
                    mtm, mtm_b = mtms.next()
                    outs.append((tt, mtm, mtm_b))
                    if tt < 2:
                        dve.op(lambda e: e.tensor_scalar(out=mtm[:, 0:Lk], in0=acc[:, 0:Lk], scalar1=-1e29, scalar2=None,
                                                         op0=ALU.is_gt), reads=[acc_b], writes=[mtm_b])
                    else:
                        ct, cb = thrs.next()
                        dve.op(lambda e: e.memset(ct[:, 0:1], 0.0), writes=[cb])
                        junkD, junkD_b = junks[len(chains)]
                        chains.append((tt, acc, acc_b, mtm, mtm_b, ct, cb, Lk, junkD, junkD_b))
                step = RNG
                for k in range(NIT):
                    step = step * 0.5
                    for (tt, acc, acc_b, mtm, mtm_b, ct, cb, Lk, junkD, junkD_b) in chains:
                        dve.op(lambda e: e.tensor_scalar(out=junkD[:, 0:Lk], in0=acc[:, 0:Lk], scalar1=ct[:, 0:1], scalar2=None,
                                                         op0=ALU.is_ge, op1=ALU.add, accum_out=ct[:, 1:2]),
                               reads=[acc_b, cb], writes=[cb, junkD_b])
                    for (tt, acc, acc_b, mtm, mtm_b, ct, cb, Lk, junkD, junkD_b) in chains:
                        dve.op(lambda e: e.tensor_scalar(out=ct[:, 2:3], in0=ct[:, 1:2], scalar1=255.5, scalar2=2.0 * step,
                                                         op0=ALU.is_ge, op1=ALU.mult), reads=[cb], writes=[cb])
                    for (tt, acc, acc_b, mtm, mtm_b, ct, cb, Lk, junkD, junkD_b) in chains:
                        dve.op(lambda e: e.scalar_tensor_tensor(out=ct[:, 0:1], in0=ct[:, 2:3], scalar=-step, in1=ct[:, 0:1],
                                                                op0=ALU.add, op1=ALU.add), reads=[cb], writes=[cb])
                for (tt, acc, acc_b, mtm, mtm_b, ct, cb, Lk, junkD, junkD_b) in chains:
                    dve.op(lambda e: e.tensor_scalar(out=ct[:, 0:1], in0=ct[:, 0:1], scalar1=-step, scalar2=None, op0=ALU.add),
                           reads=[cb], writes=[cb])
                    dve.op(lambda e: e.tensor_scalar(out=mtm[:, 0:Lk], in0=acc[:, 0:Lk], scalar1=ct[:, 0:1], scalar2=None,
                                                     op0=ALU.is_ge), reads=[acc_b, cb], writes=[mtm_b])
                return outs

            def transposes(group):
                for (tt, mtm, mtm_b) in group:
                    for j0 in range(0, tt + 1, 8):
                        nb_ = min(8, tt + 1 - j0)
                        pt, pb = psw.next()
                        pv = pt[:].bitcast(BF16)
                        for q in range(nb_):
                            j = j0 + q
                            pe.op(lambda e: e.transpose(out=pv[:, q * 128:(q + 1) * 128], in_=mtm[:, j * 128:(j + 1) * 128], identity=ident[:]),
                                  reads=[mtm_b, ident_b], writes=[pb], signal=(q == nb_ - 1))
                        ms, msb = msts.next()
                        act.op(lambda e: e.copy(out=ms[:, 0:nb_ * 128], in_=pv[:, 0:nb_ * 128]), reads=[pb], writes=[msb])
                        sp.dma(mask_scr[j0:j0 + nb_, :, tt * 128:(tt + 1) * 128].rearrange("j p t -> p j t"),
                               ms[:, 0:nb_ * 128].rearrange("p (j t) -> p j t", j=nb_), reads=[msb], writes=[mask_b])
            groups = [[2 * i + 1, 2 * i] for i in range(7, -1, -1)]
            accd = {}
            bis = {}
            for i in range(len(groups) + 2):
                if i < len(groups):
                    accd[i] = [(tt,) + accumulate(tt) for tt in groups[i]]
                if 0 <= i - 1 < len(groups):
                    bis[i - 1] = bisect(accd.pop(i - 1))
                if 0 <= i - 2 < len(groups):
                    transposes(bis.pop(i - 2))

        def gqa_attention(L, dsa):
            with ExitStack() as st:
                if dsa:
                    with ExitStack() as si:
                        dsa_indexer(si)
                    K.barrier()
                    maskT, maskT_b = K.sb(st, "maskT", [128, 16, S], BF16)
                    for j in range(16):
                        sp.dma(maskT[:, j, j * 128:S], mask_scr[j, :, j * 128:S], reads=[mask_b], writes=[maskT_b])
                vall, vall_b = K.sb(st, "vall", [128, 16, 256], BF16)
                vview = v_scr.rearrange("a p s -> (a p s)")[0:S * 256].rearrange("(t p d) -> p t d", p=128, d=256)
                sp.dma(vall[:], vview, reads=[v_b[0]], writes=[vall_b])
                cbt, cbt_b = K.sb(st, "cbt", [128, 32], F32)
                sp.dma(cbt[:], cb_d.partition_broadcast(128), writes=[cbt_b])
                sk, sk_b = K.sb(st, "sk", [128, 16], F32)
                if not dsa:
                    sp.dma(sk[:], sink_d, writes=[sk_b])
                    act.op(lambda e: e.activation(out=sk[:], in_=sk[:], func=AF.Exp), reads=[sk_b], writes=[sk_b])
                psS = Ring(banks[0:6])
                psOD = Ring(banks[6:8])
                k2s = Ring([K.sb(st, "k2_%d" % i, [128, S], BF16) for i in range(2)])
                qcs = Ring([K.sb(st, "qc%d" % i, [128, S], BF16) for i in range(2)])
                wbts = Ring([K.sb(st, "wbt%d" % i, [128, 2, 640], F32) for i in range(2)])
                qv = q_scr

                def load_chunk(c):
                    a = qcs.next(); w_ = wbts.next()
                    sp.dma(a[0][:], qv[c, 0:128, :], reads=[q_b[c]], writes=[a[1]])
                    sp.dma(w_[0][:], wbias_d[c], writes=[w_[1]])
                    return a, w_

                def load_k(g):
                    kk = k2s.next()
                    src = k_scr[g // 2, (g % 2) * 64:(g % 2) * 64 + 64, :]
                    sp.dma(kk[0][0:64, :], src, reads=[k_b[g // 2]], writes=[kk[1]])
                    sp.dma(kk[0][64:128, :], src, reads=[k_b[g // 2]], writes=[kk[1]])
                    return kk
                nxt = load_chunk(0)
                kcur = None
                for c in range(16):
                    g = c // 4
                    if c % 4 == 0:
                        kcur = load_k(g)
                    k2, k2b = kcur
                    (qc, qcb), (wbt, wbtb) = nxt
                    if c + 1 < 16:
                        nxt = load_chunk(c + 1)
                    for b in range(4):
                        Ot, Ob = psOD.next()
                        Dt, Db = psOD.next()
                        if dsa:
                            jl = list(range(0, 4 * (b + 1)))
                        else:
                            jl = [j for j in range(4 * b - 1, 4 * b + 4) if j >= 0]
                        def emit_S(j):
                            jj = j - 4 * b
                            if dsa:
                                lo = max(jj, 0) * 128
                                hi_ = 512
                            else:
                                lo = max(128 * jj, 0)
                                hi_ = min(128 * jj + 256, 512)
                            n = hi_ - lo
                            woff = lo - 128 * jj if jj >= -1 else None
                            js = slice(j * 128, (j + 1) * 128)
                            ts = slice(b * 512 + lo, b * 512 + hi_)
                            SA, SAb = psS.next()
                            SB, SBb = psS.next()
                            only = dsa and jj < -1
                            pe.op(lambda e: e.matmul(SA[:, lo:hi_], lhsT=k2[0:64, js], rhs=qc[0:64, ts], start=True, stop=only),
                                  reads=[k2b, qcb], writes=[SAb], signal=False)
                            pe.op(lambda e: e.matmul(SB[:, lo:hi_], lhsT=k2[64:128, js], rhs=qc[64:128, ts], start=True, stop=only),
                                  reads=[k2b, qcb], writes=[SBb], signal=only)
                            if dsa:
                                mrhs = maskT[:, j, ts]
                                mb_ = [maskT_b]
                                mid, midb = identC, identC_b
                            else:
                                mrhs = cmask[:, 0:2, :].rearrange("p a n -> p (a n)")[:, woff:woff + n]
                                mb_ = [cmask_b]
                                mid, midb = ident, ident_b
                            if dsa and jj < -1:
                                return (SA, SAb), (SB, SBb), lo, hi_, n, woff, jj, mrhs
                            pe.op(lambda e: e.matmul(SA[:, lo:hi_], lhsT=mid[:], rhs=mrhs, start=False, stop=True),
                                  reads=[midb] + mb_, writes=[SAb])
                            pe.op(lambda e: e.matmul(SB[:, lo:hi_], lhsT=mid[:], rhs=mrhs, start=False, stop=True),
                                  reads=[midb] + mb_, writes=[SBb])
                            return (SA, SAb), (SB, SBb), lo, hi_, n, woff, jj, mrhs
                        pend = [emit_S(jx) for jx in jl[0:2]]
                        for ji, j in enumerate(jl):
                            (SA, SAb), (SB, SBb), lo, hi_, n, woff, jj, mrhs = pend.pop(0)
                            if ji + 2 < len(jl):
                                pend.append(emit_S(jl[ji + 2]))
                            special = jj >= -1
                            for e_, (St, Sb) in enumerate(((SA, SAb), (SB, SBb))):
                                h = 2 * c + e_
                                Pt, Pb = bfring.next()
                                if special:
                                    ft, fb = f32ring.next()
                                    dve.op(lambda e: e.scalar_tensor_tensor(out=ft[:, lo:hi_], in0=St[:, lo:hi_], scalar=0.125,
                                                                            in1=wbt[:, e_, woff:woff + n], op0=ALU.mult, op1=ALU.add),
                                           reads=[Sb, wbtb], writes=[fb])
                                    if dsa:
                                        act.op(lambda e: e.activation(out=Pt[:, lo:hi_], in_=ft[:, lo:hi_], func=AF.Exp, bias=negC8[:]),
                                               reads=[fb, negC8_b], writes=[Pb])
                                    else:
                                        act.op(lambda e: e.activation(out=Pt[:, lo:hi_], in_=ft[:, lo:hi_], func=AF.Exp),
                                               reads=[fb], writes=[Pb])
                                else:
                                    act.op(lambda e: e.activation(out=Pt[:, lo:hi_], in_=St[:, lo:hi_], func=AF.Exp, scale=0.125,
                                                                  bias=cbt[:, h:h + 1]),
                                           reads=[Sb, cbt_b], writes=[Pb])
                                    dve.op(lambda e: e.tensor_tensor(out=Pt[:, lo:hi_], in0=Pt[:, lo:hi_], in1=mrhs, op=ALU.mult),
                                           reads=[Pb, maskT_b], writes=[Pb])
                                first = (j == jl[0])
                                last = (j == jl[-1])
                                pe.op(lambda e: e.matmul(Ot[e_ * 64:(e_ + 1) * 64, lo:hi_], lhsT=vall[:, j, g * 64:(g + 1) * 64],
                                                         rhs=Pt[:, lo:hi_], start=first, stop=last, skip_group_check=True),
                                      reads=[vall_b, Pb], writes=[Ob], signal=(last and e_ == 1))
                                pe.op(lambda e: e.matmul(Dt[e_ * 64:(e_ + 1) * 64, lo:hi_], lhsT=ones[:, 0:64],
                                                         rhs=Pt[:, lo:hi_], start=first, stop=last, skip_group_check=True),
                                      reads=[ones_b, Pb], writes=[Db], signal=True)
                        if dsa:
                            softmax_epilogue(Ot, Ob, Dt, Db, o_scr[c, :, b * 512:(b + 1) * 512], [o_b[c]])
                        else:
                            softmax_epilogue(Ot, Ob, Dt, Db, o_scr[c, :, b * 512:(b + 1) * 512], [o_b[c]],
                                             bias_ap=sk[:, c:c + 1], bias_bufs=[sk_b])
            K.barrier()

        def out_proj(L, first):
            p = "L%d_" % L
            with ExitStack() as st:
                yT, _ = K.sb(st, "yT", [128, 24, S], BF16)
                ybufs = [[Buf() for _ in range(24)] for _ in range(4)]
                ost = Ring([K.sb(st, "ost%d" % i, [128, 512], BF16) for i in range(3)])
                sgt = Ring([K.sb(st, "sgt%d" % i, [128, 512], BF16) for i in range(3)])
                xin = Ring([K.sb(st, "xin%d" % i, [128, 512], F32) for i in range(3)])
                xo = Ring([K.sb(st, "xo%d" % i, [128, 512], F32) for i in range(3)])
                cnt = [0]

                def build_chunk(q, c):
                    a = ost.next(); s_ = sgt.next()
                    sp.dma(a[0][:], o_scr[c, :, q * 512:(q + 1) * 512], reads=[o_b[c]], writes=[a[1]])
                    sp.dma(s_[0][:], sg_scr[c, :, q * 512:(q + 1) * 512], reads=[sg_b[c]], writes=[s_[1]])
                    cnt[0] += 1
                    eng = dve if cnt[0] % 4 else pool
                    eng.op(lambda e: e.tensor_tensor(out=yT[:, c, q * 512:(q + 1) * 512], in0=a[0][:], in1=s_[0][:], op=ALU.mult),
                           reads=[a[1], s_[1]], writes=[ybufs[q][c]])
                for c in range(24):
                    build_chunk(0, c)
                pending = [(q, c) for q in range(1, 4) for c in range(24)]
                for nb in range(4):
                    wv, wb = load_w_tm(p + "out", 24, 512, idx=nb)
                    for ta in range(NT):
                        for _ in range(6):
                            if pending:
                                build_chunk(*pending.pop(0))
                        xi, xib = xin.next()
                        if first:
                            sp.dma(xi[:], x_in[ta * 128:(ta + 1) * 128, nb * 512:(nb + 1) * 512], writes=[xib])
                        else:
                            sp.dma(xi[:], xres[ta * 128:(ta + 1) * 128, nb * 512:(nb + 1) * 512],
                                   reads=[xres_b[ta][nb]], writes=[xib])
                        pt, pb = psw.next()
                        for c in range(24):
                            pe.op(lambda e: e.matmul(pt[:], lhsT=yT[:, c, ta * 128:(ta + 1) * 128], rhs=wv[:, c, :],
                                                     start=(c == 0), stop=(c == 23)),
                                  reads=[wb, ybufs[ta // 4][c]], writes=[pb], signal=(c == 23))
                        xot, xob = xo.next()
                        dve.op(lambda e: e.tensor_tensor(out=xot[:], in0=pt[:], in1=xi[:], op=ALU.add),
                               reads=[pb, xib], writes=[xob])
                        sp.dma(xres[ta * 128:(ta + 1) * 128, nb * 512:(nb + 1) * 512], xot[:], reads=[xob],
                               writes=[xres_b[ta][nb]])
            K.barrier()

        def final_pass():
            with ExitStack() as st:
                xts = Ring([K.sb(st, "fx%d" % i, [128, D], F32) for i in range(4)])
                ots = Ring([K.sb(st, "fo%d" % i, [128, D], F32) for i in range(4)])
                junk, junk_b = K.sb(st, "fjunk", [128, D], BF16)
                if final_norm:
                    load_gain(4)
                fins = []
                for t in range(NT):
                    xt, xb = xts.next()
                    sp.dma(xt[:], xres[t * 128:(t + 1) * 128, :], reads=xres_b[t], writes=[xb])
                    if final_norm:
                        ct, cb = colring.next()
                        act.op(lambda e: e.activation(out=junk[:], in_=xt[:], func=AF.Square, accum_out=ct[:, 0:1]),
                               reads=[xb], writes=[junk_b, cb])
                        rs, rb = rstd_col(ct[:, 0:1], cb, D)
                        ot, ob = ots.next()
                        dve.op(lambda e: e.scalar_tensor_tensor(out=ot[:], in0=xt[:], scalar=rs, in1=gbc[:],
                                                                op0=ALU.mult, op1=ALU.mult),
                               reads=[xb, rb, gbc_b], writes=[ob])
                        fins.append(sp.dma(out_d[t * 128:(t + 1) * 128, :], ot[:], reads=[ob]))
                    else:
                        fins.append(sp.dma(out_d[t * 128:(t + 1) * 128, :], xt[:], reads=[xb]))
                for tk in fins:
                    sp.wait(tk)

        NEGM_col, NEGM_col_b = K.sb(top, "negmcol", [128, 1], F32)
        try:
            load_consts()
            dve.op(lambda e: e.memset(NEGM_col[:], NEGM), writes=[NEGM_col_b])
            chk("consts")
            mem_prep()
            chk("memprep")
            first = True
            for L in layers:
                kind = L % 3
                mem_kv(L)
                load_gain(L)
                chk("memkv")
                for pi in range(NPASS if kind == 0 else S // GTB):
                    if kind == 0:
                        mla_pass(L, pi, first)
                    else:
                        gqa_pass(L, pi, first, dsa=(kind == 1))
                    chk("pass%d" % pi)
                if kind == 0:
                    mla_attention()
                else:
                    gqa_attention(L, dsa=(kind == 1))
                chk("attn")
                out_proj(L, first)
                first = False
            final_pass()
        except _Stop:
            pass
        K.barrier()
    nc._wlog = wlog
    return nc


def host_prepare(inputs, layers):
    f = np.float32
    shared = {}
    gains = np.concatenate([inputs["norm_in"], inputs["final_norm"][None], inputs["mem_norm"][None]], 0).astype(f)
    shared["gains"] = np.ascontiguousarray(gains)
    inv = (1.0 / (10000.0 ** (np.arange(0, 64, 2, dtype=np.float32) / np.float32(64)))).astype(np.float32)
    ang = np.arange(S, dtype=np.float32)[:, None] * inv[None, :]
    cos = np.cos(ang).astype(f).T
    sin = np.sin(ang).astype(f).T
    shared["cosT"] = np.ascontiguousarray(np.concatenate([cos, cos], 0))
    shared["sinT"] = np.ascontiguousarray(np.concatenate([sin, sin], 0))
    R = np.zeros((64, 64), f)
    for d in range(32):
        R[d, d + 32] = -1.0
        R[d + 32, d] = 1.0
    shared["rotT"] = np.ascontiguousarray(R.T)
    sl = np.arange(128)[:, None]
    tl = np.arange(128)[None, :]
    cm = np.zeros((128, 3, 128), f)
    cm[:, 0, :] = np.where(sl <= tl, 0.0, NEGM)
    cm[:, 1, :] = np.where(tl < sl, 0.0, NEGM)
    cm[:, 2, :] = np.where(tl <= sl, 0.0, -1e30)
    shared["cmask"] = cm
    shared["ident"] = np.eye(128, dtype=f)
    rb = inputs["rel_bias"].astype(f)
    shared["cbias"] = np.ascontiguousarray(rb[31:32, :])
    delta_diag = tl - sl
    delta_off = tl - sl + 128
    bd = t5_bucket_np(delta_diag)
    bo = t5_bucket_np(delta_off)
    wb = np.zeros((32, 128, 640), f)
    for h in range(32):
        wb[h, :, 0:128] = rb[bd, h]
        wb[h, :, 128:256] = rb[bo, h]
        wb[h, :, 256:640] = rb[31, h]
    shared["wbias"] = np.ascontiguousarray(wb.reshape(16, 2, 128, 640).transpose(0, 2, 1, 3))
    sinks = inputs["c_sinks"][0].astype(f)
    spc = np.zeros((128, 16), f)
    for c in range(16):
        spc[0:64, c] = sinks[2 * c]
        spc[64:128, c] = sinks[2 * c + 1]
    shared["sinkpc"] = spc
    for L in layers:
        kind, j = L % 3, L // 3
        p = "L%d_" % L
        if kind == 0:
            W = inputs["w_in_a"][j]
            o = 0
            shared[p + "cq"] = chunk_fm(W[:, o:o + A_Q]); o += A_Q
            shared[p + "ckv"] = chunk_fm(W[:, o:o + A_KV]); o += A_KV
            shared[p + "kr"] = chunk_tm(W[:, o:o + A_ROPE]); o += A_ROPE
            shared[p + "mq"] = chunk_fm(W[:, o:o + MEMW]); o += MEMW
            shared[p + "gate"] = chunk_fm(W[:, o:o + BRW]); o += BRW
            shared[p + "gq"] = col_pc(inputs["a_q_norm"][j])
            shared[p + "gkv"] = col_pc(inputs["a_kv_norm"][j])
            wuq = inputs["w_uq"][j].reshape(12, 128, 16, 192)
            wq_h = wuq.transpose(2, 1, 0, 3)
            shared[p + "uqn"] = np.ascontiguousarray(wq_h[:, :, :, 0:128])
            wr = wq_h[:, :, :, 128:192].reshape(8, 2, 128, 12, 64)
            shared[p + "uqr"] = np.ascontiguousarray(wr.transpose(0, 2, 3, 1, 4).reshape(8, 128, 12, 128))
            wukv = inputs["w_ukv"][j].reshape(4, 128, 16, 256)
            shared[p + "ukvk"] = np.ascontiguousarray(wukv[:, :, :, 0:128].transpose(2, 1, 0, 3))
            wv = wukv[:, :, :, 128:256].reshape(4, 128, 4, 512)
            shared[p + "ukvv"] = np.ascontiguousarray(wv.transpose(2, 1, 0, 3))
        else:
            W = inputs["w_in_b"][j] if kind == 1 else inputs["w_in_c"][j]
            o = 0
            shared[p + "q"] = chunk_fm(W[:, o:o + 2048]); o += 2048
            shared[p + "k"] = chunk_fm(W[:, o:o + 256]); o += 256
            shared[p + "v"] = chunk_tm(W[:, o:o + 256]); o += 256
            if kind == 1:
                shared[p + "iq"] = chunk_fm(W[:, o:o + 1024]); o += 1024
                shared[p + "ik"] = chunk_tm(W[:, o:o + 64]); o += 64
                shared[p + "iw"] = chunk_tm(W[:, o:o + 16]); o += 16
            shared[p + "mq"] = chunk_fm(W[:, o:o + MEMW]); o += MEMW
            shared[p + "gate"] = chunk_fm(W[:, o:o + BRW]); o += BRW
            assert o == W.shape[1]
        Wm = inputs["w_mem_kv"][L]
        shared[p + "mk"] = chunk_fm(Wm[:, 0:1024])
        mvw = chunk_tm(Wm[:, 1024:2048])
        shared[p + "mv"] = np.ascontiguousarray(mvw.reshape(128, 16, 2, 512).transpose(2, 0, 1, 3))
        Wo = chunk_tm(inputs["w_out"][L])
        shared[p + "out"] = np.ascontiguousarray(Wo.reshape(128, 24, 4, 512).transpose(2, 0, 1, 3))
    return shared


_PROG_CACHE = {}


def build_full(layers, final_norm=True, dbg=(), stop_after=None):
    nc0 = build_program(list(layers), final_norm=final_norm, dbg=dbg, stop_after=stop_after)
    return build_program(list(layers), final_norm=final_norm, dbg=dbg, stop_after=stop_after, wplan=list(nc0._wlog))


def run_layers(inputs, layers, final_norm=True, x_override=None, dbg=(), stop_after=None, ncores=8):
    key = (tuple(layers), final_norm, tuple(dbg), stop_after)
    if key not in _PROG_CACHE:
        _PROG_CACHE[key] = build_full(layers, final_norm=final_norm, dbg=dbg, stop_after=stop_after)
    nc = _PROG_CACHE[key]
    shared = host_prepare(inputs, layers)
    x = inputs["x"] if x_override is None else x_override
    in_maps = []
    for b in range(ncores):
        m = dict(shared)
        m["x"] = np.ascontiguousarray(x[b], dtype=np.float32)
        m["mem"] = np.ascontiguousarray(inputs["mem"][b], dtype=np.float32)
        in_maps.append(m)
    res = run_bass_kernel_spmd(nc, in_maps, core_ids=list(range(ncores)))
    return res


def kernel(**inputs):
    inputs = {k: np.asarray(v) for k, v in inputs.items()}
    res = run_layers(inputs, [0, 1, 2, 3], final_norm=True)
    return np.stack([np.asarray(r["out"], dtype=np.float32) for r in res.results], 0)
```
